# Optimizing a Trainium2 kernel written in Bass

```python
import jax, jax.numpy as jnp
from jax import lax
import numpy as np

D_MODEL = 4096
BATCH = 1
SEQ = 8192
DEPTH = 4

N_ATTN_HEADS = 16
HEAD_DIM = 128
ATTN_WIDTH = N_ATTN_HEADS * HEAD_DIM
CONV_CHANNELS = D_MODEL - ATTN_WIDTH
MIX_WIDTH = ATTN_WIDTH + CONV_CHANNELS
IN_WIDTH = 3 * ATTN_WIDTH + 2 * CONV_CHANNELS
CONV_WIDTH = 31
D_FF = (3 * D_MODEL) // 2
ROPE_THETA = 500000.0
ROT_DIM = HEAD_DIM // 4
DILATED_BRANCHES = ((128, 1), (512, 4), (2048, 16))
BLOCK = 128
RMS_EPS = 1e-5
LN_EPS = 1e-5
FFN_RESIDUAL_SCALE = 0.5
MASK_VALUE = -1e30

kernel_name = "hymba_longnet_conformer_macaron"


def rms_norm(x, g):
    xf = x.astype(jnp.float32)
    y = xf * lax.rsqrt(jnp.mean(xf * xf, axis=-1, keepdims=True) + RMS_EPS)
    return (y * g.astype(jnp.float32)).astype(x.dtype)


def layer_norm(x, g, b):
    xf = x.astype(jnp.float32)
    mu = jnp.mean(xf, axis=-1, keepdims=True)
    xc = xf - mu
    y = xc * lax.rsqrt(jnp.mean(xc * xc, axis=-1, keepdims=True) + LN_EPS)
    return (y * g.astype(jnp.float32) + b.astype(jnp.float32)).astype(x.dtype)


def swiglu_ffn(x, w_gate, w_up, w_down):
    return (jax.nn.silu(x @ w_gate) * (x @ w_up)) @ w_down


def rotary_tables(seq):
    pos = jnp.arange(seq, dtype=jnp.float32)
    inv_freq = ROPE_THETA ** (-(jnp.arange(0, ROT_DIM, 2, dtype=jnp.float32) / ROT_DIM))
    ang = pos[:, None] * inv_freq[None, :]
    return jnp.cos(ang), jnp.sin(ang)


def partial_rotary(x, cos, sin):
    half = ROT_DIM // 2
    c = cos[None, :, None, :].astype(x.dtype)
    s = sin[None, :, None, :].astype(x.dtype)
    x1, x2, rest = x[..., :half], x[..., half:ROT_DIM], x[..., ROT_DIM:]
    return jnp.concatenate([x1 * c - x2 * s, x2 * c + x1 * s, rest], axis=-1)


def dilated_branch(q, k, v, window, dilation):
    B, S, H, Dh = q.shape
    w_sub = window // dilation
    unit = dilation * BLOCK
    s_pad = -(-S // unit) * unit
    nb = s_pad // unit

    def to_blocks(a):
        a = jnp.pad(a, ((0, 0), (0, s_pad - S), (0, 0), (0, 0)))
        return a.reshape(B, nb, BLOCK, dilation, H, Dh)

    def with_prev(a):
        prev = jnp.pad(a[:, :-1], ((0, 0), (1, 0), (0, 0), (0, 0), (0, 0), (0, 0)))
        return jnp.concatenate([prev, a], axis=2)

    qb = to_blocks(q).astype(jnp.float32)
    kk = with_prev(to_blocks(k)).astype(jnp.float32)
    vv = with_prev(to_blocks(v)).astype(jnp.float32)

    scores = jnp.einsum('bnqrhd,bnkrhd->bnrhqk', qb, kk) * (Dh ** -0.5)
    qi = jnp.arange(BLOCK)[:, None] + BLOCK
    kj = jnp.arange(2 * BLOCK)[None, :]
    delta = qi - kj
    band = (delta >= 0) & (delta <= w_sub)
    valid = (jnp.arange(nb)[:, None, None] > 0) | (kj >= BLOCK)[None]
    mask = (band[None] & valid)[None, :, None, None]
    scores = jnp.where(mask, scores, MASK_VALUE)
    m = jnp.max(scores, axis=-1, keepdims=True)
    p = jnp.exp(scores - m)
    den = jnp.sum(p, axis=-1)
    out = jnp.einsum('bnrhqk,bnkrhd->bnqrhd', p, vv)
    den_t = den.transpose(0, 1, 4, 2, 3)
    out = out / den_t[..., None]
    lse = (m[..., 0] + jnp.log(den)).transpose(0, 1, 4, 2, 3)
    out = out.reshape(B, s_pad, H, Dh)[:, :S]
    lse = lse.reshape(B, s_pad, H)[:, :S]
    return out, lse


def longnet_attention(q, k, v):
    outs, lses = [], []
    for window, dilation in DILATED_BRANCHES:
        o, l = dilated_branch(q, k, v, window, dilation)
        outs.append(o)
        lses.append(l)
    w = jax.nn.softmax(jnp.stack(lses, axis=0), axis=0)
    o = jnp.einsum('nbsh,nbshd->bshd', w, jnp.stack(outs, axis=0))
    return o.astype(q.dtype)


def conformer_conv(a, gate, w_dw, b_dw, ln_g, ln_b):
    h = a * jax.nn.sigmoid(gate)
    h = lax.conv_general_dilated(
        h, w_dw[:, None, :].astype(h.dtype), window_strides=(1,),
        padding=((CONV_WIDTH - 1, 0),), dimension_numbers=('NWC', 'WIO', 'NWC'),
        feature_group_count=CONV_CHANNELS) + b_dw.astype(h.dtype)
    h = layer_norm(h, ln_g, ln_b)
    return jax.nn.silu(h)


def setup_inputs(seed: int = 0) -> dict:
    key = jax.random.key(seed)
    ks = jax.random.split(key, 20)
    f32 = jnp.float32

    def normal(k, shape, scale):
        return jax.random.normal(k, shape, dtype=f32) * scale

    def gain(k, shape):
        return 1.0 + 0.01 * jax.random.normal(k, shape, dtype=f32)

    return {
        "x": normal(ks[0], (BATCH, SEQ, D_MODEL), 1.0),
        "ffn1_norm": gain(ks[1], (DEPTH, D_MODEL)),
        "ffn1_w_gate": normal(ks[2], (DEPTH, D_MODEL, D_FF), D_MODEL ** -0.5),
        "ffn1_w_up": normal(ks[3], (DEPTH, D_MODEL, D_FF), D_MODEL ** -0.5),
        "ffn1_w_down": normal(ks[4], (DEPTH, D_FF, D_MODEL), D_FF ** -0.5),
        "mix_norm": gain(ks[5], (DEPTH, D_MODEL)),
        "w_in": normal(ks[6], (DEPTH, D_MODEL, IN_WIDTH), D_MODEL ** -0.5),
        "conv_w": normal(ks[7], (DEPTH, CONV_WIDTH, CONV_CHANNELS), CONV_WIDTH ** -0.5),
        "conv_b": normal(ks[8], (DEPTH, CONV_CHANNELS), 0.01),
        "conv_ln_g": gain(ks[9], (DEPTH, CONV_CHANNELS)),
        "conv_ln_b": normal(ks[10], (DEPTH, CONV_CHANNELS), 0.01),
        "attn_out_norm": gain(ks[11], (DEPTH, ATTN_WIDTH)),
        "conv_out_norm": gain(ks[12], (DEPTH, CONV_CHANNELS)),
        "w_out": normal(ks[13], (DEPTH, MIX_WIDTH, D_MODEL), MIX_WIDTH ** -0.5),
        "ffn2_norm": gain(ks[14], (DEPTH, D_MODEL)),
        "ffn2_w_gate": normal(ks[15], (DEPTH, D_MODEL, D_FF), D_MODEL ** -0.5),
        "ffn2_w_up": normal(ks[16], (DEPTH, D_MODEL, D_FF), D_MODEL ** -0.5),
        "ffn2_w_down": normal(ks[17], (DEPTH, D_FF, D_MODEL), D_FF ** -0.5),
        "final_norm": gain(ks[18], (D_MODEL,)),
    }


def reference(x, ffn1_norm, ffn1_w_gate, ffn1_w_up, ffn1_w_down, mix_norm, w_in,
              conv_w, conv_b, conv_ln_g, conv_ln_b, attn_out_norm, conv_out_norm, w_out,
              ffn2_norm, ffn2_w_gate, ffn2_w_up, ffn2_w_down, final_norm):
    B, S, _ = x.shape
    cos, sin = rotary_tables(S)
    A, C = ATTN_WIDTH, CONV_CHANNELS
    for l in range(DEPTH):
        x = x + FFN_RESIDUAL_SCALE * swiglu_ffn(rms_norm(x, ffn1_norm[l]),
                                                ffn1_w_gate[l], ffn1_w_up[l], ffn1_w_down[l])
        u = rms_norm(x, mix_norm[l])
        proj = u @ w_in[l]
        q = proj[..., :A].reshape(B, S, N_ATTN_HEADS, HEAD_DIM)
        k = proj[..., A:2 * A].reshape(B, S, N_ATTN_HEADS, HEAD_DIM)
        v = proj[..., 2 * A:3 * A].reshape(B, S, N_ATTN_HEADS, HEAD_DIM)
        glu_a = proj[..., 3 * A:3 * A + C]
        glu_g = proj[..., 3 * A + C:]
        q = partial_rotary(q, cos, sin)
        k = partial_rotary(k, cos, sin)
        attn = longnet_attention(q, k, v).reshape(B, S, A)
        conv = conformer_conv(glu_a, glu_g, conv_w[l], conv_b[l], conv_ln_g[l], conv_ln_b[l])
        mixed = jnp.concatenate([rms_norm(attn, attn_out_norm[l]),
                                 rms_norm(conv, conv_out_norm[l])], axis=-1)
        x = x + mixed @ w_out[l]
        x = x + FFN_RESIDUAL_SCALE * swiglu_ffn(rms_norm(x, ffn2_norm[l]),
                                                ffn2_w_gate[l], ffn2_w_up[l], ffn2_w_down[l])
    return rms_norm(x, final_norm)
```

```python
import sys
from contextlib import ExitStack
import numpy as np
import concourse.bass as bass
import concourse.mybir as mybir
from concourse.bass_utils import run_bass_kernel_spmd

F32 = mybir.dt.float32
BF16 = mybir.dt.bfloat16
AF = mybir.ActivationFunctionType
ALU = mybir.AluOpType

D = 4096
DFF = 6144
KC = D // 128
FCN = DFF // 128
NH = 16
AW = 2048
CW = 2048
TCH = 1024
NOC = 88
PADK = 2048
PADH = 32
NEG = -30000.0
RMS_EPS = 1e-5
LN_EPS = 1e-5
ROPE_THETA = 500000.0
BRANCH_D = (1, 4, 16)


class DSem:
    def __init__(self, sem):
        self.sem = sem
        self.cnt = 0


class Buf:
    __slots__ = ("name", "w", "r", "dsem")

    def __init__(self, name, dsem=None):
        self.name = name
        self.w = None
        self.r = {}
        self.dsem = dsem


class Sched:
    def __init__(self, nc, stack, n_dsems=40):
        self.nc = nc
        self.eng = {"pe": nc.tensor, "act": nc.scalar, "dve": nc.vector, "pool": nc.gpsimd, "sp": nc.sync}
        self.esem = {}
        self.ecnt = {}
        for k in ("pe", "act", "dve", "pool"):
            self.esem[k] = stack.enter_context(nc.semaphore("e_" + k))
            self.ecnt[k] = 0
        self.waited = {k: {} for k in self.eng}
        self.free_dsems = [DSem(stack.enter_context(nc.semaphore("d%d" % i))) for i in range(n_dsems)]
        self.free_swsems = [DSem(stack.enter_context(nc.semaphore("w%d" % i))) for i in range(10)]
        self.all_dsems = list(self.free_dsems) + list(self.free_swsems)
        self.ninst = 0

    def get_dsem(self, sw=False):
        d = (self.free_swsems if sw else self.free_dsems).pop()
        d.sw = sw
        return d

    def put_dsem(self, d):
        (self.free_swsems if d.sw else self.free_dsems).append(d)

    def _wait(self, E, deps):
        w = self.waited[E]
        for key, val in deps:
            if isinstance(key, DSem):
                val = key.cnt
                sem = key.sem
            else:
                if key == E and val > self.ecnt[E]:
                    continue
                if key == E and E == "pe":
                    continue
                sem = self.esem[key]
            if w.get(key, 0) >= val:
                continue
            self.eng[E].wait_ge(sem, val)
            self.ninst += 1
            w[key] = val

    def _deps(self, reads, writes):
        deps = []
        for b in reads:
            if b.w is not None:
                deps.append(b.w)
        for b in writes:
            if b.w is not None:
                deps.append(b.w)
            deps.extend(b.r.items())
        return deps

    def _stamp(self, stamp, reads, writes):
        k, v = stamp
        for b in reads:
            if b.r.get(k, 0) < v:
                b.r[k] = v
        for b in writes:
            b.w = stamp
            b.r = {}

    def op(self, E, fn, reads=(), writes=(), signal=True):
        self._wait(E, self._deps(reads, writes))
        ins = fn(self.eng[E])
        self.ninst += 1
        if signal:
            self.ecnt[E] += 1
            ins.then_inc(self.esem[E], 1)
            stamp = (E, self.ecnt[E])
        else:
            stamp = (E, self.ecnt[E] + 1)
        self._stamp(stamp, reads, writes)
        return ins

    def dma(self, Q, out, in_, sembuf, reads=(), writes=()):
        self._wait(Q, self._deps(reads, writes))
        ins = self.eng[Q].dma_start(out=out, in_=in_)
        self.ninst += 1
        d = sembuf.dsem
        d.cnt += 16
        ins.then_inc(d.sem, 16)
        self._stamp((d, d.cnt), reads, writes)
        return ins

    def barrier(self):
        for E in self.eng:
            deps = [(k, self.ecnt[k]) for k in self.esem if k != E and self.ecnt[k] > 0]
            deps += [(d, d.cnt) for d in self.all_dsems if d.cnt > 0]
            self._wait(E, deps)

    def final_wait(self, E="sp"):
        deps = [(d, d.cnt) for d in self.all_dsems if d.cnt > 0]
        self._wait(E, deps)


def smalls_layout(L):
    off = {}
    o = 0
    for nm in ("ffn1_norm", "mix_norm", "ffn2_norm"):
        off[nm] = o
        o += L * KC
    off["final_norm"] = o
    o += KC
    for nm in ("attn_out_norm", "conv_out_norm", "conv_b", "conv_ln_g", "conv_ln_b"):
        off[nm] = o
        o += L * 16
    off["conv_w"] = o
    o += L * 16 * 31
    return off, o


def pack_smalls(inp, L):
    off, n = smalls_layout(L)
    s = np.zeros((128, n), np.float32)
    for nm in ("ffn1_norm", "mix_norm", "ffn2_norm"):
        a = np.asarray(inp[nm], np.float32)[:L].reshape(L, KC, 128).transpose(2, 0, 1).reshape(128, L * KC)
        s[:, off[nm]:off[nm] + L * KC] = a
    s[:, off["final_norm"]:off["final_norm"] + KC] = np.asarray(inp["final_norm"], np.float32).reshape(KC, 128).T
    for nm in ("attn_out_norm", "conv_out_norm", "conv_b", "conv_ln_g", "conv_ln_b"):
        a = np.asarray(inp[nm], np.float32)[:L].reshape(L, 16, 128).transpose(2, 0, 1).reshape(128, L * 16)
        s[:, off[nm]:off[nm] + L * 16] = a
    cw = np.asarray(inp["conv_w"], np.float32)[:L]
    a = cw.reshape(L, 31, 16, 128).transpose(3, 0, 2, 1).reshape(128, L * 16 * 31)
    s[:, off["conv_w"]:off["conv_w"] + L * 16 * 31] = a
    return s


def win_cols():
    cols = []
    sw = np.concatenate([np.arange(16, 32), np.arange(0, 16)])

    def swap_chunks(base):
        for s in range(4):
            for j in range(4):
                h = 4 * s + j
                cols.append(base + h * 128 + sw)

    swap_chunks(0)
    cols.append(np.arange(0, AW))
    swap_chunks(AW)
    cols.append(np.arange(AW, 2 * AW))
    cols.append(np.arange(2 * AW, 3 * AW))
    for j in range(16):
        cols.append(3 * AW + j * 128 + np.arange(128))
        cols.append(3 * AW + CW + j * 128 + np.arange(128))
    c = np.concatenate(cols)
    assert c.size == NOC * 128
    return c


OC_QSW, OC_Q, OC_KSW, OC_K, OC_V, OC_AG = 0, 4, 20, 24, 40, 56


def tile_w(w, nk, nout):
    return np.ascontiguousarray(
        w.reshape(nk, 128, nout, 128).transpose(2, 1, 0, 3)).reshape(nout * 128, nk * 128)


def tile_wd(w):
    return np.ascontiguousarray(
        w.reshape(2, 24, 128, KC, 128).transpose(0, 3, 2, 1, 4)).reshape(2 * KC * 128, 24 * 128)


def const_tiles():
    j = np.arange(128)[:, None]
    i = np.arange(128)[None, :]
    ident = (j == i).astype(np.float32)
    mA = np.where(j >= i, 0.0, NEG).astype(np.float32)
    mB = np.where(j <= i, 0.0, NEG).astype(np.float32)
    ones = np.ones((128, 128), np.float32)
    m16 = np.concatenate([mA[:, :64], mB[:, :64]], axis=1)
    return np.concatenate([ident, mA, mB, ones, m16], axis=1)


def rope_tables(pos):
    inv = (np.float32(ROPE_THETA) ** (-(np.arange(0, 32, 2, dtype=np.float32) / np.float32(32)))).astype(np.float32)
    ang = (pos.astype(np.float32)[:, None] * inv[None, :]).astype(np.float32)
    c = np.cos(ang).astype(np.float32).T
    s = np.sin(ang).astype(np.float32).T
    one = np.ones((96, pos.shape[0]), np.float32)
    zero = np.zeros((96, pos.shape[0]), np.float32)
    return np.ascontiguousarray(np.concatenate([c, c, one, -s, s, zero], axis=0))


class Cfg:
    def __init__(self, L=4, NCH=8, out_chunks=None, phases=("f1", "mix", "f2"), final=True, debug=(), stop=None):
        self.stop = stop
        self.L = L
        self.NCH = NCH
        self.T = NCH * TCH
        self.out_chunks = list(range(NCH)) if out_chunks is None else list(out_chunks)
        self.phases = phases
        self.final = final
        self.debug = debug


def build_program(cfg):
    L, NCH, T = cfg.L, cfg.NCH, cfg.T
    nc = bass.Bass("TRN2", target_bir_lowering=False)
    soff, NS = smalls_layout(L)

    def din(name, shape, dt=F32):
        return nc.dram_tensor(name, list(shape), dt, kind="ExternalInput").ap()

    xT_in = din("xT", [D, T])
    smalls_d = din("smalls", [128, NS])
    consts_d = din("consts", [128, 640])
    rope_d = din("rope", [256, T])
    kbias_d = din("kbias", [1, PADK + T])
    wts = {}
    for f in ("f1", "f2"):
        wts[f + "g"] = din(f + "g", [L * FCN * 128, KC * 128])
        wts[f + "u"] = din(f + "u", [L * FCN * 128, KC * 128])
        wts[f + "d"] = din(f + "d", [L * 2 * KC * 128, 24 * 128])
    wts["win"] = din("win", [L * NOC * 128, KC * 128])
    wts["wout"] = din("wout", [L * KC * 128, KC * 128])
    nout = len(cfg.out_chunks)
    outT = nc.dram_tensor("outT", [D, nout * TCH], F32, kind="ExternalOutput").ap()

    X = nc.dram_tensor("Xs", [D, T], F32, kind="Internal").ap()
    KT = nc.dram_tensor("KTs", [AW, PADK + T], BF16, kind="Internal").ap()
    QT = nc.dram_tensor("QTs", [AW, TCH], BF16, kind="Internal").ap()
    Vd = nc.dram_tensor("Vs", [PADK + T, AW], BF16, kind="Internal").ap()
    Hd = nc.dram_tensor("Hs", [CW, PADH + T], F32, kind="Internal").ap()

    es = ExitStack()
    with es:
        S = Sched(nc, es)

        uid = [0]

        def sb(stack, name, shape, dt):
            uid[0] += 1
            return stack.enter_context(nc.sbuf_tensor("s%d_%s" % (uid[0], name), list(shape), dt))

        Xb = [[Buf("X%d_%d" % (c, k)) for k in range(KC)] for c in range(NCH)]
        Xin_b = Buf("xin")
        KTb = [[Buf("KT%d_%d" % (c, h)) for h in range(NH)] for c in range(NCH)]
        KTpad = Buf("KTpad")
        QTb = [Buf("QT%d" % h) for h in range(NH)]
        Vb = [Buf("V%d" % c) for c in range(NCH)]
        Vpad = Buf("Vpad")
        Hb = [[Buf("H%d_%d" % (c, j)) for j in range(16)] for c in range(NCH)]
        Hpad = Buf("Hpad")
        Outb = Buf("out")
        cbuf = Buf("const_in")

        smalls = sb(es, "smalls", [128, NS], F32)
        smalls_b = Buf("smalls", S.get_dsem())
        constf = sb(es, "constf", [128, 640], BF16)
        constf_b = Buf("constf", S.get_dsem(True))
        NW = 4
        wslot = [sb(es, "wslot%d" % i, [128, KC, 128], BF16) for i in range(NW)]
        wslot_b = [Buf("wslot%d" % i, S.get_dsem(True)) for i in range(NW)]
        wrr = [0]
        psb = [es.enter_context(nc.psum_tensor("ps%d" % i, [128, 512], F32)) for i in range(8)]
        ps_b = [Buf("ps%d" % i) for i in range(8)]
        prr = [0]

        S.dma("sp", smalls[:], smalls_d[:, :], smalls_b, reads=[cbuf], writes=[smalls_b])
        S.dma("pool", constf[:], consts_d[:, :], constf_b, reads=[cbuf], writes=[constf_b])
        ident = constf[:, 0:128]
        ones_bf = constf[:, 384:512]

        def sm(name, l, n, j):
            o = soff[name] + l * n + j
            return smalls[:, o:o + 1]

        ring = [list(range(8))]

        def next_ps(n=1):
            r = []
            for _ in range(n):
                r.append(ring[0][prr[0] % len(ring[0])])
                prr[0] += 1
            return r

        def load_w(src2d, row0, ncols=KC * 128):
            i = wrr[0] % NW
            wrr[0] += 1
            dst = wslot[i][:].rearrange("p k m -> p (k m)")[:, 0:ncols]
            S.dma("pool", dst, src2d[row0:row0 + 128, 0:ncols], wslot_b[i], reads=[cbuf], writes=[wslot_b[i]])
            return wslot[i], wslot_b[i]

        with ExitStack() as ph:
            z = sb(ph, "zpad", [128, 2048], BF16)
            zb = Buf("zpad", S.get_dsem())
            zf = sb(ph, "zpadf", [128, 16 * PADH], F32)
            zfb = Buf("zpadf", S.get_dsem())
            S.op("dve", lambda e: e.memset(z[:], 0.0), writes=[zb])
            S.op("dve", lambda e: e.memset(zf[:], 0.0), writes=[zfb])
            for j in range(16):
                S.dma("sp", KT[j * 128:(j + 1) * 128, 0:PADK], z[:], zb, reads=[zb], writes=[KTpad])
                S.dma("sp", Vd[j * 128:(j + 1) * 128, :], z[:], zb, reads=[zb], writes=[Vpad])
            S.dma("sp", Hd[:, 0:PADH].rearrange("(j p) t -> p j t", p=128),
                  zf[:].rearrange("p (j t) -> p j t", j=16), zfb, reads=[zfb], writes=[Hpad])
            S.barrier()
            S.put_dsem(zb.dsem)
            S.put_dsem(zfb.dsem)

        def phase_pre(ph, src, src_bufs, c, gname, l, xb, xb_b, want_xb=True, rstd_keep=None):
            rstd = sb(ph, "rstd", [128, TCH], F32)
            rstd_b = Buf("rstd")
            with ExitStack() as sub:
                NXS = 3
                xs = [sb(sub, "xs%d" % i, [128, TCH], F32) for i in range(NXS)]
                xs_b = [Buf("xs%d" % i, S.get_dsem()) for i in range(NXS)]
                sq = [sb(sub, "sq%d" % i, [128, TCH], BF16) for i in range(2)]
                sq_b = [Buf("sq%d" % i) for i in range(2)]
                pss = next_ps(2)
                for kc in range(KC):
                    i = kc % NXS
                    S.dma("sp", xs[i][:], src[kc * 128:(kc + 1) * 128, c * TCH:(c + 1) * TCH], xs_b[i],
                          reads=[src_bufs[kc]], writes=[xs_b[i]])
                    q = kc % 2
                    S.op("act", lambda e, i=i, q=q: e.activation(out=sq[q][:], in_=xs[i][:], func=AF.Square),
                         reads=[xs_b[i]], writes=[sq_b[q]])
                    for th in range(2):
                        S.op("pe", lambda e, q=q, th=th, kc=kc: e.matmul(
                            psb[pss[th]][:], lhsT=ones_bf, rhs=sq[q][:, th * 512:(th + 1) * 512],
                            start=(kc == 0), stop=(kc == KC - 1)),
                            reads=[sq_b[q], constf_b], writes=[ps_b[pss[th]]], signal=(th == 1))
                for th in range(2):
                    S.op("act", lambda e, th=th: e.activation(
                        out=rstd[:, th * 512:(th + 1) * 512], in_=psb[pss[th]][:], func=AF.Sqrt,
                        bias=epsb[:, 0:1], scale=1.0 / D),
                        reads=[ps_b[pss[th]], eps_b], writes=[rstd_b])
                S.op("dve", lambda e: e.reciprocal(out=rstd[:], in_=rstd[:]), reads=[rstd_b], writes=[rstd_b])
                if want_xb:
                    for kc in range(KC):
                        i = kc % NXS
                        S.dma("sp", xs[i][:], src[kc * 128:(kc + 1) * 128, c * TCH:(c + 1) * TCH], xs_b[i],
                              reads=[src_bufs[kc]], writes=[xs_b[i]])
                        S.op("dve", lambda e, i=i, kc=kc: e.scalar_tensor_tensor(
                            out=xb[:, kc, :], in0=xs[i][:], scalar=sm(gname, l, KC, kc), in1=rstd[:],
                            op0=ALU.mult, op1=ALU.mult),
                            reads=[xs_b[i], rstd_b, smalls_b], writes=[xb_b[kc]])
                S.barrier()
                for b in xs_b:
                    S.put_dsem(b.dsem)
            return rstd, rstd_b

        epsb = sb(es, "epsb", [128, 2], F32)
        eps_b = Buf("eps")
        S.op("dve", lambda e: e.memset(epsb[:, 0:1], RMS_EPS), writes=[eps_b])
        S.op("dve", lambda e: e.memset(epsb[:, 1:2], LN_EPS), writes=[eps_b])

        def phase_ffn(l, c, f, gname, src, src_bufs):
            with ExitStack() as ph:
                xb = sb(ph, "xb", [128, KC, TCH], BF16)
                xb_b = [Buf("xb%d" % k) for k in range(KC)]
                phase_pre(ph, src, src_bufs, c, gname, l, xb, xb_b)
                h = sb(ph, "h", [128, 24, TCH], BF16)
                h_b = [Buf("h%d" % k) for k in range(24)]
                sg = [sb(ph, "sg%d" % i, [128, 512], F32) for i in range(2)]
                sg_b = [Buf("sg%d" % i) for i in range(2)]
                NXT = 3
                xt = [sb(ph, "xt%d" % i, [128, 512], F32) for i in range(NXT)]
                xt_b = [Buf("xt%d" % i, S.get_dsem()) for i in range(NXT)]
                xn = [sb(ph, "xn%d" % i, [128, 512], F32) for i in range(NXT)]
                xn_b = [Buf("xn%d" % i, S.get_dsem()) for i in range(NXT)]
                ei = [0]
                xi = [0]
                wg_d, wu_d, wd_d = wts[f + "g"], wts[f + "u"], wts[f + "d"]
                for half in range(2):
                    cur_src, cur_bufs = (src, src_bufs) if half == 0 else (X, Xb[c])
                    for fcl in range(24):
                        fc = half * 24 + fcl
                        wg, wg_b = load_w(wg_d, (l * FCN + fc) * 128)
                        wu, wu_b = load_w(wu_d, (l * FCN + fc) * 128)
                        for th in range(2):
                            pg, pu = next_ps(2)
                            for (w, w_b, p) in ((wg, wg_b, pg), (wu, wu_b, pu)):
                                for kc in range(KC):
                                    S.op("pe", lambda e, w=w, p=p, kc=kc, th=th: e.matmul(
                                        psb[p][:], lhsT=w[:, kc, :], rhs=xb[:, kc, th * 512:(th + 1) * 512],
                                        start=(kc == 0), stop=(kc == KC - 1)),
                                        reads=[w_b, xb_b[kc]], writes=[ps_b[p]], signal=(kc == KC - 1))
                            q = ei[0] % 2
                            ei[0] += 1
                            S.op("act", lambda e, q=q, pg=pg: e.activation(out=sg[q][:], in_=psb[pg][:], func=AF.Silu),
                                 reads=[ps_b[pg]], writes=[sg_b[q]])
                            S.op("dve", lambda e, q=q, pu=pu, fcl=fcl, th=th: e.tensor_tensor(
                                out=h[:, fcl, th * 512:(th + 1) * 512], in0=psb[pu][:], in1=sg[q][:], op=ALU.mult),
                                reads=[ps_b[pu], sg_b[q]], writes=[h_b[fcl]])
                    for dc in range(KC):
                        wd, wd_b = load_w(wd_d, ((l * 2 + half) * KC + dc) * 128, ncols=24 * 128)
                        for th in range(2):
                            i = xi[0] % NXT
                            xi[0] += 1
                            cs = slice(c * TCH + th * 512, c * TCH + (th + 1) * 512)
                            S.dma("sp", xt[i][:], cur_src[dc * 128:(dc + 1) * 128, cs], xt_b[i],
                                  reads=[cur_bufs[dc]], writes=[xt_b[i]])
                            (py,) = next_ps(1)
                            for fcl in range(24):
                                S.op("pe", lambda e, py=py, fcl=fcl, th=th, wd=wd: e.matmul(
                                    psb[py][:], lhsT=wd[:, fcl, :], rhs=h[:, fcl, th * 512:(th + 1) * 512],
                                    start=(fcl == 0), stop=(fcl == 23)),
                                    reads=[wd_b, h_b[fcl]], writes=[ps_b[py]], signal=(fcl == 23))
                            S.op("dve", lambda e, i=i, py=py: e.scalar_tensor_tensor(
                                out=xn[i][:], in0=psb[py][:], scalar=0.5, in1=xt[i][:], op0=ALU.mult, op1=ALU.add),
                                reads=[ps_b[py], xt_b[i]], writes=[xn_b[i]])
                            S.dma("sp", X[dc * 128:(dc + 1) * 128, cs], xn[i][:], xn_b[i],
                                  reads=[xn_b[i]], writes=[Xb[c][dc]])
                S.barrier()
                for b in xt_b + xn_b:
                    S.put_dsem(b.dsem)


        SCALE = float(128 ** -0.5)
        maskAB = constf[:, 128:384]
        mask16 = constf[:, 512:640]

        def proj_chunk(w_d, row0, xb, xb_b, nk=KC):
            w, w_b = load_w(w_d, row0)
            ps = next_ps(2)
            for th in range(2):
                for kc in range(nk):
                    S.op("pe", lambda e, w=w, p=ps[th], kc=kc, th=th: e.matmul(
                        psb[p][:], lhsT=w[:, kc, :], rhs=xb[:, kc, th * 512:(th + 1) * 512],
                        start=(kc == 0), stop=(kc == nk - 1)),
                        reads=[w_b, xb_b[kc]], writes=[ps_b[ps[th]]], signal=(kc == nk - 1))
            return ps

        def phase_mix(l, c):
            win_d, wout_d = wts["win"], wts["wout"]
            wrow = lambda oc: (l * NOC + oc) * 128
            with ExitStack() as ph:
                with ExitStack() as m1:
                    xb = sb(m1, "xb", [128, KC, TCH], BF16)
                    xb_b = [Buf("xb%d" % k) for k in range(KC)]
                    phase_pre(m1, X, Xb[c], c, "mix_norm", l, xb, xb_b)
                    with ExitStack() as qk:
                        rope = sb(qk, "rope", [128, 2, TCH], F32)
                        rope_b = Buf("rope", S.get_dsem())
                        if "q_norope" not in cfg.debug:
                            S.dma("sp", rope[:], rope_d[:, c * TCH:(c + 1) * TCH].rearrange("(a p) t -> p a t", a=2),
                                  rope_b, reads=[cbuf], writes=[rope_b])
                        swp = [sb(qk, "swp%d" % i, [128, TCH], F32) for i in range(4)]
                        swp_b = [Buf("swp%d" % i, S.get_dsem()) for i in range(4)]
                        qsw = [sb(qk, "qsw%d" % i, [128, TCH], F32) for i in range(2)]
                        qsw_b = [Buf("qsw%d" % i, S.get_dsem()) for i in range(2)]
                        t1 = [sb(qk, "t1_%d" % i, [128, TCH], F32) for i in range(2)]
                        t1_b = [Buf("t1_%d" % i) for i in range(2)]
                        t2 = [sb(qk, "t2_%d" % i, [128, TCH], F32) for i in range(2)]
                        for i in range(2):
                            S.op("dve", lambda e, i=i: e.memset(qsw[i][:], 0.0), writes=[qsw_b[i]])
                        t2_b = [Buf("t2_%d" % i) for i in range(2)]
                        qo = [sb(qk, "qo%d" % i, [128, TCH], BF16) for i in range(2)]
                        qo_b = [Buf("qo%d" % i, S.get_dsem()) for i in range(2)]
                        cnt = 0
                        for which, oc_sw, oc_h in ((("q", OC_QSW, OC_Q), ("k", OC_KSW, OC_K)) if "qk" not in cfg.debug else ()):
                            for s4 in range(4):
                                ps = proj_chunk(win_d, wrow(oc_sw + s4), xb, xb_b)
                                for th in range(2):
                                    S.op("act", lambda e, s4=s4, th=th, p=ps[th]: e.copy(
                                        out=swp[s4][:, th * 512:(th + 1) * 512], in_=psb[p][:]),
                                        reads=[ps_b[ps[th]]], writes=[swp_b[s4]])
                            for h in range(NH):
                                i = cnt % 2
                                cnt += 1
                                ps = proj_chunk(win_d, wrow(oc_h + h), xb, xb_b)
                                j4 = h % 4
                                if "q_nodma" not in cfg.debug:
                                    S.dma("sp", qsw[i][0:32, :], swp[h // 4][32 * j4:32 * j4 + 32, :], qsw_b[i],
                                          reads=[swp_b[h // 4]], writes=[qsw_b[i]])
                                for th in range(2):
                                    ts = slice(th * 512, (th + 1) * 512)
                                    if "q_noelem" not in cfg.debug:
                                        S.op("dve", lambda e, i=i, p=ps[th], ts=ts: e.tensor_tensor(
                                            out=t1[i][:, ts], in0=psb[p][:], in1=rope[:, 0, ts], op=ALU.mult),
                                            reads=[ps_b[ps[th]], rope_b], writes=[t1_b[i]])
                                if "q_noelem" not in cfg.debug:
                                    S.op("dve", lambda e, i=i: e.tensor_tensor(
                                        out=t2[i][:], in0=qsw[i][:], in1=rope[:, 1, :], op=ALU.mult),
                                        reads=[qsw_b[i], rope_b], writes=[t2_b[i]])
                                    S.op("dve", lambda e, i=i: e.tensor_tensor(
                                        out=qo[i][:], in0=t1[i][:], in1=t2[i][:], op=ALU.add),
                                        reads=[t1_b[i], t2_b[i]], writes=[qo_b[i]])
                                if which == "q":
                                    S.dma("sp", QT[h * 128:(h + 1) * 128, :], qo[i][:], qo_b[i],
                                          reads=[qo_b[i]], writes=[QTb[h]])
                                else:
                                    S.dma("sp", KT[h * 128:(h + 1) * 128, PADK + c * TCH:PADK + (c + 1) * TCH],
                                          qo[i][:], qo_b[i], reads=[qo_b[i]], writes=[KTb[c][h]])
                        S.barrier()
                        for b in [rope_b] + swp_b + qsw_b + qo_b:
                            S.put_dsem(b.dsem)
                    with ExitStack() as vg:
                        vT = [sb(vg, "vT%d" % i, [128, TCH], BF16) for i in range(2)]
                        vT_b = [Buf("vT%d" % i) for i in range(2)]
                        vtok = sb(vg, "vtok", [128, 8, 1024], BF16)
                        vtok_b = Buf("vtok", S.get_dsem())
                        for hv in (range(NH) if "v" not in cfg.debug else ()):
                            i = hv % 2
                            ps = proj_chunk(win_d, wrow(OC_V + hv), xb, xb_b)
                            for th in range(2):
                                S.op("act", lambda e, i=i, th=th, p=ps[th]: e.copy(
                                    out=vT[i][:, th * 512:(th + 1) * 512], in_=psb[p][:]),
                                    reads=[ps_b[ps[th]]], writes=[vT_b[i]])
                            (pt,) = next_ps(1)
                            ptv = psb[pt][:].bitcast(BF16)
                            for tt in range(8):
                                S.op("pe", lambda e, i=i, tt=tt, ptv=ptv: e.transpose(
                                    ptv[:, tt * 128:(tt + 1) * 128], vT[i][:, tt * 128:(tt + 1) * 128], ident),
                                    reads=[vT_b[i], constf_b], writes=[ps_b[pt]], signal=(tt == 7))
                            hh = hv % 8
                            S.op("dve", lambda e, hh=hh, ptv=ptv: e.tensor_copy(
                                out=vtok[:, :, hh * 128:(hh + 1) * 128],
                                in_=ptv.rearrange("p (t d) -> p t d", t=8)),
                                reads=[ps_b[pt]], writes=[vtok_b])
                            if hh == 7:
                                g8 = hv // 8
                                S.dma("sp", Vd[PADK + c * TCH:PADK + (c + 1) * TCH, g8 * 1024:(g8 + 1) * 1024]
                                      .rearrange("(t p) d -> p t d", p=128), vtok[:], vtok_b,
                                      reads=[vtok_b], writes=[Vb[c]])
                        sgm = [sb(vg, "sgm%d" % i, [128, 512], F32) for i in range(2)]
                        sgm_b = [Buf("sgm%d" % i) for i in range(2)]
                        hT = [sb(vg, "hT%d" % i, [128, TCH], F32) for i in range(2)]
                        hT_b = [Buf("hT%d" % i, S.get_dsem()) for i in range(2)]
                        si = 0
                        for j in (range(16) if "ag" not in cfg.debug else ()):
                            i = j % 2
                            pa = proj_chunk(win_d, wrow(OC_AG + 2 * j), xb, xb_b)
                            pg = proj_chunk(win_d, wrow(OC_AG + 2 * j + 1), xb, xb_b)
                            for th in range(2):
                                q = si % 2
                                si += 1
                                S.op("act", lambda e, q=q, p=pg[th]: e.activation(
                                    out=sgm[q][:], in_=psb[p][:], func=AF.Sigmoid),
                                    reads=[ps_b[pg[th]]], writes=[sgm_b[q]])
                                S.op("dve", lambda e, i=i, q=q, th=th, p=pa[th]: e.tensor_tensor(
                                    out=hT[i][:, th * 512:(th + 1) * 512], in0=psb[p][:], in1=sgm[q][:], op=ALU.mult),
                                    reads=[ps_b[pa[th]], sgm_b[q]], writes=[hT_b[i]])
                            S.dma("sp", Hd[j * 128:(j + 1) * 128, PADH + c * TCH:PADH + (c + 1) * TCH], hT[i][:],
                                  hT_b[i], reads=[hT_b[i]], writes=[Hb[c][j]])
                        S.barrier()
                        for b in [vtok_b] + hT_b:
                            S.put_dsem(b.dsem)

                if cfg.stop == "m1":
                    return
                attnT = sb(ph, "attnT", [128, NH, TCH], BF16)
                attnT_b = [Buf("attnT%d" % h) for h in range(NH)]
                with ExitStack() as at:
                    kb = sb(at, "kb", [1, 3072], BF16)
                    kb_b = Buf("kb", S.get_dsem(True))
                    S.dma("pool", kb[:], kbias_d[0:1, c * TCH:c * TCH + 3072], kb_b, reads=[cbuf], writes=[kb_b])
                    qT = [sb(at, "qT%d" % i, [128, TCH], BF16) for i in range(2)]
                    qT_b = [Buf("qT%d" % i, S.get_dsem()) for i in range(2)]
                    kT = [sb(at, "kT%d" % i, [128, 3072], BF16) for i in range(2)]
                    kT_b = [Buf("kT%d" % i, S.get_dsem()) for i in range(2)]
                    vt = [sb(at, "vt%d" % i, [128, 53, 256], BF16) for i in range(2)]
                    vt_b = [Buf("vt%d" % i, S.get_dsem()) for i in range(2)]
                    NPT = 3
                    pts = [sb(at, "pts%d" % i, [128, 256], BF16) for i in range(NPT)]
                    pts_b = [Buf("pts%d" % i) for i in range(NPT)]
                    rd = [sb(at, "rd%d" % i, [128, 512], F32) for i in range(2)]
                    rd_b = [Buf("rd%d" % i) for i in range(2)]
                    ring[0] = [4, 5, 6, 7]
                    NUM = (0, 1)
                    DEN = (2, 3)
                    kdeps = [KTpad] if c < 2 else []
                    vdeps = [Vpad] if c < 2 else []
                    for cc in range(max(0, c - 2), c + 1):
                        vdeps.append(Vb[cc])
                    pti = 0
                    for h in range(NH):
                        i = h % 2
                        S.dma("sp", qT[i][:], QT[h * 128:(h + 1) * 128, :], qT_b[i], reads=[QTb[h]], writes=[qT_b[i]])
                        S.dma("sp", kT[i][:], KT[h * 128:(h + 1) * 128, c * TCH:c * TCH + 3072], kT_b[i],
                              reads=kdeps + [KTb[cc][h] for cc in range(max(0, c - 2), c + 1)], writes=[kT_b[i]])
                        vi = (h // 2) % 2
                        if h % 2 == 0:
                            cols = slice(h * 128, h * 128 + 256)
                            r1 = PADK + c * TCH - 128
                            S.dma("sp", vt[vi][:, 0:9, :], Vd[r1:r1 + 9 * 128, cols].rearrange("(b p) d -> p b d", p=128),
                                  vt_b[vi], reads=vdeps, writes=[vt_b[vi]])
                            r4 = PADK + c * TCH - 512
                            for r in range(4):
                                S.dma("sp", vt[vi][:, 9 + 3 * r:12 + 3 * r, :],
                                      Vd[r4 + r:r4 + 1536:4, cols].rearrange("(b p) d -> p b d", p=128),
                                      vt_b[vi], reads=vdeps, writes=[vt_b[vi]])
                            rA = PADK + c * TCH - 2048
                            S.dma("sp", vt[vi][:, 21:53:2, :],
                                  Vd[rA:rA + 2048, cols].rearrange("(p r) d -> p r d", r=16),
                                  vt_b[vi], reads=vdeps, writes=[vt_b[vi]])
                            rB = PADK + c * TCH
                            S.dma("sp", vt[vi][0:64, 22:53:2, :],
                                  Vd[rB:rB + 1024, cols].rearrange("(p r) d -> p r d", r=16),
                                  vt_b[vi], reads=vdeps, writes=[vt_b[vi]])
                        hh = h % 2
                        first = [True, True, True, True]
                        tiles = []
                        for qb in range(8):
                            tiles.append((slice(128 * qb, 128 * qb + 128), slice(2048 - 128 + 128 * qb, 2048 + 128 * qb),
                                          slice(2048 + 128 * qb, 2048 + 128 * qb + 128), 128, qb, qb + 1, 128,
                                          [(qb // 4, slice((qb % 4) * 128, (qb % 4) * 128 + 128), slice(0, 128))]))
                        for r in range(4):
                            for qb in range(2):
                                tiles.append((slice(512 * qb + r, 512 * qb + 512, 4),
                                              slice(2048 + 512 * qb - 512 + r, 2048 + 512 * qb, 4),
                                              slice(2048 + 512 * qb + r, 2048 + 512 * qb + 512, 4), 128,
                                              9 + 3 * r + qb, 9 + 3 * r + qb + 1, 128,
                                              [(qb, slice(r, 512, 4), slice(0, 128))]))
                        for r in range(16):
                            tiles.append((slice(r, 1024, 16), slice(r, 2048, 16), slice(2048 + r, 3072, 16), 64,
                                          21 + 2 * r, 22 + 2 * r, 64,
                                          [(0, slice(r, 512, 16), slice(0, 32)), (1, slice(r, 512, 16), slice(32, 64))]))
                        for (qs, ka, kbs, KB, sA, sB, Nq, outs) in tiles:
                            (sp_,) = next_ps(1)
                            pbank = psb[sp_]
                            rds = [kT_b[i], qT_b[i]]
                            S.op("pe", lambda e, pbank=pbank, ka=ka, qs=qs, Nq=Nq, i=i: e.matmul(
                                pbank[:, 0:Nq], lhsT=kT[i][:, ka], rhs=qT[i][:, qs], start=True, stop=False,
                                skip_group_check=True), reads=rds, writes=[ps_b[sp_]], signal=False)
                            S.op("pe", lambda e, pbank=pbank, kbs=kbs, qs=qs, Nq=Nq, KB=KB, i=i: e.matmul(
                                pbank[0:KB, Nq:2 * Nq], lhsT=kT[i][:, kbs], rhs=qT[i][:, qs], start=False, stop=False,
                                skip_group_check=True), reads=rds, writes=[ps_b[sp_]], signal=False)
                            S.op("pe", lambda e, pbank=pbank, ka=ka, Nq=Nq: e.matmul(
                                pbank[:, 0:Nq], lhsT=kb[0:1, ka], rhs=constf[0:1, 384:384 + Nq], start=False, stop=False,
                                skip_group_check=True), reads=[kb_b, constf_b], writes=[ps_b[sp_]], signal=False)
                            msk = maskAB if Nq == 128 else mask16
                            S.op("pe", lambda e, pbank=pbank, Nq=Nq, msk=msk: e.matmul(
                                pbank[:, 0:2 * Nq], lhsT=ident, rhs=msk, start=False, stop=True,
                                skip_group_check=True), reads=[constf_b], writes=[ps_b[sp_]], signal=True)
                            pi = pti % NPT
                            pti += 1
                            S.op("act", lambda e, pbank=pbank, pi=pi, Nq=Nq: e.activation(
                                out=pts[pi][:, 0:2 * Nq], in_=pbank[:, 0:2 * Nq], func=AF.Exp, scale=SCALE),
                                reads=[ps_b[sp_]], writes=[pts_b[pi]])
                            nout_ = len(outs)
                            for oi, (bk, ocs, qsub) in enumerate(outs):
                                for kind in range(2):
                                    bank = (NUM if kind == 0 else DEN)[bk]
                                    fidx = kind * 2 + bk
                                    st = first[fidx]
                                    first[fidx] = False
                                    if kind == 0:
                                        lA = vt[vi][:, sA, hh * 128:(hh + 1) * 128]
                                        lB = vt[vi][0:KB, sB, hh * 128:(hh + 1) * 128]
                                        rdl = [vt_b[vi], pts_b[pi]]
                                    else:
                                        lA = ones_bf
                                        lB = constf[0:KB, 384:512]
                                        rdl = [constf_b, pts_b[pi]]
                                    qa = slice(qsub.start, qsub.stop)
                                    qbb = slice(Nq + qsub.start, Nq + qsub.stop)
                                    S.op("pe", lambda e, bank=bank, ocs=ocs, lA=lA, pi=pi, qa=qa, st=st: e.matmul(
                                        psb[bank][:, ocs], lhsT=lA, rhs=pts[pi][:, qa], start=st, stop=False,
                                        skip_group_check=True), reads=rdl, writes=[ps_b[bank]], signal=False)
                                    last = (oi == nout_ - 1 and kind == 1)
                                    S.op("pe", lambda e, bank=bank, ocs=ocs, lB=lB, pi=pi, qbb=qbb, KB=KB: e.matmul(
                                        psb[bank][:, ocs], lhsT=lB, rhs=pts[pi][0:KB, qbb], start=False, stop=True,
                                        skip_group_check=True), reads=rdl, writes=[ps_b[bank]], signal=last)
                        for bk in range(2):
                            S.op("dve", lambda e, bk=bk: e.tensor_scalar(
                                out=rd[bk][:], in0=psb[DEN[bk]][:], scalar1=1e-30, scalar2=None, op0=ALU.max),
                                reads=[ps_b[DEN[bk]]], writes=[rd_b[bk]])
                            S.op("dve", lambda e, bk=bk: e.reciprocal(out=rd[bk][:], in_=rd[bk][:]),
                                 reads=[rd_b[bk]], writes=[rd_b[bk]])
                            S.op("dve", lambda e, bk=bk, h=h: e.tensor_tensor(
                                out=attnT[:, h, bk * 512:(bk + 1) * 512], in0=psb[NUM[bk]][:], in1=rd[bk][:], op=ALU.mult),
                                reads=[ps_b[NUM[bk]], rd_b[bk]], writes=[attnT_b[h]])
                    ring[0] = list(range(8))
                    S.barrier()
                    for b in [kb_b] + qT_b + kT_b + vt_b:
                        S.put_dsem(b.dsem)

                if cfg.stop == "a":
                    return

                def chan_stats(src3, src_b, n, sq, sq_b, pss, with_sum=None):
                    for j in range(n):
                        q = j % 2
                        S.op("act", lambda e, q=q, j=j: e.activation(out=sq[q][:], in_=src3[:, j, :], func=AF.Square),
                             reads=[src_b[j]], writes=[sq_b[q]])
                        for th in range(2):
                            S.op("pe", lambda e, q=q, th=th, j=j: e.matmul(
                                psb[pss[th]][:], lhsT=ones_bf, rhs=sq[q][:, th * 512:(th + 1) * 512],
                                start=(j == 0), stop=(j == n - 1)),
                                reads=[sq_b[q], constf_b], writes=[ps_b[pss[th]]], signal=(th == 1))

                def rstd_from(pss, dst, dst_b, n, eps_col):
                    for th in range(2):
                        S.op("act", lambda e, th=th: e.activation(
                            out=dst[:, th * 512:(th + 1) * 512], in_=psb[pss[th]][:], func=AF.Sqrt,
                            bias=epsb[:, eps_col:eps_col + 1], scale=1.0 / n),
                            reads=[ps_b[pss[th]], eps_b], writes=[dst_b])
                    S.op("dve", lambda e: e.reciprocal(out=dst[:], in_=dst[:]), reads=[dst_b], writes=[dst_b])

                with ExitStack() as an:
                    sq = [sb(an, "asq%d" % i, [128, TCH], BF16) for i in range(2)]
                    sq_b = [Buf("asq%d" % i) for i in range(2)]
                    rsa = sb(an, "rsa", [128, TCH], F32)
                    rsa_b = Buf("rsa")
                    pss = next_ps(2)
                    chan_stats(attnT, attnT_b, NH, sq, sq_b, pss)
                    rstd_from(pss, rsa, rsa_b, AW, 0)
                    for h in range(NH):
                        S.op("dve", lambda e, h=h: e.scalar_tensor_tensor(
                            out=attnT[:, h, :], in0=attnT[:, h, :], scalar=sm("attn_out_norm", l, 16, h), in1=rsa[:],
                            op0=ALU.mult, op1=ALU.mult),
                            reads=[attnT_b[h], rsa_b, smalls_b], writes=[attnT_b[h]])
                    S.barrier()

                convT = sb(ph, "convT", [128, 16, TCH], BF16)
                convT_b = [Buf("convT%d" % j) for j in range(16)]
                with ExitStack() as cv:
                    hin = [sb(cv, "hin%d" % i, [128, TCH + PADH], F32) for i in range(2)]
                    hin_b = [Buf("hin%d" % i, S.get_dsem()) for i in range(2)]
                    acc = [sb(cv, "acc%d" % i, [128, TCH], F32) for i in range(2)]
                    acc_b = [Buf("acc%d" % i) for i in range(2)]
                    sq = [sb(cv, "csq%d" % i, [128, TCH], BF16) for i in range(2)]
                    sq_b = [Buf("csq%d" % i) for i in range(2)]
                    mean = sb(cv, "mean", [128, TCH], F32)
                    mean_b = Buf("mean")
                    rln = sb(cv, "rln", [128, TCH], F32)
                    rln_b = Buf("rln")
                    tmp = [sb(cv, "ctmp%d" % i, [128, TCH], F32) for i in range(2)]
                    tmp_b = [Buf("ctmp%d" % i) for i in range(2)]
                    psum_s = next_ps(2)
                    psum_q = next_ps(2)
                    wo = soff["conv_w"] + l * 16 * 31
                    for j in range(16):
                        i = j % 2
                        hdeps = [Hb[c][j]] + ([Hb[c - 1][j]] if c > 0 else [Hpad])
                        S.dma("sp", hin[i][:], Hd[j * 128:(j + 1) * 128, c * TCH:c * TCH + TCH + PADH], hin_b[i],
                              reads=hdeps, writes=[hin_b[i]])
                        wc = lambda k, j=j: smalls[:, wo + j * 31 + k:wo + j * 31 + k + 1]
                        S.op("dve", lambda e, i=i, j=j, wc=wc: e.tensor_scalar(
                            out=acc[i][:], in0=hin[i][:, 2:2 + TCH], scalar1=wc(0), scalar2=sm("conv_b", l, 16, j),
                            op0=ALU.mult, op1=ALU.add),
                            reads=[hin_b[i], smalls_b], writes=[acc_b[i]])
                        for k in range(1, 31):
                            S.op("dve", lambda e, i=i, k=k, wc=wc: e.scalar_tensor_tensor(
                                out=acc[i][:], in0=hin[i][:, 2 + k:2 + k + TCH], scalar=wc(k), in1=acc[i][:],
                                op0=ALU.mult, op1=ALU.add),
                                reads=[hin_b[i], smalls_b, acc_b[i]], writes=[acc_b[i]])
                        S.op("act", lambda e, i=i, j=j: e.copy(out=convT[:, j, :], in_=acc[i][:]),
                             reads=[acc_b[i]], writes=[convT_b[j]])
                        S.op("act", lambda e, i=i: e.activation(out=sq[i][:], in_=acc[i][:], func=AF.Square),
                             reads=[acc_b[i]], writes=[sq_b[i]])
                        for th in range(2):
                            ts = slice(th * 512, (th + 1) * 512)
                            S.op("pe", lambda e, j=j, th=th, ts=ts: e.matmul(
                                psb[psum_s[th]][:], lhsT=ones_bf, rhs=convT[:, j, ts], start=(j == 0), stop=(j == 15)),
                                reads=[convT_b[j], constf_b], writes=[ps_b[psum_s[th]]], signal=False)
                            S.op("pe", lambda e, i=i, j=j, th=th, ts=ts: e.matmul(
                                psb[psum_q[th]][:], lhsT=ones_bf, rhs=sq[i][:, ts], start=(j == 0), stop=(j == 15)),
                                reads=[sq_b[i], constf_b], writes=[ps_b[psum_q[th]]], signal=True)
                    for th in range(2):
                        ts = slice(th * 512, (th + 1) * 512)
                        S.op("dve", lambda e, th=th, ts=ts: e.tensor_scalar(
                            out=mean[:, ts], in0=psb[psum_s[th]][:], scalar1=1.0 / CW, scalar2=None, op0=ALU.mult),
                            reads=[ps_b[psum_s[th]]], writes=[mean_b])
                        S.op("dve", lambda e, ts=ts: e.tensor_tensor(
                            out=tmp[0][:, ts], in0=mean[:, ts], in1=mean[:, ts], op=ALU.mult),
                            reads=[mean_b], writes=[tmp_b[0]])
                        S.op("dve", lambda e, th=th, ts=ts: e.scalar_tensor_tensor(
                            out=rln[:, ts], in0=psb[psum_q[th]][:], scalar=1.0 / CW, in1=tmp[0][:, ts],
                            op0=ALU.mult, op1=ALU.subtract),
                            reads=[ps_b[psum_q[th]], tmp_b[0]], writes=[rln_b])
                    S.op("act", lambda e: e.activation(out=rln[:], in_=rln[:], func=AF.Sqrt, bias=epsb[:, 1:2], scale=1.0),
                         reads=[rln_b, eps_b], writes=[rln_b])
                    S.op("dve", lambda e: e.reciprocal(out=rln[:], in_=rln[:]), reads=[rln_b], writes=[rln_b])
                    pss = next_ps(2)
                    for j in range(16):
                        i = j % 2
                        S.op("dve", lambda e, i=i, j=j: e.tensor_tensor(
                            out=tmp[i][:], in0=convT[:, j, :], in1=mean[:], op=ALU.subtract),
                            reads=[convT_b[j], mean_b], writes=[tmp_b[i]])
                        S.op("dve", lambda e, i=i: e.tensor_tensor(
                            out=tmp[i][:], in0=tmp[i][:], in1=rln[:], op=ALU.mult),
                            reads=[tmp_b[i], rln_b], writes=[tmp_b[i]])
                        S.op("act", lambda e, i=i, j=j: e.activation(
                            out=convT[:, j, :], in_=tmp[i][:], func=AF.Silu,
                            bias=sm("conv_ln_b", l, 16, j), scale=sm("conv_ln_g", l, 16, j)),
                            reads=[tmp_b[i], smalls_b], writes=[convT_b[j]])
                        S.op("act", lambda e, i=i, j=j: e.activation(out=sq[i][:], in_=convT[:, j, :], func=AF.Square),
                             reads=[convT_b[j]], writes=[sq_b[i]])
                        for th in range(2):
                            S.op("pe", lambda e, i=i, th=th, j=j: e.matmul(
                                psb[pss[th]][:], lhsT=ones_bf, rhs=sq[i][:, th * 512:(th + 1) * 512],
                                start=(j == 0), stop=(j == 15)),
                                reads=[sq_b[i], constf_b], writes=[ps_b[pss[th]]], signal=(th == 1))
                    rsc = mean
                    rstd_from(pss, rsc, mean_b, CW, 0)
                    for j in range(16):
                        S.op("dve", lambda e, j=j: e.scalar_tensor_tensor(
                            out=convT[:, j, :], in0=convT[:, j, :], scalar=sm("conv_out_norm", l, 16, j), in1=rsc[:],
                            op0=ALU.mult, op1=ALU.mult),
                            reads=[convT_b[j], mean_b, smalls_b], writes=[convT_b[j]])
                    S.barrier()
                    for b in hin_b:
                        S.put_dsem(b.dsem)

                if cfg.stop == "c":
                    return
                with ExitStack() as m2:
                    NXT = 3
                    xt = [sb(m2, "mxt%d" % i, [128, 512], F32) for i in range(NXT)]
                    xt_b = [Buf("mxt%d" % i, S.get_dsem()) for i in range(NXT)]
                    xn = [sb(m2, "mxn%d" % i, [128, 512], F32) for i in range(NXT)]
                    xn_b = [Buf("mxn%d" % i, S.get_dsem()) for i in range(NXT)]
                    xi = 0
                    for dc in range(KC):
                        w, w_b = load_w(wout_d, (l * KC + dc) * 128)
                        for th in range(2):
                            i = xi % NXT
                            xi += 1
                            cs = slice(c * TCH + th * 512, c * TCH + (th + 1) * 512)
                            ts = slice(th * 512, (th + 1) * 512)
                            S.dma("sp", xt[i][:], X[dc * 128:(dc + 1) * 128, cs], xt_b[i],
                                  reads=[Xb[c][dc]], writes=[xt_b[i]])
                            (py,) = next_ps(1)
                            for kc in range(KC):
                                if kc < 16:
                                    rhs, rb = attnT[:, kc, ts], attnT_b[kc]
                                else:
                                    rhs, rb = convT[:, kc - 16, ts], convT_b[kc - 16]
                                S.op("pe", lambda e, py=py, w=w, kc=kc, rhs=rhs: e.matmul(
                                    psb[py][:], lhsT=w[:, kc, :], rhs=rhs, start=(kc == 0), stop=(kc == KC - 1)),
                                    reads=[w_b, rb], writes=[ps_b[py]], signal=(kc == KC - 1))
                            S.op("dve", lambda e, i=i, py=py: e.tensor_tensor(
                                out=xn[i][:], in0=psb[py][:], in1=xt[i][:], op=ALU.add),
                                reads=[ps_b[py], xt_b[i]], writes=[xn_b[i]])
                            S.dma("sp", X[dc * 128:(dc + 1) * 128, cs], xn[i][:], xn_b[i],
                                  reads=[xn_b[i]], writes=[Xb[c][dc]])
                    S.barrier()
                    for b in xt_b + xn_b:
                        S.put_dsem(b.dsem)

        def phase_final(c, oc):
            with ExitStack() as ph:
                rstd, rstd_b = phase_pre(ph, X, Xb[c], c, "final_norm", 0, None, None, want_xb=False)
                NXS = 3
                xs = [sb(ph, "fx%d" % i, [128, TCH], F32) for i in range(NXS)]
                xs_b = [Buf("fx%d" % i, S.get_dsem()) for i in range(NXS)]
                xo = [sb(ph, "fo%d" % i, [128, TCH], F32) for i in range(NXS)]
                xo_b = [Buf("fo%d" % i, S.get_dsem()) for i in range(NXS)]
                for kc in range(KC):
                    i = kc % NXS
                    S.dma("sp", xs[i][:], X[kc * 128:(kc + 1) * 128, c * TCH:(c + 1) * TCH], xs_b[i],
                          reads=[Xb[c][kc]], writes=[xs_b[i]])
                    o = soff["final_norm"] + kc
                    S.op("dve", lambda e, i=i, o=o: e.scalar_tensor_tensor(
                        out=xo[i][:], in0=xs[i][:], scalar=smalls[:, o:o + 1], in1=rstd[:],
                        op0=ALU.mult, op1=ALU.mult),
                        reads=[xs_b[i], rstd_b, smalls_b], writes=[xo_b[i]])
                    S.dma("sp", outT[kc * 128:(kc + 1) * 128, oc * TCH:(oc + 1) * TCH], xo[i][:], xo_b[i],
                          reads=[xo_b[i]], writes=[Outb])
                S.barrier()
                for b in xs_b + xo_b:
                    S.put_dsem(b.dsem)

        def phase_copy_in(c):
            with ExitStack() as ph:
                NXS = 3
                xs = [sb(ph, "ci%d" % i, [128, TCH], F32) for i in range(NXS)]
                xs_b = [Buf("ci%d" % i, S.get_dsem()) for i in range(NXS)]
                for kc in range(KC):
                    i = kc % NXS
                    S.dma("sp", xs[i][:], xT_in[kc * 128:(kc + 1) * 128, c * TCH:(c + 1) * TCH], xs_b[i],
                          reads=[Xin_b], writes=[xs_b[i]])
                    S.dma("sp", X[kc * 128:(kc + 1) * 128, c * TCH:(c + 1) * TCH], xs[i][:], xs_b[i],
                          reads=[xs_b[i]], writes=[Xb[c][kc]])
                S.barrier()
                for b in xs_b:
                    S.put_dsem(b.dsem)

        from_in = [True] * NCH
        xin_bufs = [Xin_b] * KC
        for l in range(L):
            for c in range(NCH):
                if "f1" in cfg.phases:
                    if from_in[c]:
                        phase_ffn(l, c, "f1", "ffn1_norm", xT_in, xin_bufs)
                        from_in[c] = False
                    else:
                        phase_ffn(l, c, "f1", "ffn1_norm", X, Xb[c])
                if "mix" in cfg.phases:
                    if from_in[c]:
                        phase_copy_in(c)
                        from_in[c] = False
                    phase_mix(l, c)
                if "f2" in cfg.phases:
                    phase_ffn(l, c, "f2", "ffn2_norm", X, Xb[c])
        if cfg.final:
            for oc, c in enumerate(cfg.out_chunks):
                phase_final(c, oc)
        S.final_wait("sp")
        print("instructions emitted:", S.ninst, file=sys.stderr)
    return nc


def make_inputs(inp, cfg, win_start=0):
    L, T = cfg.L, cfg.T
    x = np.asarray(inp["x"], np.float32)[0]
    Sq = x.shape[0]
    xw = np.zeros((T, D), np.float32)
    lo = max(win_start, 0)
    hi = min(win_start + T, Sq)
    xw[lo - win_start:hi - win_start] = x[lo:hi]
    m = {"xT": np.ascontiguousarray(xw.T)}
    m["smalls"] = pack_smalls(inp, L)
    m["consts"] = const_tiles()
    pos = np.arange(win_start, win_start + T).astype(np.float32)
    m["rope"] = rope_tables(pos)
    gpos = np.arange(win_start - PADK, win_start + T)
    m["kbias"] = np.where(gpos >= 0, 0.0, NEG).astype(np.float32)[None, :]
    for f, pre in (("f1", "ffn1"), ("f2", "ffn2")):
        m[f + "g"] = np.concatenate([tile_w(np.asarray(inp[pre + "_w_gate"][l], np.float32), KC, FCN) for l in range(L)], 0)
        m[f + "u"] = np.concatenate([tile_w(np.asarray(inp[pre + "_w_up"][l], np.float32), KC, FCN) for l in range(L)], 0)
        m[f + "d"] = np.concatenate([tile_wd(np.asarray(inp[pre + "_w_down"][l], np.float32)) for l in range(L)], 0)
    cols = win_cols()
    m["win"] = np.concatenate([tile_w(np.asarray(inp["w_in"][l], np.float32)[:, cols], KC, NOC) for l in range(L)], 0)
    m["wout"] = np.concatenate([tile_w(np.asarray(inp["w_out"][l], np.float32), KC, KC) for l in range(L)], 0)
    return m


_PROG = {}


def kernel(**inputs):
    cfg = Cfg(L=4, NCH=8)
    key = "full"
    if key not in _PROG:
        _PROG[key] = build_program(cfg)
    nc = _PROG[key]
    m = make_inputs(inputs, cfg, 0)
    res = run_bass_kernel_spmd(nc, [m], core_ids=[0])
    oT = res.results[0]["outT"]
    return np.ascontiguousarray(oT.T)[None].astype(np.float32)
```

```python
import sys
from contextlib import ExitStack
import numpy as np
import concourse.bass as bass
import concourse.mybir as mybir
from concourse.bass_utils import run_bass_kernel_spmd

F32 = mybir.dt.float32
BF16 = mybir.dt.bfloat16
AF = mybir.ActivationFunctionType
ALU = mybir.AluOpType

D = 4096
DFF = 6144
KC = D // 128
FCN = DFF // 128
NH = 16
AW = 2048
CW = 2048
TCH = 1024
NOC = 88
PADK = 2048
PADH = 32
NEG = -30000.0
RMS_EPS = 1e-5
LN_EPS = 1e-5
ROPE_THETA = 500000.0
BRANCH_D = (1, 4, 16)


class DSem:
    def __init__(self, sem):
        self.sem = sem
        self.cnt = 0


class Buf:
    __slots__ = ("name", "w", "r", "dsem")

    def __init__(self, name, dsem=None):
        self.name = name
        self.w = None
        self.r = {}
        self.dsem = dsem


class Sched:
    def __init__(self, nc, stack, n_dsems=40):
        self.nc = nc
        self.eng = {"pe": nc.tensor, "act": nc.scalar, "dve": nc.vector, "pool": nc.gpsimd, "sp": nc.sync}
        self.esem = {}
        self.ecnt = {}
        for k in ("pe", "act", "dve", "pool"):
            self.esem[k] = stack.enter_context(nc.semaphore("e_" + k))
            self.ecnt[k] = 0
        self.waited = {k: {} for k in self.eng}
        self.free_dsems = [DSem(stack.enter_context(nc.semaphore("d%d" % i))) for i in range(n_dsems)]
        self.free_swsems = [DSem(stack.enter_context(nc.semaphore("w%d" % i))) for i in range(10)]
        self.all_dsems = list(self.free_dsems) + list(self.free_swsems)
        self.ninst = 0

    def get_dsem(self, sw=False):
        d = (self.free_swsems if sw else self.free_dsems).pop()
        d.sw = sw
        return d

    def put_dsem(self, d):
        (self.free_swsems if d.sw else self.free_dsems).append(d)

    def _wait(self, E, deps):
        w = self.waited[E]
        for key, val in deps:
            if isinstance(key, DSem):
                val = key.cnt
                sem = key.sem
            else:
                if key == E and val > self.ecnt[E]:
                    continue
                if key == E and E == "pe":
                    continue
                sem = self.esem[key]
            if w.get(key, 0) >= val:
                continue
            self.eng[E].wait_ge(sem, val)
            self.ninst += 1
            w[key] = val

    def _deps(self, reads, writes):
        deps = []
        for b in reads:
            if b.w is not None:
                deps.append(b.w)
        for b in writes:
            if b.w is not None:
                deps.append(b.w)
            deps.extend(b.r.items())
        return deps

    def _stamp(self, stamp, reads, writes):
        k, v = stamp
        for b in reads:
            if b.r.get(k, 0) < v:
                b.r[k] = v
        for b in writes:
            b.w = stamp
            b.r = {}

    def op(self, E, fn, reads=(), writes=(), signal=True):
        self._wait(E, self._deps(reads, writes))
        ins = fn(self.eng[E])
        self.ninst += 1
        if signal:
            self.ecnt[E] += 1
            ins.then_inc(self.esem[E], 1)
            stamp = (E, self.ecnt[E])
        else:
            stamp = (E, self.ecnt[E] + 1)
        self._stamp(stamp, reads, writes)
        return ins

    def dma(self, Q, out, in_, sembuf, reads=(), writes=()):
        self._wait(Q, self._deps(reads, writes))
        ins = self.eng[Q].dma_start(out=out, in_=in_)
        self.ninst += 1
        d = sembuf.dsem
        d.cnt += 16
        ins.then_inc(d.sem, 16)
        self._stamp((d, d.cnt), reads, writes)
        return ins

    def barrier(self):
        for E in self.eng:
            deps = [(k, self.ecnt[k]) for k in self.esem if k != E and self.ecnt[k] > 0]
            deps += [(d, d.cnt) for d in self.all_dsems if d.cnt > 0]
            self._wait(E, deps)

    def final_wait(self, E="sp"):
        deps = [(d, d.cnt) for d in self.all_dsems if d.cnt > 0]
        self._wait(E, deps)


def smalls_layout(L):
    off = {}
    o = 0
    for nm in ("ffn1_norm", "mix_norm", "ffn2_norm"):
        off[nm] = o
        o += L * KC
    off["final_norm"] = o
    o += KC
    for nm in ("attn_out_norm", "conv_out_norm", "conv_b", "conv_ln_g", "conv_ln_b"):
        off[nm] = o
        o += L * 16
    off["conv_w"] = o
    o += L * 16 * 31
    return off, o


def pack_smalls(inp, L):
    off, n = smalls_layout(L)
    s = np.zeros((128, n), np.float32)
    for nm in ("ffn1_norm", "mix_norm", "ffn2_norm"):
        a = np.asarray(inp[nm], np.float32)[:L].reshape(L, KC, 128).transpose(2, 0, 1).reshape(128, L * KC)
        s[:, off[nm]:off[nm] + L * KC] = a
    s[:, off["final_norm"]:off["final_norm"] + KC] = np.asarray(inp["final_norm"], np.float32).reshape(KC, 128).T
    for nm in ("attn_out_norm", "conv_out_norm", "conv_b", "conv_ln_g", "conv_ln_b"):
        a = np.asarray(inp[nm], np.float32)[:L].reshape(L, 16, 128).transpose(2, 0, 1).reshape(128, L * 16)
        s[:, off[nm]:off[nm] + L * 16] = a
    cw = np.asarray(inp["conv_w"], np.float32)[:L]
    a = cw.reshape(L, 31, 16, 128).transpose(3, 0, 2, 1).reshape(128, L * 16 * 31)
    s[:, off["conv_w"]:off["conv_w"] + L * 16 * 31] = a
    return s


def win_cols():
    cols = []
    sw = np.concatenate([np.arange(16, 32), np.arange(0, 16)])

    def swap_chunks(base):
        for s in range(4):
            for j in range(4):
                h = 4 * s + j
                cols.append(base + h * 128 + sw)

    swap_chunks(0)
    cols.append(np.arange(0, AW))
    swap_chunks(AW)
    cols.append(np.arange(AW, 2 * AW))
    cols.append(np.arange(2 * AW, 3 * AW))
    for j in range(16):
        cols.append(3 * AW + j * 128 + np.arange(128))
        cols.append(3 * AW + CW + j * 128 + np.arange(128))
    c = np.concatenate(cols)
    assert c.size == NOC * 128
    return c


OC_QSW, OC_Q, OC_KSW, OC_K, OC_V, OC_AG = 0, 4, 20, 24, 40, 56


def tile_w(w, nk, nout):
    return np.ascontiguousarray(
        w.reshape(nk, 128, nout, 128).transpose(2, 1, 0, 3)).reshape(nout * 128, nk * 128)


def tile_wd(w):
    return np.ascontiguousarray(
        w.reshape(2, 24, 128, KC, 128).transpose(0, 3, 2, 1, 4)).reshape(2 * KC * 128, 24 * 128)


def const_tiles():
    j = np.arange(128)[:, None]
    i = np.arange(128)[None, :]
    ident = (j == i).astype(np.float32)
    mA = np.where(j >= i, 0.0, NEG).astype(np.float32)
    mB = np.where(j <= i, 0.0, NEG).astype(np.float32)
    ones = np.ones((128, 128), np.float32)
    m16 = np.concatenate([mA[:, :64], mB[:, :64]], axis=1)
    z1 = lambda m: (m == 0).astype(np.float32)
    return np.concatenate([ident, mA, mB, ones, m16, z1(mA), z1(mB), z1(m16)], axis=1)


def rope_tables(pos):
    inv = (np.float32(ROPE_THETA) ** (-(np.arange(0, 32, 2, dtype=np.float32) / np.float32(32)))).astype(np.float32)
    ang = (pos.astype(np.float32)[:, None] * inv[None, :]).astype(np.float32)
    c = np.cos(ang).astype(np.float32).T
    s = np.sin(ang).astype(np.float32).T
    one = np.ones((96, pos.shape[0]), np.float32)
    zero = np.zeros((96, pos.shape[0]), np.float32)
    return np.ascontiguousarray(np.concatenate([c, c, one, -s, s, zero], axis=0))


class Cfg:
    def __init__(self, L=4, NCH=8, out_chunks=None, phases=("f1", "mix", "f2"), final=True, debug=(), stop=None):
        self.stop = stop
        self.L = L
        self.NCH = NCH
        self.T = NCH * TCH
        self.out_chunks = list(range(NCH)) if out_chunks is None else list(out_chunks)
        self.phases = phases
        self.final = final
        self.debug = debug


def build_program(cfg):
    L, NCH, T = cfg.L, cfg.NCH, cfg.T
    nc = bass.Bass("TRN2", target_bir_lowering=False)
    soff, NS = smalls_layout(L)

    def din(name, shape, dt=F32):
        return nc.dram_tensor(name, list(shape), dt, kind="ExternalInput").ap()

    xT_in = din("xT", [D, T])
    smalls_d = din("smalls", [128, NS])
    consts_d = din("consts", [128, 1024])
    rope_d = din("rope", [256, T])
    kbias_d = din("kbias", [1, PADK + T])
    wts = {}
    for f in ("f1", "f2"):
        wts[f + "g"] = din(f + "g", [L * FCN * 128, KC * 128])
        wts[f + "u"] = din(f + "u", [L * FCN * 128, KC * 128])
        wts[f + "d"] = din(f + "d", [L * 2 * KC * 128, 24 * 128])
    wts["win"] = din("win", [L * NOC * 128, KC * 128])
    wts["wout"] = din("wout", [L * KC * 128, KC * 128])
    nout = len(cfg.out_chunks)
    outT = nc.dram_tensor("outT", [D, nout * TCH], F32, kind="ExternalOutput").ap()

    X = nc.dram_tensor("Xs", [D, T], F32, kind="Internal").ap()
    KT = nc.dram_tensor("KTs", [AW, PADK + T], BF16, kind="Internal").ap()
    QT = nc.dram_tensor("QTs", [AW, TCH], BF16, kind="Internal").ap()
    Vd = nc.dram_tensor("Vs", [PADK + T, AW], BF16, kind="Internal").ap()
    Hd = nc.dram_tensor("Hs", [CW, PADH + T], BF16, kind="Internal").ap()

    es = ExitStack()
    with es:
        S = Sched(nc, es)

        uid = [0]

        def sb(stack, name, shape, dt):
            uid[0] += 1
            return stack.enter_context(nc.sbuf_tensor("s%d_%s" % (uid[0], name), list(shape), dt))

        Xb = [[Buf("X%d_%d" % (c, k)) for k in range(KC)] for c in range(NCH)]
        Xin_b = Buf("xin")
        KTb = [[Buf("KT%d_%d" % (c, h)) for h in range(NH)] for c in range(NCH)]
        KTpad = Buf("KTpad")
        QTb = [Buf("QT%d" % h) for h in range(NH)]
        Vb = [Buf("V%d" % c) for c in range(NCH)]
        Vpad = Buf("Vpad")
        Hb = [[Buf("H%d_%d" % (c, j)) for j in range(16)] for c in range(NCH)]
        Hpad = Buf("Hpad")
        Outb = Buf("out")
        cbuf = Buf("const_in")

        smalls = sb(es, "smalls", [128, NS], F32)
        smalls_b = Buf("smalls", S.get_dsem())
        constf = sb(es, "constf", [128, 1024], BF16)
        constf_b = Buf("constf", S.get_dsem(True))
        NW = 4
        wslot = [sb(es, "wslot%d" % i, [128, KC, 128], BF16) for i in range(NW)]
        wslot_b = [Buf("wslot%d" % i, S.get_dsem(True)) for i in range(NW)]
        wrr = [0]
        psb = [es.enter_context(nc.psum_tensor("ps%d" % i, [128, 512], F32)) for i in range(8)]
        ps_b = [Buf("ps%d" % i) for i in range(8)]
        prr = [0]

        S.dma("sp", smalls[:], smalls_d[:, :], smalls_b, reads=[cbuf], writes=[smalls_b])
        S.dma("pool", constf[:], consts_d[:, :], constf_b, reads=[cbuf], writes=[constf_b])
        ident = constf[:, 0:128]
        ones_bf = constf[:, 384:512]

        def sm(name, l, n, j):
            o = soff[name] + l * n + j
            return smalls[:, o:o + 1]

        ring = [list(range(8))]

        def next_ps(n=1):
            r = []
            for _ in range(n):
                r.append(ring[0][prr[0] % len(ring[0])])
                prr[0] += 1
            return r

        def load_w(src2d, row0, ncols=KC * 128):
            i = wrr[0] % NW
            wrr[0] += 1
            dst = wslot[i][:].rearrange("p k m -> p (k m)")[:, 0:ncols]
            S.dma("pool", dst, src2d[row0:row0 + 128, 0:ncols], wslot_b[i], reads=[cbuf], writes=[wslot_b[i]])
            return wslot[i], wslot_b[i]

        with ExitStack() as ph:
            z = sb(ph, "zpad", [128, 2048], BF16)
            zb = Buf("zpad", S.get_dsem())
            zf = sb(ph, "zpadf", [128, 16 * PADH], BF16)
            zfb = Buf("zpadf", S.get_dsem())
            S.op("dve", lambda e: e.memset(z[:], 0.0), writes=[zb])
            S.op("dve", lambda e: e.memset(zf[:], 0.0), writes=[zfb])
            for j in range(16):
                S.dma("sp", KT[j * 128:(j + 1) * 128, 0:PADK], z[:], zb, reads=[zb], writes=[KTpad])
                S.dma("sp", Vd[j * 128:(j + 1) * 128, :], z[:], zb, reads=[zb], writes=[Vpad])
            S.dma("sp", Hd[:, 0:PADH].rearrange("(j p) t -> p j t", p=128),
                  zf[:].rearrange("p (j t) -> p j t", j=16), zfb, reads=[zfb], writes=[Hpad])
            S.barrier()
            S.put_dsem(zb.dsem)
            S.put_dsem(zfb.dsem)

        def phase_pre(ph, src, src_bufs, c, gname, l, xb, xb_b, want_xb=True, rstd_keep=None):
            rstd = sb(ph, "rstd", [128, TCH], F32)
            rstd_b = Buf("rstd")
            with ExitStack() as sub:
                NXS = 4
                xs = [sb(sub, "xs%d" % i, [128, TCH], F32) for i in range(NXS)]
                xs_b = [Buf("xs%d" % i, S.get_dsem()) for i in range(NXS)]
                sq = [sb(sub, "sq%d" % i, [128, TCH], BF16) for i in range(2)]
                sq_b = [Buf("sq%d" % i) for i in range(2)]
                pss = next_ps(2)
                for kc in range(KC):
                    i = kc % NXS
                    S.dma("sp", xs[i][:], src[kc * 128:(kc + 1) * 128, c * TCH:(c + 1) * TCH], xs_b[i],
                          reads=[src_bufs[kc]], writes=[xs_b[i]])
                    q = kc % 2
                    S.op("act", lambda e, i=i, q=q: e.activation(out=sq[q][:], in_=xs[i][:], func=AF.Square),
                         reads=[xs_b[i]], writes=[sq_b[q]])
                    if want_xb:
                        S.op("dve", lambda e, i=i, kc=kc: e.tensor_scalar(
                            out=xb[:, kc, :], in0=xs[i][:], scalar1=sm(gname, l, KC, kc), scalar2=None, op0=ALU.mult),
                            reads=[xs_b[i], smalls_b], writes=[xb_b[kc]])
                    for th in range(2):
                        S.op("pe", lambda e, q=q, th=th, kc=kc: e.matmul(
                            psb[pss[th]][:], lhsT=ones_bf, rhs=sq[q][:, th * 512:(th + 1) * 512],
                            start=(kc == 0), stop=(kc == KC - 1)),
                            reads=[sq_b[q], constf_b], writes=[ps_b[pss[th]]], signal=(th == 1))
                for th in range(2):
                    S.op("act", lambda e, th=th: e.activation(
                        out=rstd[:, th * 512:(th + 1) * 512], in_=psb[pss[th]][:], func=AF.Sqrt,
                        bias=epsb[:, 0:1], scale=1.0 / D),
                        reads=[ps_b[pss[th]], eps_b], writes=[rstd_b])
                S.op("dve", lambda e: e.reciprocal(out=rstd[:], in_=rstd[:]), reads=[rstd_b], writes=[rstd_b])
                if want_xb:
                    for kc in range(KC):
                        S.op("dve", lambda e, kc=kc: e.tensor_tensor(
                            out=xb[:, kc, :], in0=xb[:, kc, :], in1=rstd[:], op=ALU.mult),
                            reads=[rstd_b], writes=[xb_b[kc]])
                S.barrier()
                for b in xs_b:
                    S.put_dsem(b.dsem)
            return rstd, rstd_b

        epsb = sb(es, "epsb", [128, 2], F32)
        eps_b = Buf("eps")
        S.op("dve", lambda e: e.memset(epsb[:, 0:1], RMS_EPS), writes=[eps_b])
        S.op("dve", lambda e: e.memset(epsb[:, 1:2], LN_EPS), writes=[eps_b])

        def phase_ffn(l, c, f, gname, src, src_bufs):
            with ExitStack() as ph:
                xb = sb(ph, "xb", [128, KC, TCH], BF16)
                xb_b = [Buf("xb%d" % k) for k in range(KC)]
                phase_pre(ph, src, src_bufs, c, gname, l, xb, xb_b)
                h = sb(ph, "h", [128, 24, TCH], BF16)
                h_b = [Buf("h%d" % k) for k in range(24)]
                sg = [sb(ph, "sg%d" % i, [128, 512], F32) for i in range(2)]
                sg_b = [Buf("sg%d" % i) for i in range(2)]
                NXT = 3
                xt = [sb(ph, "xt%d" % i, [128, 512], F32) for i in range(NXT)]
                xt_b = [Buf("xt%d" % i, S.get_dsem()) for i in range(NXT)]
                xn = [sb(ph, "xn%d" % i, [128, 512], F32) for i in range(NXT)]
                xn_b = [Buf("xn%d" % i, S.get_dsem()) for i in range(NXT)]
                ei = [0]
                xi = [0]
                wg_d, wu_d, wd_d = wts[f + "g"], wts[f + "u"], wts[f + "d"]
                for half in range(2):
                    cur_src, cur_bufs = (src, src_bufs) if half == 0 else (X, Xb[c])
                    for fcl in range(24):
                        fc = half * 24 + fcl
                        wg, wg_b = load_w(wg_d, (l * FCN + fc) * 128)
                        wu, wu_b = load_w(wu_d, (l * FCN + fc) * 128)
                        for th in range(2):
                            pg, pu = next_ps(2)
                            for (w, w_b, p) in ((wg, wg_b, pg), (wu, wu_b, pu)):
                                for kc in range(KC):
                                    S.op("pe", lambda e, w=w, p=p, kc=kc, th=th: e.matmul(
                                        psb[p][:], lhsT=w[:, kc, :], rhs=xb[:, kc, th * 512:(th + 1) * 512],
                                        start=(kc == 0), stop=(kc == KC - 1)),
                                        reads=[w_b, xb_b[kc]], writes=[ps_b[p]], signal=(kc == KC - 1))
                            q = ei[0] % 2
                            ei[0] += 1
                            S.op("act", lambda e, q=q, pg=pg: e.activation(out=sg[q][:], in_=psb[pg][:], func=AF.Silu),
                                 reads=[ps_b[pg]], writes=[sg_b[q]])
                            S.op("dve", lambda e, q=q, pu=pu, fcl=fcl, th=th: e.tensor_tensor(
                                out=h[:, fcl, th * 512:(th + 1) * 512], in0=psb[pu][:], in1=sg[q][:], op=ALU.mult),
                                reads=[ps_b[pu], sg_b[q]], writes=[h_b[fcl]])
                    for dc in range(KC):
                        wd, wd_b = load_w(wd_d, ((l * 2 + half) * KC + dc) * 128, ncols=24 * 128)
                        for th in range(2):
                            i = xi[0] % NXT
                            xi[0] += 1
                            cs = slice(c * TCH + th * 512, c * TCH + (th + 1) * 512)
                            S.dma("sp", xt[i][:], cur_src[dc * 128:(dc + 1) * 128, cs], xt_b[i],
                                  reads=[cur_bufs[dc]], writes=[xt_b[i]])
                            (py,) = next_ps(1)
                            for fcl in range(24):
                                S.op("pe", lambda e, py=py, fcl=fcl, th=th, wd=wd: e.matmul(
                                    psb[py][:], lhsT=wd[:, fcl, :], rhs=h[:, fcl, th * 512:(th + 1) * 512],
                                    start=(fcl == 0), stop=(fcl == 23)),
                                    reads=[wd_b, h_b[fcl]], writes=[ps_b[py]], signal=(fcl == 23))
                            S.op("dve", lambda e, i=i, py=py: e.scalar_tensor_tensor(
                                out=xn[i][:], in0=psb[py][:], scalar=0.5, in1=xt[i][:], op0=ALU.mult, op1=ALU.add),
                                reads=[ps_b[py], xt_b[i]], writes=[xn_b[i]])
                            S.dma("sp", X[dc * 128:(dc + 1) * 128, cs], xn[i][:], xn_b[i],
                                  reads=[xn_b[i]], writes=[Xb[c][dc]])
                S.barrier()
                for b in xt_b + xn_b:
                    S.put_dsem(b.dsem)


        SCALE = float(128 ** -0.5)
        maskAB = constf[:, 640:896]
        mask16 = constf[:, 896:1024]

        def proj_chunk(w_d, row0, xb, xb_b, nk=KC):
            w, w_b = load_w(w_d, row0)
            ps = next_ps(2)
            for th in range(2):
                for kc in range(nk):
                    S.op("pe", lambda e, w=w, p=ps[th], kc=kc, th=th: e.matmul(
                        psb[p][:], lhsT=w[:, kc, :], rhs=xb[:, kc, th * 512:(th + 1) * 512],
                        start=(kc == 0), stop=(kc == nk - 1)),
                        reads=[w_b, xb_b[kc]], writes=[ps_b[ps[th]]], signal=(kc == nk - 1))
            return ps

        def phase_mix(l, c):
            win_d, wout_d = wts["win"], wts["wout"]
            wrow = lambda oc: (l * NOC + oc) * 128
            with ExitStack() as ph:
                with ExitStack() as m1:
                    xb = sb(m1, "xb", [128, KC, TCH], BF16)
                    xb_b = [Buf("xb%d" % k) for k in range(KC)]
                    phase_pre(m1, X, Xb[c], c, "mix_norm", l, xb, xb_b)
                    with ExitStack() as qk:
                        rope = sb(qk, "rope", [128, 2, TCH], F32)
                        rope_b = Buf("rope", S.get_dsem())
                        if "q_norope" not in cfg.debug:
                            S.dma("sp", rope[:], rope_d[:, c * TCH:(c + 1) * TCH].rearrange("(a p) t -> p a t", a=2),
                                  rope_b, reads=[cbuf], writes=[rope_b])
                        swp = [sb(qk, "swp%d" % i, [128, TCH], F32) for i in range(4)]
                        swp_b = [Buf("swp%d" % i, S.get_dsem()) for i in range(4)]
                        qsw = [sb(qk, "qsw%d" % i, [128, TCH], F32) for i in range(2)]
                        qsw_b = [Buf("qsw%d" % i, S.get_dsem()) for i in range(2)]
                        t1 = [sb(qk, "t1_%d" % i, [128, TCH], F32) for i in range(2)]
                        t1_b = [Buf("t1_%d" % i) for i in range(2)]
                        t2 = [sb(qk, "t2_%d" % i, [128, TCH], F32) for i in range(2)]
                        for i in range(2):
                            S.op("dve", lambda e, i=i: e.memset(qsw[i][:], 0.0), writes=[qsw_b[i]])
                        t2_b = [Buf("t2_%d" % i) for i in range(2)]
                        qo = [sb(qk, "qo%d" % i, [128, TCH], BF16) for i in range(2)]
                        qo_b = [Buf("qo%d" % i, S.get_dsem()) for i in range(2)]
                        cnt = 0
                        for which, oc_sw, oc_h in ((("q", OC_QSW, OC_Q), ("k", OC_KSW, OC_K)) if "qk" not in cfg.debug else ()):
                            for s4 in range(4):
                                ps = proj_chunk(win_d, wrow(oc_sw + s4), xb, xb_b)
                                for th in range(2):
                                    S.op("act", lambda e, s4=s4, th=th, p=ps[th]: e.copy(
                                        out=swp[s4][:, th * 512:(th + 1) * 512], in_=psb[p][:]),
                                        reads=[ps_b[ps[th]]], writes=[swp_b[s4]])
                            for h in range(NH):
                                i = cnt % 2
                                cnt += 1
                                ps = proj_chunk(win_d, wrow(oc_h + h), xb, xb_b)
                                j4 = h % 4
                                if "q_nodma" not in cfg.debug:
                                    S.dma("sp", qsw[i][0:32, :], swp[h // 4][32 * j4:32 * j4 + 32, :], qsw_b[i],
                                          reads=[swp_b[h // 4]], writes=[qsw_b[i]])
                                for th in range(2):
                                    ts = slice(th * 512, (th + 1) * 512)
                                    if "q_noelem" not in cfg.debug:
                                        S.op("dve", lambda e, i=i, p=ps[th], ts=ts: e.tensor_tensor(
                                            out=t1[i][:, ts], in0=psb[p][:], in1=rope[:, 0, ts], op=ALU.mult),
                                            reads=[ps_b[ps[th]], rope_b], writes=[t1_b[i]])
                                if "q_noelem" not in cfg.debug:
                                    S.op("dve", lambda e, i=i: e.tensor_tensor(
                                        out=t2[i][:], in0=qsw[i][:], in1=rope[:, 1, :], op=ALU.mult),
                                        reads=[qsw_b[i], rope_b], writes=[t2_b[i]])
                                    S.op("dve", lambda e, i=i: e.tensor_tensor(
                                        out=qo[i][:], in0=t1[i][:], in1=t2[i][:], op=ALU.add),
                                        reads=[t1_b[i], t2_b[i]], writes=[qo_b[i]])
                                if which == "q":
                                    S.dma("sp", QT[h * 128:(h + 1) * 128, :], qo[i][:], qo_b[i],
                                          reads=[qo_b[i]], writes=[QTb[h]])
                                else:
                                    S.dma("sp", KT[h * 128:(h + 1) * 128, PADK + c * TCH:PADK + (c + 1) * TCH],
                                          qo[i][:], qo_b[i], reads=[qo_b[i]], writes=[KTb[c][h]])
                        S.barrier()
                        for b in [rope_b] + swp_b + qsw_b + qo_b:
                            S.put_dsem(b.dsem)
                    with ExitStack() as vg:
                        vT = [sb(vg, "vT%d" % i, [128, TCH], BF16) for i in range(2)]
                        vT_b = [Buf("vT%d" % i) for i in range(2)]
                        vtok = sb(vg, "vtok", [128, 8, 1024], BF16)
                        vtok_b = Buf("vtok", S.get_dsem())
                        for hv in (range(NH) if "v" not in cfg.debug else ()):
                            i = hv % 2
                            ps = proj_chunk(win_d, wrow(OC_V + hv), xb, xb_b)
                            for th in range(2):
                                S.op("act", lambda e, i=i, th=th, p=ps[th]: e.copy(
                                    out=vT[i][:, th * 512:(th + 1) * 512], in_=psb[p][:]),
                                    reads=[ps_b[ps[th]]], writes=[vT_b[i]])
                            (pt,) = next_ps(1)
                            ptv = psb[pt][:].bitcast(BF16)
                            for tt in range(8):
                                S.op("pe", lambda e, i=i, tt=tt, ptv=ptv: e.transpose(
                                    ptv[:, tt * 128:(tt + 1) * 128], vT[i][:, tt * 128:(tt + 1) * 128], ident),
                                    reads=[vT_b[i], constf_b], writes=[ps_b[pt]], signal=(tt == 7))
                            hh = hv % 8
                            S.op("dve", lambda e, hh=hh, ptv=ptv: e.tensor_copy(
                                out=vtok[:, :, hh * 128:(hh + 1) * 128],
                                in_=ptv.rearrange("p (t d) -> p t d", t=8)),
                                reads=[ps_b[pt]], writes=[vtok_b])
                            if hh == 7:
                                g8 = hv // 8
                                S.dma("sp", Vd[PADK + c * TCH:PADK + (c + 1) * TCH, g8 * 1024:(g8 + 1) * 1024]
                                      .rearrange("(t p) d -> p t d", p=128), vtok[:], vtok_b,
                                      reads=[vtok_b], writes=[Vb[c]])
                        sgm = [sb(vg, "sgm%d" % i, [128, 512], F32) for i in range(2)]
                        sgm_b = [Buf("sgm%d" % i) for i in range(2)]
                        hT = [sb(vg, "hT%d" % i, [128, TCH], BF16) for i in range(2)]
                        hT_b = [Buf("hT%d" % i, S.get_dsem()) for i in range(2)]
                        si = 0
                        for j in (range(16) if "ag" not in cfg.debug else ()):
                            i = j % 2
                            pa = proj_chunk(win_d, wrow(OC_AG + 2 * j), xb, xb_b)
                            pg = proj_chunk(win_d, wrow(OC_AG + 2 * j + 1), xb, xb_b)
                            for th in range(2):
                                q = si % 2
                                si += 1
                                S.op("act", lambda e, q=q, p=pg[th]: e.activation(
                                    out=sgm[q][:], in_=psb[p][:], func=AF.Sigmoid),
                                    reads=[ps_b[pg[th]]], writes=[sgm_b[q]])
                                S.op("dve", lambda e, i=i, q=q, th=th, p=pa[th]: e.tensor_tensor(
                                    out=hT[i][:, th * 512:(th + 1) * 512], in0=psb[p][:], in1=sgm[q][:], op=ALU.mult),
                                    reads=[ps_b[pa[th]], sgm_b[q]], writes=[hT_b[i]])
                            S.dma("sp", Hd[j * 128:(j + 1) * 128, PADH + c * TCH:PADH + (c + 1) * TCH], hT[i][:],
                                  hT_b[i], reads=[hT_b[i]], writes=[Hb[c][j]])
                        S.barrier()
                        for b in [vtok_b] + hT_b:
                            S.put_dsem(b.dsem)

                if cfg.stop == "m1":
                    return
                attnT = sb(ph, "attnT", [128, NH, TCH], BF16)
                attnT_b = [Buf("attnT%d" % h) for h in range(NH)]
                with ExitStack() as at:
                    kb = sb(at, "kb", [1, 3072], BF16)
                    kb_b = Buf("kb", S.get_dsem(True))
                    S.dma("pool", kb[:], kbias_d[0:1, c * TCH:c * TCH + 3072], kb_b, reads=[cbuf], writes=[kb_b])
                    qT = [sb(at, "qT%d" % i, [128, TCH], BF16) for i in range(2)]
                    qT_b = [Buf("qT%d" % i, S.get_dsem()) for i in range(2)]
                    kT = [sb(at, "kT%d" % i, [128, 3072], BF16) for i in range(2)]
                    kT_b = [Buf("kT%d" % i, S.get_dsem()) for i in range(2)]
                    vt = [sb(at, "vt%d" % i, [128, 53, 256], BF16) for i in range(2)]
                    vt_b = [Buf("vt%d" % i, S.get_dsem()) for i in range(2)]
                    NPT = 4
                    pts = [sb(at, "pts%d" % i, [128, 256], BF16) for i in range(NPT)]
                    pts_b = [Buf("pts%d" % i) for i in range(NPT)]
                    rd = [sb(at, "rd%d" % i, [128, 512], F32) for i in range(2)]
                    rd_b = [Buf("rd%d" % i) for i in range(2)]
                    ring[0] = [4, 5, 6, 7]
                    NUM = (0, 1)
                    DEN = (2, 3)
                    kdeps = [KTpad] if c < 2 else []
                    vdeps = [Vpad] if c < 2 else []
                    for cc in range(max(0, c - 2), c + 1):
                        vdeps.append(Vb[cc])
                    ptis = [0]
                    for h in range(NH):
                        i = h % 2
                        S.dma("sp", qT[i][:], QT[h * 128:(h + 1) * 128, :], qT_b[i], reads=[QTb[h]], writes=[qT_b[i]])
                        S.dma("sp", kT[i][:], KT[h * 128:(h + 1) * 128, c * TCH:c * TCH + 3072], kT_b[i],
                              reads=kdeps + [KTb[cc][h] for cc in range(max(0, c - 2), c + 1)], writes=[kT_b[i]])
                        vi = (h // 2) % 2
                        if h % 2 == 0:
                            cols = slice(h * 128, h * 128 + 256)
                            r1 = PADK + c * TCH - 128
                            S.dma("sp", vt[vi][:, 0:9, :], Vd[r1:r1 + 9 * 128, cols].rearrange("(b p) d -> p b d", p=128),
                                  vt_b[vi], reads=vdeps, writes=[vt_b[vi]])
                            r4 = PADK + c * TCH - 512
                            for r in range(4):
                                S.dma("sp", vt[vi][:, 9 + 3 * r:12 + 3 * r, :],
                                      Vd[r4 + r:r4 + 1536:4, cols].rearrange("(b p) d -> p b d", p=128),
                                      vt_b[vi], reads=vdeps, writes=[vt_b[vi]])
                            rA = PADK + c * TCH - 2048
                            S.dma("sp", vt[vi][:, 21:53:2, :],
                                  Vd[rA:rA + 2048, cols].rearrange("(p r) d -> p r d", r=16),
                                  vt_b[vi], reads=vdeps, writes=[vt_b[vi]])
                            rB = PADK + c * TCH
                            S.dma("sp", vt[vi][0:64, 22:53:2, :],
                                  Vd[rB:rB + 1024, cols].rearrange("(p r) d -> p r d", r=16),
                                  vt_b[vi], reads=vdeps, writes=[vt_b[vi]])
                        hh = h % 2
                        first = [True, True, True, True]
                        tiles = []
                        for qb in range(8):
                            tiles.append((slice(128 * qb, 128 * qb + 128), slice(2048 - 128 + 128 * qb, 2048 + 128 * qb),
                                          slice(2048 + 128 * qb, 2048 + 128 * qb + 128), 128, qb, qb + 1, 128,
                                          [(qb // 4, slice((qb % 4) * 128, (qb % 4) * 128 + 128), slice(0, 128))]))
                        for r in range(4):
                            for qb in range(2):
                                tiles.append((slice(512 * qb + r, 512 * qb + 512, 4),
                                              slice(2048 + 512 * qb - 512 + r, 2048 + 512 * qb, 4),
                                              slice(2048 + 512 * qb + r, 2048 + 512 * qb + 512, 4), 128,
                                              9 + 3 * r + qb, 9 + 3 * r + qb + 1, 128,
                                              [(qb, slice(r, 512, 4), slice(0, 128))]))
                        for r in range(16):
                            tiles.append((slice(r, 1024, 16), slice(r, 2048, 16), slice(2048 + r, 3072, 16), 64,
                                          21 + 2 * r, 22 + 2 * r, 64,
                                          [(0, slice(r, 512, 16), slice(0, 32)), (1, slice(r, 512, 16), slice(32, 64))]))
                        def stage1(tile):
                            pass
                            (qs, ka, kbs, KB, sA, sB, Nq, outs) = tile
                            (sp_,) = next_ps(1)
                            pbank = psb[sp_]
                            rds = [kT_b[i], qT_b[i]]
                            S.op("pe", lambda e, pbank=pbank, ka=ka, qs=qs, Nq=Nq, i=i: e.matmul(
                                pbank[:, 0:Nq], lhsT=kT[i][:, ka], rhs=qT[i][:, qs], start=True, stop=False,
                                skip_group_check=True), reads=rds, writes=[ps_b[sp_]], signal=False)
                            need_kb = (ka.start < 2048 - c * TCH)
                            S.op("pe", lambda e, pbank=pbank, kbs=kbs, qs=qs, Nq=Nq, KB=KB, i=i: e.matmul(
                                pbank[0:KB, Nq:2 * Nq], lhsT=kT[i][:, kbs], rhs=qT[i][:, qs], start=False, stop=not need_kb,
                                skip_group_check=True), reads=rds, writes=[ps_b[sp_]], signal=not need_kb)
                            if need_kb:
                                S.op("pe", lambda e, pbank=pbank, ka=ka, Nq=Nq: e.matmul(
                                    pbank[:, 0:Nq], lhsT=kb[0:1, ka], rhs=constf[0:1, 384:384 + Nq], start=False, stop=True,
                                    skip_group_check=True), reads=[kb_b, constf_b], writes=[ps_b[sp_]], signal=True)
                            msk = maskAB if Nq == 128 else mask16
                            pi = ptis[0] % NPT
                            ptis[0] += 1
                            S.op("act", lambda e, pbank=pbank, pi=pi, Nq=Nq: e.activation(
                                out=pts[pi][:, 0:2 * Nq], in_=pbank[:, 0:2 * Nq], func=AF.Exp, scale=SCALE),
                                reads=[ps_b[sp_]], writes=[pts_b[pi]])
                            S.op("dve", lambda e, pi=pi, Nq=Nq, msk=msk: e.tensor_tensor(
                                out=pts[pi][:, 0:2 * Nq], in0=pts[pi][:, 0:2 * Nq], in1=msk, op=ALU.mult),
                                reads=[pts_b[pi], constf_b], writes=[pts_b[pi]])
                            return pi
                        def stage2(tile, pi):
                            (qs, ka, kbs, KB, sA, sB, Nq, outs) = tile
                            nout_ = len(outs)
                            for oi, (bk, ocs, qsub) in enumerate(outs):
                                for kind in range(2):
                                    bank = (NUM if kind == 0 else DEN)[bk]
                                    fidx = kind * 2 + bk
                                    st = first[fidx]
                                    first[fidx] = False
                                    if kind == 0:
                                        lA = vt[vi][:, sA, hh * 128:(hh + 1) * 128]
                                        lB = vt[vi][0:KB, sB, hh * 128:(hh + 1) * 128]
                                        rdl = [vt_b[vi], pts_b[pi]]
                                    else:
                                        lA = ones_bf
                                        lB = constf[0:KB, 384:512]
                                        rdl = [constf_b, pts_b[pi]]
                                    qa = slice(qsub.start, qsub.stop)
                                    qbb = slice(Nq + qsub.start, Nq + qsub.stop)
                                    S.op("pe", lambda e, bank=bank, ocs=ocs, lA=lA, pi=pi, qa=qa, st=st: e.matmul(
                                        psb[bank][:, ocs], lhsT=lA, rhs=pts[pi][:, qa], start=st, stop=False,
                                        skip_group_check=True), reads=rdl, writes=[ps_b[bank]], signal=False)
                                    last = (oi == nout_ - 1 and kind == 1)
                                    S.op("pe", lambda e, bank=bank, ocs=ocs, lB=lB, pi=pi, qbb=qbb, KB=KB: e.matmul(
                                        psb[bank][:, ocs], lhsT=lB, rhs=pts[pi][0:KB, qbb], start=False, stop=True,
                                        skip_group_check=True), reads=rdl, writes=[ps_b[bank]], signal=last)
                        LAG = 2
                        pis = {}
                        for idx in range(len(tiles) + LAG):
                            if idx < len(tiles):
                                pis[idx] = stage1(tiles[idx])
                            if idx >= LAG:
                                stage2(tiles[idx - LAG], pis[idx - LAG])
                        for bk in range(2):
                            S.op("dve", lambda e, bk=bk: e.tensor_scalar(
                                out=rd[bk][:], in0=psb[DEN[bk]][:], scalar1=1e-30, scalar2=None, op0=ALU.max),
                                reads=[ps_b[DEN[bk]]], writes=[rd_b[bk]])
                            S.op("dve", lambda e, bk=bk: e.reciprocal(out=rd[bk][:], in_=rd[bk][:]),
                                 reads=[rd_b[bk]], writes=[rd_b[bk]])
                            S.op("dve", lambda e, bk=bk, h=h: e.tensor_tensor(
                                out=attnT[:, h, bk * 512:(bk + 1) * 512], in0=psb[NUM[bk]][:], in1=rd[bk][:], op=ALU.mult),
                                reads=[ps_b[NUM[bk]], rd_b[bk]], writes=[attnT_b[h]])
                    ring[0] = list(range(8))
                    S.barrier()
                    for b in [kb_b] + qT_b + kT_b + vt_b:
                        S.put_dsem(b.dsem)

                if cfg.stop == "a":
                    return

                def chan_stats(src3, src_b, n, sq, sq_b, pss, with_sum=None):
                    for j in range(n):
                        q = j % 2
                        S.op("act", lambda e, q=q, j=j: e.activation(out=sq[q][:], in_=src3[:, j, :], func=AF.Square),
                             reads=[src_b[j]], writes=[sq_b[q]])
                        for th in range(2):
                            S.op("pe", lambda e, q=q, th=th, j=j: e.matmul(
                                psb[pss[th]][:], lhsT=ones_bf, rhs=sq[q][:, th * 512:(th + 1) * 512],
                                start=(j == 0), stop=(j == n - 1)),
                                reads=[sq_b[q], constf_b], writes=[ps_b[pss[th]]], signal=(th == 1))

                def rstd_from(pss, dst, dst_b, n, eps_col):
                    for th in range(2):
                        S.op("act", lambda e, th=th: e.activation(
                            out=dst[:, th * 512:(th + 1) * 512], in_=psb[pss[th]][:], func=AF.Sqrt,
                            bias=epsb[:, eps_col:eps_col + 1], scale=1.0 / n),
                            reads=[ps_b[pss[th]], eps_b], writes=[dst_b])
                    S.op("dve", lambda e: e.reciprocal(out=dst[:], in_=dst[:]), reads=[dst_b], writes=[dst_b])

                with ExitStack() as an:
                    sq = [sb(an, "asq%d" % i, [128, TCH], BF16) for i in range(2)]
                    sq_b = [Buf("asq%d" % i) for i in range(2)]
                    rsa = sb(an, "rsa", [128, TCH], F32)
                    rsa_b = Buf("rsa")
                    pss = next_ps(2)
                    chan_stats(attnT, attnT_b, NH, sq, sq_b, pss)
                    rstd_from(pss, rsa, rsa_b, AW, 0)
                    for h in range(NH):
                        S.op("dve", lambda e, h=h: e.scalar_tensor_tensor(
                            out=attnT[:, h, :], in0=attnT[:, h, :], scalar=sm("attn_out_norm", l, 16, h), in1=rsa[:],
                            op0=ALU.mult, op1=ALU.mult),
                            reads=[attnT_b[h], rsa_b, smalls_b], writes=[attnT_b[h]])
                    S.barrier()

                convT = sb(ph, "convT", [128, 16, TCH], BF16)
                convT_b = [Buf("convT%d" % j) for j in range(16)]
                with ExitStack() as cv:
                    hin = [sb(cv, "hin%d" % i, [128, TCH + PADH], BF16) for i in range(2)]
                    hin_b = [Buf("hin%d" % i, S.get_dsem()) for i in range(2)]
                    dg = [sb(cv, "dg%d" % i, [128, 31, 128], BF16) for i in range(2)]
                    dg_b = [Buf("dg%d" % i) for i in range(2)]
                    sq = [sb(cv, "csq%d" % i, [128, 512], BF16) for i in range(2)]
                    sq_b = [Buf("csq%d" % i) for i in range(2)]
                    mean = sb(cv, "mean", [128, TCH], F32)
                    mean_b = Buf("mean")
                    rln = sb(cv, "rln", [128, TCH], F32)
                    rln_b = Buf("rln")
                    tmp = [sb(cv, "ctmp%d" % i, [128, TCH], F32) for i in range(2)]
                    tmp_b = [Buf("ctmp%d" % i) for i in range(2)]
                    psum_s = next_ps(2)
                    psum_q = next_ps(2)
                    ring[0] = [b_ for b_ in range(8) if b_ not in psum_s + psum_q]
                    wo = soff["conv_w"] + l * 16 * 31
                    qi = 0
                    for j in range(16):
                        i = j % 2
                        hdeps = [Hb[c][j]] + ([Hb[c - 1][j]] if c > 0 else [Hpad])
                        S.dma("sp", hin[i][:], Hd[j * 128:(j + 1) * 128, c * TCH:c * TCH + TCH + PADH], hin_b[i],
                              reads=hdeps, writes=[hin_b[i]])
                        for k in range(31):
                            o = wo + j * 31 + k
                            S.op("dve", lambda e, i=i, k=k, o=o: e.tensor_scalar(
                                out=dg[i][:, k, :], in0=ident, scalar1=smalls[:, o:o + 1], scalar2=None, op0=ALU.mult),
                                reads=[constf_b, smalls_b], writes=[dg_b[i]], signal=(k == 30))
                        for th in range(2):
                            ts = slice(th * 512, (th + 1) * 512)
                            (pc,) = next_ps(1)
                            for k in range(31):
                                S.op("pe", lambda e, i=i, k=k, pc=pc, th=th: e.matmul(
                                    psb[pc][:], lhsT=dg[i][:, k, :], rhs=hin[i][:, 2 + k + th * 512:2 + k + th * 512 + 512],
                                    start=(k == 0), stop=(k == 30)),
                                    reads=[dg_b[i], hin_b[i]], writes=[ps_b[pc]], signal=(k == 30))
                            q = qi % 2
                            qi += 1
                            S.op("act", lambda e, j=j, ts=ts, pc=pc: e.activation(
                                out=convT[:, j, ts], in_=psb[pc][:], func=AF.Identity, bias=sm("conv_b", l, 16, j), scale=1.0),
                                reads=[ps_b[pc], smalls_b], writes=[convT_b[j]])
                            S.op("act", lambda e, j=j, q=q, pc=pc: e.activation(
                                out=sq[q][:], in_=psb[pc][:], func=AF.Square, bias=sm("conv_b", l, 16, j), scale=1.0),
                                reads=[ps_b[pc], smalls_b], writes=[sq_b[q]])
                            S.op("pe", lambda e, j=j, th=th, ts=ts: e.matmul(
                                psb[psum_s[th]][:], lhsT=ones_bf, rhs=convT[:, j, ts], start=(j == 0), stop=(j == 15)),
                                reads=[convT_b[j], constf_b], writes=[ps_b[psum_s[th]]], signal=False)
                            S.op("pe", lambda e, q=q, j=j, th=th: e.matmul(
                                psb[psum_q[th]][:], lhsT=ones_bf, rhs=sq[q][:], start=(j == 0), stop=(j == 15)),
                                reads=[sq_b[q], constf_b], writes=[ps_b[psum_q[th]]], signal=True)
                    ring[0] = list(range(8))
                    sq = [sb(cv, "csq2_%d" % i, [128, TCH], BF16) for i in range(2)]
                    sq_b = [Buf("csq2_%d" % i) for i in range(2)]
                    for th in range(2):
                        ts = slice(th * 512, (th + 1) * 512)
                        S.op("dve", lambda e, th=th, ts=ts: e.tensor_scalar(
                            out=mean[:, ts], in0=psb[psum_s[th]][:], scalar1=1.0 / CW, scalar2=None, op0=ALU.mult),
                            reads=[ps_b[psum_s[th]]], writes=[mean_b])
                        S.op("dve", lambda e, ts=ts: e.tensor_tensor(
                            out=tmp[0][:, ts], in0=mean[:, ts], in1=mean[:, ts], op=ALU.mult),
                            reads=[mean_b], writes=[tmp_b[0]])
                        S.op("dve", lambda e, th=th, ts=ts: e.scalar_tensor_tensor(
                            out=rln[:, ts], in0=psb[psum_q[th]][:], scalar=1.0 / CW, in1=tmp[0][:, ts],
                            op0=ALU.mult, op1=ALU.subtract),
                            reads=[ps_b[psum_q[th]], tmp_b[0]], writes=[rln_b])
                    S.op("act", lambda e: e.activation(out=rln[:], in_=rln[:], func=AF.Sqrt, bias=epsb[:, 1:2], scale=1.0),
                         reads=[rln_b, eps_b], writes=[rln_b])
                    S.op("dve", lambda e: e.reciprocal(out=rln[:], in_=rln[:]), reads=[rln_b], writes=[rln_b])
                    pss = next_ps(2)
                    for j in range(16):
                        i = j % 2
                        S.op("dve", lambda e, i=i, j=j: e.tensor_tensor(
                            out=tmp[i][:], in0=convT[:, j, :], in1=mean[:], op=ALU.subtract),
                            reads=[convT_b[j], mean_b], writes=[tmp_b[i]])
                        S.op("dve", lambda e, i=i: e.tensor_tensor(
                            out=tmp[i][:], in0=tmp[i][:], in1=rln[:], op=ALU.mult),
                            reads=[tmp_b[i], rln_b], writes=[tmp_b[i]])
                        S.op("act", lambda e, i=i, j=j: e.activation(
                            out=convT[:, j, :], in_=tmp[i][:], func=AF.Silu,
                            bias=sm("conv_ln_b", l, 16, j), scale=sm("conv_ln_g", l, 16, j)),
                            reads=[tmp_b[i], smalls_b], writes=[convT_b[j]])
                        S.op("act", lambda e, i=i, j=j: e.activation(out=sq[i][:], in_=convT[:, j, :], func=AF.Square),
                             reads=[convT_b[j]], writes=[sq_b[i]])
                        for th in range(2):
                            S.op("pe", lambda e, i=i, th=th, j=j: e.matmul(
                                psb[pss[th]][:], lhsT=ones_bf, rhs=sq[i][:, th * 512:(th + 1) * 512],
                                start=(j == 0), stop=(j == 15)),
                                reads=[sq_b[i], constf_b], writes=[ps_b[pss[th]]], signal=(th == 1))
                    rsc = mean
                    rstd_from(pss, rsc, mean_b, CW, 0)
                    for j in range(16):
                        S.op("dve", lambda e, j=j: e.scalar_tensor_tensor(
                            out=convT[:, j, :], in0=convT[:, j, :], scalar=sm("conv_out_norm", l, 16, j), in1=rsc[:],
                            op0=ALU.mult, op1=ALU.mult),
                            reads=[convT_b[j], mean_b, smalls_b], writes=[convT_b[j]])
                    S.barrier()
                    for b in hin_b:
                        S.put_dsem(b.dsem)

                if cfg.stop == "c":
                    return
                with ExitStack() as m2:
                    NXT = 3
                    xt = [sb(m2, "mxt%d" % i, [128, 512], F32) for i in range(NXT)]
                    xt_b = [Buf("mxt%d" % i, S.get_dsem()) for i in range(NXT)]
                    xn = [sb(m2, "mxn%d" % i, [128, 512], F32) for i in range(NXT)]
                    xn_b = [Buf("mxn%d" % i, S.get_dsem()) for i in range(NXT)]
                    xi = 0
                    for dc in range(KC):
                        w, w_b = load_w(wout_d, (l * KC + dc) * 128)
                        for th in range(2):
                            i = xi % NXT
                            xi += 1
                            cs = slice(c * TCH + th * 512, c * TCH + (th + 1) * 512)
                            ts = slice(th * 512, (th + 1) * 512)
                            S.dma("sp", xt[i][:], X[dc * 128:(dc + 1) * 128, cs], xt_b[i],
                                  reads=[Xb[c][dc]], writes=[xt_b[i]])
                            (py,) = next_ps(1)
                            for kc in range(KC):
                                if kc < 16:
                                    rhs, rb = attnT[:, kc, ts], attnT_b[kc]
                                else:
                                    rhs, rb = convT[:, kc - 16, ts], convT_b[kc - 16]
                                S.op("pe", lambda e, py=py, w=w, kc=kc, rhs=rhs: e.matmul(
                                    psb[py][:], lhsT=w[:, kc, :], rhs=rhs, start=(kc == 0), stop=(kc == KC - 1)),
                                    reads=[w_b, rb], writes=[ps_b[py]], signal=(kc == KC - 1))
                            S.op("dve", lambda e, i=i, py=py: e.tensor_tensor(
                                out=xn[i][:], in0=psb[py][:], in1=xt[i][:], op=ALU.add),
                                reads=[ps_b[py], xt_b[i]], writes=[xn_b[i]])
                            S.dma("sp", X[dc * 128:(dc + 1) * 128, cs], xn[i][:], xn_b[i],
                                  reads=[xn_b[i]], writes=[Xb[c][dc]])
                    S.barrier()
                    for b in xt_b + xn_b:
                        S.put_dsem(b.dsem)

        def phase_final(c, oc):
            with ExitStack() as ph:
                rstd, rstd_b = phase_pre(ph, X, Xb[c], c, "final_norm", 0, None, None, want_xb=False)
                NXS = 3
                xs = [sb(ph, "fx%d" % i, [128, TCH], F32) for i in range(NXS)]
                xs_b = [Buf("fx%d" % i, S.get_dsem()) for i in range(NXS)]
                xo = [sb(ph, "fo%d" % i, [128, TCH], F32) for i in range(NXS)]
                xo_b = [Buf("fo%d" % i, S.get_dsem()) for i in range(NXS)]
                for kc in range(KC):
                    i = kc % NXS
                    S.dma("sp", xs[i][:], X[kc * 128:(kc + 1) * 128, c * TCH:(c + 1) * TCH], xs_b[i],
                          reads=[Xb[c][kc]], writes=[xs_b[i]])
                    o = soff["final_norm"] + kc
                    S.op("dve", lambda e, i=i, o=o: e.scalar_tensor_tensor(
                        out=xo[i][:], in0=xs[i][:], scalar=smalls[:, o:o + 1], in1=rstd[:],
                        op0=ALU.mult, op1=ALU.mult),
                        reads=[xs_b[i], rstd_b, smalls_b], writes=[xo_b[i]])
                    S.dma("sp", outT[kc * 128:(kc + 1) * 128, oc * TCH:(oc + 1) * TCH], xo[i][:], xo_b[i],
                          reads=[xo_b[i]], writes=[Outb])
                S.barrier()
                for b in xs_b + xo_b:
                    S.put_dsem(b.dsem)

        def phase_copy_in(c):
            with ExitStack() as ph:
                NXS = 3
                xs = [sb(ph, "ci%d" % i, [128, TCH], F32) for i in range(NXS)]
                xs_b = [Buf("ci%d" % i, S.get_dsem()) for i in range(NXS)]
                for kc in range(KC):
                    i = kc % NXS
                    S.dma("sp", xs[i][:], xT_in[kc * 128:(kc + 1) * 128, c * TCH:(c + 1) * TCH], xs_b[i],
                          reads=[Xin_b], writes=[xs_b[i]])
                    S.dma("sp", X[kc * 128:(kc + 1) * 128, c * TCH:(c + 1) * TCH], xs[i][:], xs_b[i],
                          reads=[xs_b[i]], writes=[Xb[c][kc]])
                S.barrier()
                for b in xs_b:
                    S.put_dsem(b.dsem)

        from_in = [True] * NCH
        xin_bufs = [Xin_b] * KC
        for l in range(L):
            for c in range(NCH):
                if "f1" in cfg.phases:
                    if from_in[c]:
                        phase_ffn(l, c, "f1", "ffn1_norm", xT_in, xin_bufs)
                        from_in[c] = False
                    else:
                        phase_ffn(l, c, "f1", "ffn1_norm", X, Xb[c])
                if "mix" in cfg.phases:
                    if from_in[c]:
                        phase_copy_in(c)
                        from_in[c] = False
                    phase_mix(l, c)
                if "f2" in cfg.phases:
                    phase_ffn(l, c, "f2", "ffn2_norm", X, Xb[c])
        if cfg.final:
            for oc, c in enumerate(cfg.out_chunks):
                phase_final(c, oc)
        S.final_wait("sp")
        print("instructions emitted:", S.ninst, file=sys.stderr)
    return nc


def make_inputs(inp, cfg, win_start=0):
    L, T = cfg.L, cfg.T
    x = np.asarray(inp["x"], np.float32)[0]
    Sq = x.shape[0]
    xw = np.zeros((T, D), np.float32)
    lo = max(win_start, 0)
    hi = min(win_start + T, Sq)
    xw[lo - win_start:hi - win_start] = x[lo:hi]
    m = {"xT": np.ascontiguousarray(xw.T)}
    m["smalls"] = pack_smalls(inp, L)
    m["consts"] = const_tiles()
    pos = np.arange(win_start, win_start + T).astype(np.float32)
    m["rope"] = rope_tables(pos)
    gpos = np.arange(win_start - PADK, win_start + T)
    m["kbias"] = np.where(gpos >= 0, 0.0, NEG).astype(np.float32)[None, :]
    for f, pre in (("f1", "ffn1"), ("f2", "ffn2")):
        m[f + "g"] = np.concatenate([tile_w(np.asarray(inp[pre + "_w_gate"][l], np.float32), KC, FCN) for l in range(L)], 0)
        m[f + "u"] = np.concatenate([tile_w(np.asarray(inp[pre + "_w_up"][l], np.float32), KC, FCN) for l in range(L)], 0)
        m[f + "d"] = np.concatenate([tile_wd(np.asarray(inp[pre + "_w_down"][l], np.float32)) for l in range(L)], 0)
    cols = win_cols()
    m["win"] = np.concatenate([tile_w(np.asarray(inp["w_in"][l], np.float32)[:, cols], KC, NOC) for l in range(L)], 0)
    m["wout"] = np.concatenate([tile_w(np.asarray(inp["w_out"][l], np.float32), KC, KC) for l in range(L)], 0)
    return m


_PROG = {}


def kernel(**inputs):
    cfg = Cfg(L=4, NCH=8)
    key = "full"
    if key not in _PROG:
        _PROG[key] = build_program(cfg)
    nc = _PROG[key]
    m = make_inputs(inputs, cfg, 0)
    res = run_bass_kernel_spmd(nc, [m], core_ids=[0])
    oT = res.results[0]["outT"]
    return np.ascontiguousarray(oT.T)[None].astype(np.float32)
```

```python
import sys
from contextlib import ExitStack
import numpy as np
import concourse.bass as bass
import concourse.mybir as mybir
from concourse.bass_utils import run_bass_kernel_spmd

F32 = mybir.dt.float32
BF16 = mybir.dt.bfloat16
AF = mybir.ActivationFunctionType
ALU = mybir.AluOpType

D = 4096
DFF = 6144
KC = D // 128
FCN = DFF // 128
NH = 16
AW = 2048
CW = 2048
TCH = 1024
NOC = 88
PADK = 2048
PADH = 32
NEG = -30000.0
RMS_EPS = 1e-5
LN_EPS = 1e-5
ROPE_THETA = 500000.0
BRANCH_D = (1, 4, 16)


class DSem:
    def __init__(self, sem):
        self.sem = sem
        self.cnt = 0


class Buf:
    __slots__ = ("name", "w", "r", "dsem")

    def __init__(self, name, dsem=None):
        self.name = name
        self.w = None
        self.r = {}
        self.dsem = dsem


class Sched:
    def __init__(self, nc, stack, n_dsems=40):
        self.nc = nc
        self.eng = {"pe": nc.tensor, "act": nc.scalar, "dve": nc.vector, "pool": nc.gpsimd, "sp": nc.sync}
        self.esem = {}
        self.ecnt = {}
        for k in ("pe", "act", "dve", "pool"):
            self.esem[k] = stack.enter_context(nc.semaphore("e_" + k))
            self.ecnt[k] = 0
        self.waited = {k: {} for k in self.eng}
        self.free_dsems = [DSem(stack.enter_context(nc.semaphore("d%d" % i))) for i in range(n_dsems)]
        self.free_swsems = [DSem(stack.enter_context(nc.semaphore("w%d" % i))) for i in range(10)]
        self.all_dsems = list(self.free_dsems) + list(self.free_swsems)
        self.ninst = 0
        self.deferred = []

    def get_dsem(self, sw=False):
        d = (self.free_swsems if sw else self.free_dsems).pop()
        d.sw = sw
        return d

    def put_dsem(self, d):
        (self.free_swsems if d.sw else self.free_dsems).append(d)

    def _wait(self, E, deps):
        w = self.waited[E]
        for key, val in deps:
            if isinstance(key, DSem):
                val = key.cnt
                sem = key.sem
            else:
                if key == E and val > self.ecnt[E]:
                    continue
                if key == E and E == "pe":
                    continue
                sem = self.esem[key]
            if w.get(key, 0) >= val:
                continue
            self.eng[E].wait_ge(sem, val)
            self.ninst += 1
            w[key] = val

    def _deps(self, reads, writes):
        deps = []
        for b in reads:
            if b.w is not None:
                deps.append(b.w)
        for b in writes:
            if b.w is not None:
                deps.append(b.w)
            deps.extend(b.r.items())
        return deps

    def _stamp(self, stamp, reads, writes):
        k, v = stamp
        for b in reads:
            if b.r.get(k, 0) < v:
                b.r[k] = v
        for b in writes:
            b.w = stamp
            b.r = {}

    def op(self, E, fn, reads=(), writes=(), signal=True):
        self._wait(E, self._deps(reads, writes))
        ins = fn(self.eng[E])
        self.ninst += 1
        if signal:
            self.ecnt[E] += 1
            ins.then_inc(self.esem[E], 1)
            stamp = (E, self.ecnt[E])
        else:
            stamp = (E, self.ecnt[E] + 1)
        self._stamp(stamp, reads, writes)
        return ins

    def dma(self, Q, out, in_, sembuf, reads=(), writes=()):
        self._wait(Q, self._deps(reads, writes))
        ins = self.eng[Q].dma_start(out=out, in_=in_)
        self.ninst += 1
        d = sembuf.dsem
        d.cnt += 16
        ins.then_inc(d.sem, 16)
        self._stamp((d, d.cnt), reads, writes)
        return ins

    def barrier(self):
        for E in self.eng:
            deps = [(k, self.ecnt[k]) for k in self.esem if k != E and self.ecnt[k] > 0]
            deps += [(d, d.cnt) for d in self.all_dsems if d.cnt > 0]
            self._wait(E, deps)
        for d in self.deferred:
            self.put_dsem(d)
        self.deferred = []

    def final_wait(self, E="sp"):
        deps = [(d, d.cnt) for d in self.all_dsems if d.cnt > 0]
        self._wait(E, deps)


def smalls_layout(L):
    off = {}
    o = 0
    for nm in ("ffn1_norm", "mix_norm", "ffn2_norm"):
        off[nm] = o
        o += L * KC
    off["final_norm"] = o
    o += KC
    for nm in ("attn_out_norm", "conv_out_norm", "conv_b", "conv_ln_g", "conv_ln_b"):
        off[nm] = o
        o += L * 16
    off["conv_w"] = o
    o += L * 16 * 31
    return off, o


def pack_smalls(inp, L):
    off, n = smalls_layout(L)
    s = np.zeros((128, n), np.float32)
    for nm in ("ffn1_norm", "mix_norm", "ffn2_norm"):
        a = np.asarray(inp[nm], np.float32)[:L].reshape(L, KC, 128).transpose(2, 0, 1).reshape(128, L * KC)
        s[:, off[nm]:off[nm] + L * KC] = a
    s[:, off["final_norm"]:off["final_norm"] + KC] = np.asarray(inp["final_norm"], np.float32).reshape(KC, 128).T
    for nm in ("attn_out_norm", "conv_out_norm", "conv_b", "conv_ln_g", "conv_ln_b"):
        a = np.asarray(inp[nm], np.float32)[:L].reshape(L, 16, 128).transpose(2, 0, 1).reshape(128, L * 16)
        s[:, off[nm]:off[nm] + L * 16] = a
    cw = np.asarray(inp["conv_w"], np.float32)[:L]
    a = cw.reshape(L, 31, 16, 128).transpose(3, 0, 2, 1).reshape(128, L * 16 * 31)
    s[:, off["conv_w"]:off["conv_w"] + L * 16 * 31] = a
    return s


def win_cols():
    cols = []
    sw = np.concatenate([np.arange(16, 32), np.arange(0, 16)])

    def swap_chunks(base):
        for s in range(4):
            for j in range(4):
                h = 4 * s + j
                cols.append(base + h * 128 + sw)

    swap_chunks(0)
    cols.append(np.arange(0, AW))
    swap_chunks(AW)
    cols.append(np.arange(AW, 2 * AW))
    cols.append(np.arange(2 * AW, 3 * AW))
    for j in range(16):
        cols.append(3 * AW + j * 128 + np.arange(128))
        cols.append(3 * AW + CW + j * 128 + np.arange(128))
    c = np.concatenate(cols)
    assert c.size == NOC * 128
    return c


OC_QSW, OC_Q, OC_KSW, OC_K, OC_V, OC_AG = 0, 4, 20, 24, 40, 56


def tile_w(w, nk, nout):
    return np.ascontiguousarray(
        w.reshape(nk, 128, nout, 128).transpose(2, 1, 0, 3)).reshape(nout * 128, nk * 128)


def tile_wd(w):
    return np.ascontiguousarray(
        w.reshape(2, 24, 128, KC, 128).transpose(0, 3, 2, 1, 4)).reshape(2 * KC * 128, 24 * 128)


def const_tiles():
    j = np.arange(128)[:, None]
    i = np.arange(128)[None, :]
    ident = (j == i).astype(np.float32)
    mA = np.where(j >= i, 0.0, NEG).astype(np.float32)
    mB = np.where(j <= i, 0.0, NEG).astype(np.float32)
    ones = np.ones((128, 128), np.float32)
    m16 = np.concatenate([mA[:, :64], mB[:, :64]], axis=1)
    z1 = lambda m: (m == 0).astype(np.float32)
    return np.concatenate([ident, mA, mB, ones, m16, z1(mA), z1(mB), z1(m16)], axis=1)


def rope_tables(pos):
    inv = (np.float32(ROPE_THETA) ** (-(np.arange(0, 32, 2, dtype=np.float32) / np.float32(32)))).astype(np.float32)
    ang = (pos.astype(np.float32)[:, None] * inv[None, :]).astype(np.float32)
    c = np.cos(ang).astype(np.float32).T
    s = np.sin(ang).astype(np.float32).T
    one = np.ones((96, pos.shape[0]), np.float32)
    zero = np.zeros((96, pos.shape[0]), np.float32)
    return np.ascontiguousarray(np.concatenate([c, c, one, -s, s, zero], axis=0))


class Cfg:
    def __init__(self, L=4, NCH=8, out_chunks=None, phases=("f1", "mix", "f2"), final=True, debug=(), stop=None):
        self.stop = stop
        self.L = L
        self.NCH = NCH
        self.T = NCH * TCH
        self.out_chunks = list(range(NCH)) if out_chunks is None else list(out_chunks)
        self.phases = phases
        self.final = final
        self.debug = debug


def build_program(cfg):
    L, NCH, T = cfg.L, cfg.NCH, cfg.T
    nc = bass.Bass("TRN2", target_bir_lowering=False)
    soff, NS = smalls_layout(L)

    def din(name, shape, dt=F32):
        return nc.dram_tensor(name, list(shape), dt, kind="ExternalInput").ap()

    xT_in = din("xT", [D, T])
    smalls_d = din("smalls", [128, NS])
    consts_d = din("consts", [128, 1024])
    rope_d = din("rope", [256, T])
    kbias_d = din("kbias", [1, PADK + T])
    wts = {}
    for f in ("f1", "f2"):
        wts[f + "g"] = din(f + "g", [L * FCN * 128, KC * 128])
        wts[f + "u"] = din(f + "u", [L * FCN * 128, KC * 128])
        wts[f + "d"] = din(f + "d", [L * 2 * KC * 128, 24 * 128])
    wts["win"] = din("win", [L * NOC * 128, KC * 128])
    wts["wout"] = din("wout", [L * KC * 128, KC * 128])
    nout = len(cfg.out_chunks)
    outT = nc.dram_tensor("outT", [D, nout * TCH], F32, kind="ExternalOutput").ap()

    X = nc.dram_tensor("Xs", [D, T], F32, kind="Internal").ap()
    KT = nc.dram_tensor("KTs", [AW, PADK + T], BF16, kind="Internal").ap()
    QT = nc.dram_tensor("QTs", [AW, TCH], BF16, kind="Internal").ap()
    Vd = nc.dram_tensor("Vs", [PADK + T, AW], BF16, kind="Internal").ap()
    Hd = nc.dram_tensor("Hs", [CW, PADH + T], BF16, kind="Internal").ap()

    es = ExitStack()
    with es:
        S = Sched(nc, es)

        uid = [0]

        def sb(stack, name, shape, dt):
            uid[0] += 1
            return stack.enter_context(nc.sbuf_tensor("s%d_%s" % (uid[0], name), list(shape), dt))

        Xb = [[Buf("X%d_%d" % (c, k)) for k in range(KC)] for c in range(NCH)]
        Xin_b = Buf("xin")
        KTb = [[Buf("KT%d_%d" % (c, h)) for h in range(NH)] for c in range(NCH)]
        KTpad = Buf("KTpad")
        QTb = [Buf("QT%d" % h) for h in range(NH)]
        Vb = [Buf("V%d" % c) for c in range(NCH)]
        Vpad = Buf("Vpad")
        Hb = [[Buf("H%d_%d" % (c, j)) for j in range(16)] for c in range(NCH)]
        Hpad = Buf("Hpad")
        Outb = Buf("out")
        cbuf = Buf("const_in")

        smalls = sb(es, "smalls", [128, NS], F32)
        smalls_b = Buf("smalls", S.get_dsem())
        constf = sb(es, "constf", [128, 1024], BF16)
        constf_b = Buf("constf", S.get_dsem(True))
        NW = 4
        wslot = [sb(es, "wslot%d" % i, [128, KC, 128], BF16) for i in range(NW)]
        wslot_b = [Buf("wslot%d" % i, S.get_dsem(True)) for i in range(NW)]
        wrr = [0]
        psb = [es.enter_context(nc.psum_tensor("ps%d" % i, [128, 512], F32)) for i in range(8)]
        ps_b = [Buf("ps%d" % i) for i in range(8)]
        prr = [0]

        S.dma("sp", smalls[:], smalls_d[:, :], smalls_b, reads=[cbuf], writes=[smalls_b])
        S.dma("pool", constf[:], consts_d[:, :], constf_b, reads=[cbuf], writes=[constf_b])
        ident = constf[:, 0:128]
        ones_bf = constf[:, 384:512]

        def sm(name, l, n, j):
            o = soff[name] + l * n + j
            return smalls[:, o:o + 1]

        ring = [list(range(8))]

        def next_ps(n=1):
            r = []
            for _ in range(n):
                r.append(ring[0][prr[0] % len(ring[0])])
                prr[0] += 1
            return r

        def load_w(src2d, row0, ncols=KC * 128):
            i = wrr[0] % NW
            wrr[0] += 1
            dst = wslot[i][:].rearrange("p k m -> p (k m)")[:, 0:ncols]
            S.dma("pool", dst, src2d[row0:row0 + 128, 0:ncols], wslot_b[i], reads=[cbuf], writes=[wslot_b[i]])
            return wslot[i], wslot_b[i]

        with ExitStack() as ph:
            z = sb(ph, "zpad", [128, 2048], BF16)
            zb = Buf("zpad", S.get_dsem())
            zf = sb(ph, "zpadf", [128, 16 * PADH], BF16)
            zfb = Buf("zpadf", S.get_dsem())
            S.op("dve", lambda e: e.memset(z[:], 0.0), writes=[zb])
            S.op("dve", lambda e: e.memset(zf[:], 0.0), writes=[zfb])
            for j in range(16):
                S.dma("sp", KT[j * 128:(j + 1) * 128, 0:PADK], z[:], zb, reads=[zb], writes=[KTpad])
                S.dma("sp", Vd[j * 128:(j + 1) * 128, :], z[:], zb, reads=[zb], writes=[Vpad])
            S.dma("sp", Hd[:, 0:PADH].rearrange("(j p) t -> p j t", p=128),
                  zf[:].rearrange("p (j t) -> p j t", j=16), zfb, reads=[zfb], writes=[Hpad])
            S.barrier()
            S.put_dsem(zb.dsem)
            S.put_dsem(zfb.dsem)

        def phase_pre(ph, src, src_bufs, c, gname, l, xb, xb_b, want_xb=True, rstd_keep=None):
            rstd = sb(ph, "rstd", [128, TCH], F32)
            rstd_b = Buf("rstd")
            if True:
                sub = ph
                NXS = 4
                xs = [sb(sub, "xs%d" % i, [128, TCH], F32) for i in range(NXS)]
                xs_b = [Buf("xs%d" % i, S.get_dsem()) for i in range(NXS)]
                sq = [sb(sub, "sq%d" % i, [128, TCH], BF16) for i in range(2)]
                sq_b = [Buf("sq%d" % i) for i in range(2)]
                pss = next_ps(2)
                for kc in range(KC):
                    i = kc % NXS
                    S.dma("sp", xs[i][:], src[kc * 128:(kc + 1) * 128, c * TCH:(c + 1) * TCH], xs_b[i],
                          reads=[src_bufs[kc]], writes=[xs_b[i]])
                    q = kc % 2
                    S.op("act", lambda e, i=i, q=q: e.activation(out=sq[q][:], in_=xs[i][:], func=AF.Square),
                         reads=[xs_b[i]], writes=[sq_b[q]])
                    if want_xb:
                        S.op("dve", lambda e, i=i, kc=kc: e.tensor_scalar(
                            out=xb[:, kc, :], in0=xs[i][:], scalar1=sm(gname, l, KC, kc), scalar2=None, op0=ALU.mult),
                            reads=[xs_b[i], smalls_b], writes=[xb_b[kc]])
                    for th in range(2):
                        S.op("pe", lambda e, q=q, th=th, kc=kc: e.matmul(
                            psb[pss[th]][:], lhsT=ones_bf, rhs=sq[q][:, th * 512:(th + 1) * 512],
                            start=(kc == 0), stop=(kc == KC - 1)),
                            reads=[sq_b[q], constf_b], writes=[ps_b[pss[th]]], signal=(th == 1))
                for th in range(2):
                    S.op("act", lambda e, th=th: e.activation(
                        out=rstd[:, th * 512:(th + 1) * 512], in_=psb[pss[th]][:], func=AF.Sqrt,
                        bias=epsb[:, 0:1], scale=1.0 / D),
                        reads=[ps_b[pss[th]], eps_b], writes=[rstd_b])
                S.op("dve", lambda e: e.reciprocal(out=rstd[:], in_=rstd[:]), reads=[rstd_b], writes=[rstd_b])
                if want_xb:
                    for kc in range(KC):
                        S.op("dve", lambda e, kc=kc: e.tensor_tensor(
                            out=xb[:, kc, :], in0=xb[:, kc, :], in1=rstd[:], op=ALU.mult),
                            reads=[rstd_b], writes=[xb_b[kc]])
                S.deferred.extend(b.dsem for b in xs_b)
            return rstd, rstd_b

        epsb = sb(es, "epsb", [128, 2], F32)
        eps_b = Buf("eps")
        S.op("dve", lambda e: e.memset(epsb[:, 0:1], RMS_EPS), writes=[eps_b])
        S.op("dve", lambda e: e.memset(epsb[:, 1:2], LN_EPS), writes=[eps_b])

        def phase_ffn(l, c, f, gname, src, src_bufs):
            with ExitStack() as ph:
                xb = sb(ph, "xb", [128, KC, TCH], BF16)
                xb_b = [Buf("xb%d" % k) for k in range(KC)]
                phase_pre(ph, src, src_bufs, c, gname, l, xb, xb_b)
                h = sb(ph, "h", [128, 24, TCH], BF16)
                h_b = [Buf("h%d" % k) for k in range(24)]
                sg = [sb(ph, "sg%d" % i, [128, 512], F32) for i in range(2)]
                sg_b = [Buf("sg%d" % i) for i in range(2)]
                NXT = 3
                xt = [sb(ph, "xt%d" % i, [128, 512], F32) for i in range(NXT)]
                xt_b = [Buf("xt%d" % i, S.get_dsem()) for i in range(NXT)]
                xn = [sb(ph, "xn%d" % i, [128, 512], F32) for i in range(NXT)]
                xn_b = [Buf("xn%d" % i, S.get_dsem()) for i in range(NXT)]
                ei = [0]
                xi = [0]
                wg_d, wu_d, wd_d = wts[f + "g"], wts[f + "u"], wts[f + "d"]
                for half in range(2):
                    cur_src, cur_bufs = (src, src_bufs) if half == 0 else (X, Xb[c])
                    for fcl in range(24):
                        fc = half * 24 + fcl
                        wg, wg_b = load_w(wg_d, (l * FCN + fc) * 128)
                        wu, wu_b = load_w(wu_d, (l * FCN + fc) * 128)
                        for th in range(2):
                            pg, pu = next_ps(2)
                            for (w, w_b, p) in ((wg, wg_b, pg), (wu, wu_b, pu)):
                                for kc in range(KC):
                                    S.op("pe", lambda e, w=w, p=p, kc=kc, th=th: e.matmul(
                                        psb[p][:], lhsT=w[:, kc, :], rhs=xb[:, kc, th * 512:(th + 1) * 512],
                                        start=(kc == 0), stop=(kc == KC - 1)),
                                        reads=[w_b, xb_b[kc]], writes=[ps_b[p]], signal=(kc == KC - 1))
                            q = ei[0] % 2
                            ei[0] += 1
                            S.op("act", lambda e, q=q, pg=pg: e.activation(out=sg[q][:], in_=psb[pg][:], func=AF.Silu),
                                 reads=[ps_b[pg]], writes=[sg_b[q]])
                            S.op("dve", lambda e, q=q, pu=pu, fcl=fcl, th=th: e.tensor_tensor(
                                out=h[:, fcl, th * 512:(th + 1) * 512], in0=psb[pu][:], in1=sg[q][:], op=ALU.mult),
                                reads=[ps_b[pu], sg_b[q]], writes=[h_b[fcl]])
                    for dc in range(KC):
                        wd, wd_b = load_w(wd_d, ((l * 2 + half) * KC + dc) * 128, ncols=24 * 128)
                        for th in range(2):
                            i = xi[0] % NXT
                            xi[0] += 1
                            cs = slice(c * TCH + th * 512, c * TCH + (th + 1) * 512)
                            S.dma("sp", xt[i][:], cur_src[dc * 128:(dc + 1) * 128, cs], xt_b[i],
                                  reads=[cur_bufs[dc]], writes=[xt_b[i]])
                            (py,) = next_ps(1)
                            for fcl in range(24):
                                S.op("pe", lambda e, py=py, fcl=fcl, th=th, wd=wd: e.matmul(
                                    psb[py][:], lhsT=wd[:, fcl, :], rhs=h[:, fcl, th * 512:(th + 1) * 512],
                                    start=(fcl == 0), stop=(fcl == 23)),
                                    reads=[wd_b, h_b[fcl]], writes=[ps_b[py]], signal=(fcl == 23))
                            S.op("dve", lambda e, i=i, py=py: e.scalar_tensor_tensor(
                                out=xn[i][:], in0=psb[py][:], scalar=0.5, in1=xt[i][:], op0=ALU.mult, op1=ALU.add),
                                reads=[ps_b[py], xt_b[i]], writes=[xn_b[i]])
                            S.dma("sp", X[dc * 128:(dc + 1) * 128, cs], xn[i][:], xn_b[i],
                                  reads=[xn_b[i]], writes=[Xb[c][dc]])
                S.barrier()
                for b in xt_b + xn_b:
                    S.put_dsem(b.dsem)


        SCALE = float(128 ** -0.5)
        maskAB = constf[:, 640:896]
        mask16 = constf[:, 896:1024]

        def proj_chunk(w_d, row0, xb, xb_b, nk=KC):
            w, w_b = load_w(w_d, row0)
            ps = next_ps(2)
            for th in range(2):
                for kc in range(nk):
                    S.op("pe", lambda e, w=w, p=ps[th], kc=kc, th=th: e.matmul(
                        psb[p][:], lhsT=w[:, kc, :], rhs=xb[:, kc, th * 512:(th + 1) * 512],
                        start=(kc == 0), stop=(kc == nk - 1)),
                        reads=[w_b, xb_b[kc]], writes=[ps_b[ps[th]]], signal=(kc == nk - 1))
            return ps

        def phase_mix(l, c):
            win_d, wout_d = wts["win"], wts["wout"]
            wrow = lambda oc: (l * NOC + oc) * 128
            with ExitStack() as ph:
                with ExitStack() as m1:
                    xb = sb(m1, "xb", [128, KC, TCH], BF16)
                    xb_b = [Buf("xb%d" % k) for k in range(KC)]
                    phase_pre(m1, X, Xb[c], c, "mix_norm", l, xb, xb_b)
                    with ExitStack() as qk:
                        rope = sb(qk, "rope", [128, 2, TCH], F32)
                        rope_b = Buf("rope", S.get_dsem())
                        if "q_norope" not in cfg.debug:
                            S.dma("sp", rope[:], rope_d[:, c * TCH:(c + 1) * TCH].rearrange("(a p) t -> p a t", a=2),
                                  rope_b, reads=[cbuf], writes=[rope_b])
                        swp = [sb(qk, "swp%d" % i, [128, TCH], F32) for i in range(4)]
                        swp_b = [Buf("swp%d" % i, S.get_dsem()) for i in range(4)]
                        qsw = [sb(qk, "qsw%d" % i, [128, TCH], F32) for i in range(2)]
                        qsw_b = [Buf("qsw%d" % i, S.get_dsem()) for i in range(2)]
                        t1 = [sb(qk, "t1_%d" % i, [128, TCH], F32) for i in range(2)]
                        t1_b = [Buf("t1_%d" % i) for i in range(2)]
                        t2 = [sb(qk, "t2_%d" % i, [128, TCH], F32) for i in range(2)]
                        for i in range(2):
                            S.op("dve", lambda e, i=i: e.memset(qsw[i][:], 0.0), writes=[qsw_b[i]])
                        t2_b = [Buf("t2_%d" % i) for i in range(2)]
                        qo = [sb(qk, "qo%d" % i, [128, TCH], BF16) for i in range(2)]
                        qo_b = [Buf("qo%d" % i, S.get_dsem()) for i in range(2)]
                        cnt = 0
                        for which, oc_sw, oc_h in ((("q", OC_QSW, OC_Q), ("k", OC_KSW, OC_K)) if "qk" not in cfg.debug else ()):
                            for s4 in range(4):
                                ps = proj_chunk(win_d, wrow(oc_sw + s4), xb, xb_b)
                                for th in range(2):
                                    S.op("act", lambda e, s4=s4, th=th, p=ps[th]: e.copy(
                                        out=swp[s4][:, th * 512:(th + 1) * 512], in_=psb[p][:]),
                                        reads=[ps_b[ps[th]]], writes=[swp_b[s4]])
                            for h in range(NH):
                                i = cnt % 2
                                cnt += 1
                                ps = proj_chunk(win_d, wrow(oc_h + h), xb, xb_b)
                                j4 = h % 4
                                if "q_nodma" not in cfg.debug:
                                    S.dma("sp", qsw[i][0:32, :], swp[h // 4][32 * j4:32 * j4 + 32, :], qsw_b[i],
                                          reads=[swp_b[h // 4]], writes=[qsw_b[i]])
                                for th in range(2):
                                    ts = slice(th * 512, (th + 1) * 512)
                                    if "q_noelem" not in cfg.debug:
                                        S.op("dve", lambda e, i=i, p=ps[th], ts=ts: e.tensor_tensor(
                                            out=t1[i][:, ts], in0=psb[p][:], in1=rope[:, 0, ts], op=ALU.mult),
                                            reads=[ps_b[ps[th]], rope_b], writes=[t1_b[i]])
                                if "q_noelem" not in cfg.debug:
                                    S.op("dve", lambda e, i=i: e.tensor_tensor(
                                        out=t2[i][:], in0=qsw[i][:], in1=rope[:, 1, :], op=ALU.mult),
                                        reads=[qsw_b[i], rope_b], writes=[t2_b[i]])
                                    S.op("dve", lambda e, i=i: e.tensor_tensor(
                                        out=qo[i][:], in0=t1[i][:], in1=t2[i][:], op=ALU.add),
                                        reads=[t1_b[i], t2_b[i]], writes=[qo_b[i]])
                                if which == "q":
                                    S.dma("sp", QT[h * 128:(h + 1) * 128, :], qo[i][:], qo_b[i],
                                          reads=[qo_b[i]], writes=[QTb[h]])
                                else:
                                    S.dma("sp", KT[h * 128:(h + 1) * 128, PADK + c * TCH:PADK + (c + 1) * TCH],
                                          qo[i][:], qo_b[i], reads=[qo_b[i]], writes=[KTb[c][h]])
                        S.barrier()
                        for b in [rope_b] + swp_b + qsw_b + qo_b:
                            S.put_dsem(b.dsem)
                    with ExitStack() as vg:
                        vT = [sb(vg, "vT%d" % i, [128, TCH], BF16) for i in range(2)]
                        vT_b = [Buf("vT%d" % i) for i in range(2)]
                        vtok = sb(vg, "vtok", [128, 8, 1024], BF16)
                        vtok_b = Buf("vtok", S.get_dsem())
                        for hv in (range(NH) if "v" not in cfg.debug else ()):
                            i = hv % 2
                            ps = proj_chunk(win_d, wrow(OC_V + hv), xb, xb_b)
                            for th in range(2):
                                S.op("act", lambda e, i=i, th=th, p=ps[th]: e.copy(
                                    out=vT[i][:, th * 512:(th + 1) * 512], in_=psb[p][:]),
                                    reads=[ps_b[ps[th]]], writes=[vT_b[i]])
                            (pt,) = next_ps(1)
                            ptv = psb[pt][:].bitcast(BF16)
                            for tt in range(8):
                                S.op("pe", lambda e, i=i, tt=tt, ptv=ptv: e.transpose(
                                    ptv[:, tt * 128:(tt + 1) * 128], vT[i][:, tt * 128:(tt + 1) * 128], ident),
                                    reads=[vT_b[i], constf_b], writes=[ps_b[pt]], signal=(tt == 7))
                            hh = hv % 8
                            S.op("dve", lambda e, hh=hh, ptv=ptv: e.tensor_copy(
                                out=vtok[:, :, hh * 128:(hh + 1) * 128],
                                in_=ptv.rearrange("p (t d) -> p t d", t=8)),
                                reads=[ps_b[pt]], writes=[vtok_b])
                            if hh == 7:
                                g8 = hv // 8
                                S.dma("sp", Vd[PADK + c * TCH:PADK + (c + 1) * TCH, g8 * 1024:(g8 + 1) * 1024]
                                      .rearrange("(t p) d -> p t d", p=128), vtok[:], vtok_b,
                                      reads=[vtok_b], writes=[Vb[c]])
                        sgm = [sb(vg, "sgm%d" % i, [128, 512], F32) for i in range(2)]
                        sgm_b = [Buf("sgm%d" % i) for i in range(2)]
                        hT = [sb(vg, "hT%d" % i, [128, TCH], BF16) for i in range(2)]
                        hT_b = [Buf("hT%d" % i, S.get_dsem()) for i in range(2)]
                        si = 0
                        for j in (range(16) if "ag" not in cfg.debug else ()):
                            i = j % 2
                            pa = proj_chunk(win_d, wrow(OC_AG + 2 * j), xb, xb_b)
                            pg = proj_chunk(win_d, wrow(OC_AG + 2 * j + 1), xb, xb_b)
                            for th in range(2):
                                q = si % 2
                                si += 1
                                S.op("act", lambda e, q=q, p=pg[th]: e.activation(
                                    out=sgm[q][:], in_=psb[p][:], func=AF.Sigmoid),
                                    reads=[ps_b[pg[th]]], writes=[sgm_b[q]])
                                S.op("dve", lambda e, i=i, q=q, th=th, p=pa[th]: e.tensor_tensor(
                                    out=hT[i][:, th * 512:(th + 1) * 512], in0=psb[p][:], in1=sgm[q][:], op=ALU.mult),
                                    reads=[ps_b[pa[th]], sgm_b[q]], writes=[hT_b[i]])
                            S.dma("sp", Hd[j * 128:(j + 1) * 128, PADH + c * TCH:PADH + (c + 1) * TCH], hT[i][:],
                                  hT_b[i], reads=[hT_b[i]], writes=[Hb[c][j]])
                        S.barrier()
                        for b in [vtok_b] + hT_b:
                            S.put_dsem(b.dsem)

                if cfg.stop == "m1":
                    return
                attnT = sb(ph, "attnT", [128, NH, TCH], BF16)
                attnT_b = [Buf("attnT%d" % h) for h in range(NH)]
                with ExitStack() as at:
                    kb = sb(at, "kb", [1, 3072], BF16)
                    kb_b = Buf("kb", S.get_dsem(True))
                    S.dma("pool", kb[:], kbias_d[0:1, c * TCH:c * TCH + 3072], kb_b, reads=[cbuf], writes=[kb_b])
                    qT = [sb(at, "qT%d" % i, [128, TCH], BF16) for i in range(2)]
                    qT_b = [Buf("qT%d" % i, S.get_dsem()) for i in range(2)]
                    kT = [sb(at, "kT%d" % i, [128, 3072], BF16) for i in range(2)]
                    kT_b = [Buf("kT%d" % i, S.get_dsem()) for i in range(2)]
                    vt = [sb(at, "vt%d" % i, [128, 53, 256], BF16) for i in range(2)]
                    vt_b = [Buf("vt%d" % i, S.get_dsem()) for i in range(2)]
                    NPT = 5
                    pts = [sb(at, "pts%d" % i, [128, 256], BF16) for i in range(NPT)]
                    pts_b = [Buf("pts%d" % i) for i in range(NPT)]
                    rd = [sb(at, "rd%d" % i, [128, 512], F32) for i in range(2)]
                    rd_b = [Buf("rd%d" % i) for i in range(2)]
                    ring[0] = [4, 5, 6, 7]
                    NUM = (0, 1)
                    DEN = (2, 3)
                    kdeps = [KTpad] if c < 2 else []
                    vdeps = [Vpad] if c < 2 else []
                    for cc in range(max(0, c - 2), c + 1):
                        vdeps.append(Vb[cc])
                    ptis = [0]
                    for h in range(NH):
                        i = h % 2
                        S.dma("sp", qT[i][:], QT[h * 128:(h + 1) * 128, :], qT_b[i], reads=[QTb[h]], writes=[qT_b[i]])
                        S.dma("sp", kT[i][:], KT[h * 128:(h + 1) * 128, c * TCH:c * TCH + 3072], kT_b[i],
                              reads=kdeps + [KTb[cc][h] for cc in range(max(0, c - 2), c + 1)], writes=[kT_b[i]])
                        vi = (h // 2) % 2
                        if h % 2 == 0:
                            cols = slice(h * 128, h * 128 + 256)
                            r1 = PADK + c * TCH - 128
                            S.dma("sp", vt[vi][:, 0:9, :], Vd[r1:r1 + 9 * 128, cols].rearrange("(b p) d -> p b d", p=128),
                                  vt_b[vi], reads=vdeps, writes=[vt_b[vi]])
                            r4 = PADK + c * TCH - 512
                            for r in range(4):
                                S.dma("sp", vt[vi][:, 9 + 3 * r:12 + 3 * r, :],
                                      Vd[r4 + r:r4 + 1536:4, cols].rearrange("(b p) d -> p b d", p=128),
                                      vt_b[vi], reads=vdeps, writes=[vt_b[vi]])
                            rA = PADK + c * TCH - 2048
                            S.dma("sp", vt[vi][:, 21:53:2, :],
                                  Vd[rA:rA + 2048, cols].rearrange("(p r) d -> p r d", r=16),
                                  vt_b[vi], reads=vdeps, writes=[vt_b[vi]])
                            rB = PADK + c * TCH
                            S.dma("sp", vt[vi][0:64, 22:53:2, :],
                                  Vd[rB:rB + 1024, cols].rearrange("(p r) d -> p r d", r=16),
                                  vt_b[vi], reads=vdeps, writes=[vt_b[vi]])
                        hh = h % 2
                        first = [True, True, True, True]
                        tiles = []
                        for qb in range(8):
                            tiles.append((slice(128 * qb, 128 * qb + 128), slice(2048 - 128 + 128 * qb, 2048 + 128 * qb),
                                          slice(2048 + 128 * qb, 2048 + 128 * qb + 128), 128, qb, qb + 1, 128,
                                          [(qb // 4, slice((qb % 4) * 128, (qb % 4) * 128 + 128), slice(0, 128))]))
                        for r in range(4):
                            for qb in range(2):
                                tiles.append((slice(512 * qb + r, 512 * qb + 512, 4),
                                              slice(2048 + 512 * qb - 512 + r, 2048 + 512 * qb, 4),
                                              slice(2048 + 512 * qb + r, 2048 + 512 * qb + 512, 4), 128,
                                              9 + 3 * r + qb, 9 + 3 * r + qb + 1, 128,
                                              [(qb, slice(r, 512, 4), slice(0, 128))]))
                        for r in range(16):
                            tiles.append((slice(r, 1024, 16), slice(r, 2048, 16), slice(2048 + r, 3072, 16), 64,
                                          21 + 2 * r, 22 + 2 * r, 64,
                                          [(0, slice(r, 512, 16), slice(0, 32)), (1, slice(r, 512, 16), slice(32, 64))]))
                        def stage1(tile):
                            pass
                            (qs, ka, kbs, KB, sA, sB, Nq, outs) = tile
                            (sp_,) = next_ps(1)
                            pbank = psb[sp_]
                            rds = [kT_b[i], qT_b[i]]
                            S.op("pe", lambda e, pbank=pbank, ka=ka, qs=qs, Nq=Nq, i=i: e.matmul(
                                pbank[:, 0:Nq], lhsT=kT[i][:, ka], rhs=qT[i][:, qs], start=True, stop=False,
                                skip_group_check=True), reads=rds, writes=[ps_b[sp_]], signal=False)
                            need_kb = (ka.start < 2048 - c * TCH)
                            S.op("pe", lambda e, pbank=pbank, kbs=kbs, qs=qs, Nq=Nq, KB=KB, i=i: e.matmul(
                                pbank[0:KB, Nq:2 * Nq], lhsT=kT[i][:, kbs], rhs=qT[i][:, qs], start=False, stop=not need_kb,
                                skip_group_check=True), reads=rds, writes=[ps_b[sp_]], signal=not need_kb)
                            if need_kb:
                                S.op("pe", lambda e, pbank=pbank, ka=ka, Nq=Nq: e.matmul(
                                    pbank[:, 0:Nq], lhsT=kb[0:1, ka], rhs=constf[0:1, 384:384 + Nq], start=False, stop=True,
                                    skip_group_check=True), reads=[kb_b, constf_b], writes=[ps_b[sp_]], signal=True)
                            msk = maskAB if Nq == 128 else mask16
                            pi = ptis[0] % NPT
                            ptis[0] += 1
                            S.op("act", lambda e, pbank=pbank, pi=pi, Nq=Nq: e.activation(
                                out=pts[pi][:, 0:2 * Nq], in_=pbank[:, 0:2 * Nq], func=AF.Exp, scale=SCALE),
                                reads=[ps_b[sp_]], writes=[pts_b[pi]])
                            S.op("dve", lambda e, pi=pi, Nq=Nq, msk=msk: e.tensor_tensor(
                                out=pts[pi][:, 0:2 * Nq], in0=pts[pi][:, 0:2 * Nq], in1=msk, op=ALU.mult),
                                reads=[pts_b[pi], constf_b], writes=[pts_b[pi]])
                            return pi
                        def stage2(tile, pi):
                            (qs, ka, kbs, KB, sA, sB, Nq, outs) = tile
                            nout_ = len(outs)
                            for oi, (bk, ocs, qsub) in enumerate(outs):
                                for kind in range(2):
                                    bank = (NUM if kind == 0 else DEN)[bk]
                                    fidx = kind * 2 + bk
                                    st = first[fidx]
                                    first[fidx] = False
                                    if kind == 0:
                                        lA = vt[vi][:, sA, hh * 128:(hh + 1) * 128]
                                        lB = vt[vi][0:KB, sB, hh * 128:(hh + 1) * 128]
                                        rdl = [vt_b[vi], pts_b[pi]]
                                    else:
                                        lA = ones_bf
                                        lB = constf[0:KB, 384:512]
                                        rdl = [constf_b, pts_b[pi]]
                                    qa = slice(qsub.start, qsub.stop)
                                    qbb = slice(Nq + qsub.start, Nq + qsub.stop)
                                    S.op("pe", lambda e, bank=bank, ocs=ocs, lA=lA, pi=pi, qa=qa, st=st: e.matmul(
                                        psb[bank][:, ocs], lhsT=lA, rhs=pts[pi][:, qa], start=st, stop=False,
                                        skip_group_check=True), reads=rdl, writes=[ps_b[bank]], signal=False)
                                    last = (oi == nout_ - 1 and kind == 1)
                                    S.op("pe", lambda e, bank=bank, ocs=ocs, lB=lB, pi=pi, qbb=qbb, KB=KB: e.matmul(
                                        psb[bank][:, ocs], lhsT=lB, rhs=pts[pi][0:KB, qbb], start=False, stop=True,
                                        skip_group_check=True), reads=rdl, writes=[ps_b[bank]], signal=last)
                        LAG = 3
                        pis = {}
                        for idx in range(len(tiles) + LAG):
                            if idx < len(tiles):
                                pis[idx] = stage1(tiles[idx])
                            if idx >= LAG:
                                stage2(tiles[idx - LAG], pis[idx - LAG])
                        for bk in range(2):
                            S.op("dve", lambda e, bk=bk: e.tensor_scalar(
                                out=rd[bk][:], in0=psb[DEN[bk]][:], scalar1=1e-30, scalar2=None, op0=ALU.max),
                                reads=[ps_b[DEN[bk]]], writes=[rd_b[bk]])
                            S.op("dve", lambda e, bk=bk: e.reciprocal(out=rd[bk][:], in_=rd[bk][:]),
                                 reads=[rd_b[bk]], writes=[rd_b[bk]])
                            S.op("dve", lambda e, bk=bk, h=h: e.tensor_tensor(
                                out=attnT[:, h, bk * 512:(bk + 1) * 512], in0=psb[NUM[bk]][:], in1=rd[bk][:], op=ALU.mult),
                                reads=[ps_b[NUM[bk]], rd_b[bk]], writes=[attnT_b[h]])
                    ring[0] = list(range(8))
                    S.barrier()
                    for b in [kb_b] + qT_b + kT_b + vt_b:
                        S.put_dsem(b.dsem)

                if cfg.stop == "a":
                    return

                def chan_stats(src3, src_b, n, sq, sq_b, pss, with_sum=None):
                    for j in range(n):
                        q = j % 2
                        S.op("act", lambda e, q=q, j=j: e.activation(out=sq[q][:], in_=src3[:, j, :], func=AF.Square),
                             reads=[src_b[j]], writes=[sq_b[q]])
                        for th in range(2):
                            S.op("pe", lambda e, q=q, th=th, j=j: e.matmul(
                                psb[pss[th]][:], lhsT=ones_bf, rhs=sq[q][:, th * 512:(th + 1) * 512],
                                start=(j == 0), stop=(j == n - 1)),
                                reads=[sq_b[q], constf_b], writes=[ps_b[pss[th]]], signal=(th == 1))

                def rstd_from(pss, dst, dst_b, n, eps_col):
                    for th in range(2):
                        S.op("act", lambda e, th=th: e.activation(
                            out=dst[:, th * 512:(th + 1) * 512], in_=psb[pss[th]][:], func=AF.Sqrt,
                            bias=epsb[:, eps_col:eps_col + 1], scale=1.0 / n),
                            reads=[ps_b[pss[th]], eps_b], writes=[dst_b])
                    S.op("dve", lambda e: e.reciprocal(out=dst[:], in_=dst[:]), reads=[dst_b], writes=[dst_b])

                with ExitStack() as an:
                    sq = [sb(an, "asq%d" % i, [128, TCH], BF16) for i in range(2)]
                    sq_b = [Buf("asq%d" % i) for i in range(2)]
                    rsa = sb(an, "rsa", [128, TCH], F32)
                    rsa_b = Buf("rsa")
                    pss = next_ps(2)
                    chan_stats(attnT, attnT_b, NH, sq, sq_b, pss)
                    rstd_from(pss, rsa, rsa_b, AW, 0)
                    for h in range(NH):
                        S.op("dve", lambda e, h=h: e.scalar_tensor_tensor(
                            out=attnT[:, h, :], in0=attnT[:, h, :], scalar=sm("attn_out_norm", l, 16, h), in1=rsa[:],
                            op0=ALU.mult, op1=ALU.mult),
                            reads=[attnT_b[h], rsa_b, smalls_b], writes=[attnT_b[h]])
                    S.barrier()

                convT = sb(ph, "convT", [128, 16, TCH], BF16)
                convT_b = [Buf("convT%d" % j) for j in range(16)]
                with ExitStack() as cv:
                    hin = [sb(cv, "hin%d" % i, [128, TCH + PADH], BF16) for i in range(2)]
                    hin_b = [Buf("hin%d" % i, S.get_dsem()) for i in range(2)]
                    dg = [sb(cv, "dg%d" % i, [128, 31, 128], BF16) for i in range(2)]
                    dg_b = [Buf("dg%d" % i) for i in range(2)]
                    sq = [sb(cv, "csq%d" % i, [128, 512], BF16) for i in range(2)]
                    sq_b = [Buf("csq%d" % i) for i in range(2)]
                    mean = sb(cv, "mean", [128, TCH], F32)
                    mean_b = Buf("mean")
                    rln = sb(cv, "rln", [128, TCH], F32)
                    rln_b = Buf("rln")
                    tmp = [sb(cv, "ctmp%d" % i, [128, TCH], F32) for i in range(2)]
                    tmp_b = [Buf("ctmp%d" % i) for i in range(2)]
                    psum_s = next_ps(2)
                    psum_q = next_ps(2)
                    ring[0] = [b_ for b_ in range(8) if b_ not in psum_s + psum_q]
                    wo = soff["conv_w"] + l * 16 * 31
                    qi = 0
                    for j in range(16):
                        i = j % 2
                        hdeps = [Hb[c][j]] + ([Hb[c - 1][j]] if c > 0 else [Hpad])
                        S.dma("sp", hin[i][:], Hd[j * 128:(j + 1) * 128, c * TCH:c * TCH + TCH + PADH], hin_b[i],
                              reads=hdeps, writes=[hin_b[i]])
                        for k in range(31):
                            o = wo + j * 31 + k
                            S.op("dve", lambda e, i=i, k=k, o=o: e.tensor_scalar(
                                out=dg[i][:, k, :], in0=ident, scalar1=smalls[:, o:o + 1], scalar2=None, op0=ALU.mult),
                                reads=[constf_b, smalls_b], writes=[dg_b[i]], signal=(k == 30))
                        for th in range(2):
                            ts = slice(th * 512, (th + 1) * 512)
                            (pc,) = next_ps(1)
                            for k in range(31):
                                S.op("pe", lambda e, i=i, k=k, pc=pc, th=th: e.matmul(
                                    psb[pc][:], lhsT=dg[i][:, k, :], rhs=hin[i][:, 2 + k + th * 512:2 + k + th * 512 + 512],
                                    start=(k == 0), stop=(k == 30)),
                                    reads=[dg_b[i], hin_b[i]], writes=[ps_b[pc]], signal=(k == 30))
                            q = qi % 2
                            qi += 1
                            S.op("act", lambda e, j=j, ts=ts, pc=pc: e.activation(
                                out=convT[:, j, ts], in_=psb[pc][:], func=AF.Identity, bias=sm("conv_b", l, 16, j), scale=1.0),
                                reads=[ps_b[pc], smalls_b], writes=[convT_b[j]])
                            S.op("act", lambda e, j=j, q=q, pc=pc: e.activation(
                                out=sq[q][:], in_=psb[pc][:], func=AF.Square, bias=sm("conv_b", l, 16, j), scale=1.0),
                                reads=[ps_b[pc], smalls_b], writes=[sq_b[q]])
                            S.op("pe", lambda e, j=j, th=th, ts=ts: e.matmul(
                                psb[psum_s[th]][:], lhsT=ones_bf, rhs=convT[:, j, ts], start=(j == 0), stop=(j == 15)),
                                reads=[convT_b[j], constf_b], writes=[ps_b[psum_s[th]]], signal=False)
                            S.op("pe", lambda e, q=q, j=j, th=th: e.matmul(
                                psb[psum_q[th]][:], lhsT=ones_bf, rhs=sq[q][:], start=(j == 0), stop=(j == 15)),
                                reads=[sq_b[q], constf_b], writes=[ps_b[psum_q[th]]], signal=True)
                    ring[0] = list(range(8))
                    sq = [sb(cv, "csq2_%d" % i, [128, TCH], BF16) for i in range(2)]
                    sq_b = [Buf("csq2_%d" % i) for i in range(2)]
                    for th in range(2):
                        ts = slice(th * 512, (th + 1) * 512)
                        S.op("dve", lambda e, th=th, ts=ts: e.tensor_scalar(
                            out=mean[:, ts], in0=psb[psum_s[th]][:], scalar1=1.0 / CW, scalar2=None, op0=ALU.mult),
                            reads=[ps_b[psum_s[th]]], writes=[mean_b])
                        S.op("dve", lambda e, ts=ts: e.tensor_tensor(
                            out=tmp[0][:, ts], in0=mean[:, ts], in1=mean[:, ts], op=ALU.mult),
                            reads=[mean_b], writes=[tmp_b[0]])
                        S.op("dve", lambda e, th=th, ts=ts: e.scalar_tensor_tensor(
                            out=rln[:, ts], in0=psb[psum_q[th]][:], scalar=1.0 / CW, in1=tmp[0][:, ts],
                            op0=ALU.mult, op1=ALU.subtract),
                            reads=[ps_b[psum_q[th]], tmp_b[0]], writes=[rln_b])
                    S.op("act", lambda e: e.activation(out=rln[:], in_=rln[:], func=AF.Sqrt, bias=epsb[:, 1:2], scale=1.0),
                         reads=[rln_b, eps_b], writes=[rln_b])
                    S.op("dve", lambda e: e.reciprocal(out=rln[:], in_=rln[:]), reads=[rln_b], writes=[rln_b])
                    pss = next_ps(2)
                    for j in range(16):
                        i = j % 2
                        S.op("dve", lambda e, i=i, j=j: e.tensor_tensor(
                            out=tmp[i][:], in0=convT[:, j, :], in1=mean[:], op=ALU.subtract),
                            reads=[convT_b[j], mean_b], writes=[tmp_b[i]])
                        S.op("dve", lambda e, i=i: e.tensor_tensor(
                            out=tmp[i][:], in0=tmp[i][:], in1=rln[:], op=ALU.mult),
                            reads=[tmp_b[i], rln_b], writes=[tmp_b[i]])
                        S.op("act", lambda e, i=i, j=j: e.activation(
                            out=convT[:, j, :], in_=tmp[i][:], func=AF.Silu,
                            bias=sm("conv_ln_b", l, 16, j), scale=sm("conv_ln_g", l, 16, j)),
                            reads=[tmp_b[i], smalls_b], writes=[convT_b[j]])
                        S.op("act", lambda e, i=i, j=j: e.activation(out=sq[i][:], in_=convT[:, j, :], func=AF.Square),
                             reads=[convT_b[j]], writes=[sq_b[i]])
                        for th in range(2):
                            S.op("pe", lambda e, i=i, th=th, j=j: e.matmul(
                                psb[pss[th]][:], lhsT=ones_bf, rhs=sq[i][:, th * 512:(th + 1) * 512],
                                start=(j == 0), stop=(j == 15)),
                                reads=[sq_b[i], constf_b], writes=[ps_b[pss[th]]], signal=(th == 1))
                    rsc = mean
                    rstd_from(pss, rsc, mean_b, CW, 0)
                    for j in range(16):
                        S.op("dve", lambda e, j=j: e.scalar_tensor_tensor(
                            out=convT[:, j, :], in0=convT[:, j, :], scalar=sm("conv_out_norm", l, 16, j), in1=rsc[:],
                            op0=ALU.mult, op1=ALU.mult),
                            reads=[convT_b[j], mean_b, smalls_b], writes=[convT_b[j]])
                    S.barrier()
                    for b in hin_b:
                        S.put_dsem(b.dsem)

                if cfg.stop == "c":
                    return
                with ExitStack() as m2:
                    NXT = 3
                    xt = [sb(m2, "mxt%d" % i, [128, 512], F32) for i in range(NXT)]
                    xt_b = [Buf("mxt%d" % i, S.get_dsem()) for i in range(NXT)]
                    xn = [sb(m2, "mxn%d" % i, [128, 512], F32) for i in range(NXT)]
                    xn_b = [Buf("mxn%d" % i, S.get_dsem()) for i in range(NXT)]
                    xi = 0
                    for dc in range(KC):
                        w, w_b = load_w(wout_d, (l * KC + dc) * 128)
                        for th in range(2):
                            i = xi % NXT
                            xi += 1
                            cs = slice(c * TCH + th * 512, c * TCH + (th + 1) * 512)
                            ts = slice(th * 512, (th + 1) * 512)
                            S.dma("sp", xt[i][:], X[dc * 128:(dc + 1) * 128, cs], xt_b[i],
                                  reads=[Xb[c][dc]], writes=[xt_b[i]])
                            (py,) = next_ps(1)
                            for kc in range(KC):
                                if kc < 16:
                                    rhs, rb = attnT[:, kc, ts], attnT_b[kc]
                                else:
                                    rhs, rb = convT[:, kc - 16, ts], convT_b[kc - 16]
                                S.op("pe", lambda e, py=py, w=w, kc=kc, rhs=rhs: e.matmul(
                                    psb[py][:], lhsT=w[:, kc, :], rhs=rhs, start=(kc == 0), stop=(kc == KC - 1)),
                                    reads=[w_b, rb], writes=[ps_b[py]], signal=(kc == KC - 1))
                            S.op("dve", lambda e, i=i, py=py: e.tensor_tensor(
                                out=xn[i][:], in0=psb[py][:], in1=xt[i][:], op=ALU.add),
                                reads=[ps_b[py], xt_b[i]], writes=[xn_b[i]])
                            S.dma("sp", X[dc * 128:(dc + 1) * 128, cs], xn[i][:], xn_b[i],
                                  reads=[xn_b[i]], writes=[Xb[c][dc]])
                    S.barrier()
                    for b in xt_b + xn_b:
                        S.put_dsem(b.dsem)

        def phase_final(c, oc):
            with ExitStack() as ph:
                rstd, rstd_b = phase_pre(ph, X, Xb[c], c, "final_norm", 0, None, None, want_xb=False)
                NXS = 3
                xs = [sb(ph, "fx%d" % i, [128, TCH], F32) for i in range(NXS)]
                xs_b = [Buf("fx%d" % i, S.get_dsem()) for i in range(NXS)]
                xo = [sb(ph, "fo%d" % i, [128, TCH], F32) for i in range(NXS)]
                xo_b = [Buf("fo%d" % i, S.get_dsem()) for i in range(NXS)]
                for kc in range(KC):
                    i = kc % NXS
                    S.dma("sp", xs[i][:], X[kc * 128:(kc + 1) * 128, c * TCH:(c + 1) * TCH], xs_b[i],
                          reads=[Xb[c][kc]], writes=[xs_b[i]])
                    o = soff["final_norm"] + kc
                    S.op("dve", lambda e, i=i, o=o: e.scalar_tensor_tensor(
                        out=xo[i][:], in0=xs[i][:], scalar=smalls[:, o:o + 1], in1=rstd[:],
                        op0=ALU.mult, op1=ALU.mult),
                        reads=[xs_b[i], rstd_b, smalls_b], writes=[xo_b[i]])
                    S.dma("sp", outT[kc * 128:(kc + 1) * 128, oc * TCH:(oc + 1) * TCH], xo[i][:], xo_b[i],
                          reads=[xo_b[i]], writes=[Outb])
                S.barrier()
                for b in xs_b + xo_b:
                    S.put_dsem(b.dsem)

        def phase_copy_in(c):
            with ExitStack() as ph:
                NXS = 3
                xs = [sb(ph, "ci%d" % i, [128, TCH], F32) for i in range(NXS)]
                xs_b = [Buf("ci%d" % i, S.get_dsem()) for i in range(NXS)]
                for kc in range(KC):
                    i = kc % NXS
                    S.dma("sp", xs[i][:], xT_in[kc * 128:(kc + 1) * 128, c * TCH:(c + 1) * TCH], xs_b[i],
                          reads=[Xin_b], writes=[xs_b[i]])
                    S.dma("sp", X[kc * 128:(kc + 1) * 128, c * TCH:(c + 1) * TCH], xs[i][:], xs_b[i],
                          reads=[xs_b[i]], writes=[Xb[c][kc]])
                S.barrier()
                for b in xs_b:
                    S.put_dsem(b.dsem)

        from_in = [True] * NCH
        xin_bufs = [Xin_b] * KC
        for l in range(L):
            for c in range(NCH):
                if "f1" in cfg.phases:
                    if from_in[c]:
                        phase_ffn(l, c, "f1", "ffn1_norm", xT_in, xin_bufs)
                        from_in[c] = False
                    else:
                        phase_ffn(l, c, "f1", "ffn1_norm", X, Xb[c])
                if "mix" in cfg.phases:
                    if from_in[c]:
                        phase_copy_in(c)
                        from_in[c] = False
                    phase_mix(l, c)
                if "f2" in cfg.phases:
                    phase_ffn(l, c, "f2", "ffn2_norm", X, Xb[c])
        if cfg.final:
            for oc, c in enumerate(cfg.out_chunks):
                phase_final(c, oc)
        S.final_wait("sp")
        print("instructions emitted:", S.ninst, file=sys.stderr)
    return nc


def make_inputs(inp, cfg, win_start=0):
    L, T = cfg.L, cfg.T
    x = np.asarray(inp["x"], np.float32)[0]
    Sq = x.shape[0]
    xw = np.zeros((T, D), np.float32)
    lo = max(win_start, 0)
    hi = min(win_start + T, Sq)
    xw[lo - win_start:hi - win_start] = x[lo:hi]
    m = {"xT": np.ascontiguousarray(xw.T)}
    m["smalls"] = pack_smalls(inp, L)
    m["consts"] = const_tiles()
    pos = np.arange(win_start, win_start + T).astype(np.float32)
    m["rope"] = rope_tables(pos)
    gpos = np.arange(win_start - PADK, win_start + T)
    m["kbias"] = np.where(gpos >= 0, 0.0, NEG).astype(np.float32)[None, :]
    for f, pre in (("f1", "ffn1"), ("f2", "ffn2")):
        m[f + "g"] = np.concatenate([tile_w(np.asarray(inp[pre + "_w_gate"][l], np.float32), KC, FCN) for l in range(L)], 0)
        m[f + "u"] = np.concatenate([tile_w(np.asarray(inp[pre + "_w_up"][l], np.float32), KC, FCN) for l in range(L)], 0)
        m[f + "d"] = np.concatenate([tile_wd(np.asarray(inp[pre + "_w_down"][l], np.float32)) for l in range(L)], 0)
    cols = win_cols()
    m["win"] = np.concatenate([tile_w(np.asarray(inp["w_in"][l], np.float32)[:, cols], KC, NOC) for l in range(L)], 0)
    m["wout"] = np.concatenate([tile_w(np.asarray(inp["w_out"][l], np.float32), KC, KC) for l in range(L)], 0)
    return m


_PROG = {}


def kernel(**inputs):
    cfg = Cfg(L=4, NCH=8)
    key = "full"
    if key not in _PROG:
        _PROG[key] = build_program(cfg)
    nc = _PROG[key]
    m = make_inputs(inputs, cfg, 0)
    res = run_bass_kernel_spmd(nc, [m], core_ids=[0])
    oT = res.results[0]["outT"]
    return np.ascontiguousarray(oT.T)[None].astype(np.float32)
```

```python
import sys
from contextlib import ExitStack
import numpy as np
import concourse.bass as bass
import concourse.mybir as mybir
from concourse.bass_utils import run_bass_kernel_spmd

F32 = mybir.dt.float32
BF16 = mybir.dt.bfloat16
AF = mybir.ActivationFunctionType
ALU = mybir.AluOpType

D = 4096
DFF = 6144
KC = D // 128
FCN = DFF // 128
NH = 16
AW = 2048
CW = 2048
TCH = 1024
NOC = 88
PADK = 2048
PADH = 32
NEG = -30000.0
RMS_EPS = 1e-5
LN_EPS = 1e-5
ROPE_THETA = 500000.0
BRANCH_D = (1, 4, 16)


class DSem:
    def __init__(self, sem):
        self.sem = sem
        self.cnt = 0


class Buf:
    __slots__ = ("name", "w", "r", "dsem")

    def __init__(self, name, dsem=None):
        self.name = name
        self.w = None
        self.r = {}
        self.dsem = dsem


class Sched:
    def __init__(self, nc, stack, n_dsems=40):
        self.nc = nc
        self.eng = {"pe": nc.tensor, "act": nc.scalar, "dve": nc.vector, "pool": nc.gpsimd, "sp": nc.sync}
        self.esem = {}
        self.ecnt = {}
        for k in ("pe", "act", "dve", "pool"):
            self.esem[k] = stack.enter_context(nc.semaphore("e_" + k))
            self.ecnt[k] = 0
        self.waited = {k: {} for k in self.eng}
        self.free_dsems = [DSem(stack.enter_context(nc.semaphore("d%d" % i))) for i in range(n_dsems)]
        self.free_swsems = [DSem(stack.enter_context(nc.semaphore("w%d" % i))) for i in range(10)]
        self.all_dsems = list(self.free_dsems) + list(self.free_swsems)
        self.ninst = 0
        self.deferred = []

    def get_dsem(self, sw=False):
        d = (self.free_swsems if sw else self.free_dsems).pop()
        d.sw = sw
        return d

    def put_dsem(self, d):
        (self.free_swsems if d.sw else self.free_dsems).append(d)

    def _wait(self, E, deps):
        w = self.waited[E]
        for key, val in deps:
            if isinstance(key, DSem):
                val = key.cnt
                sem = key.sem
            else:
                if key == E and val > self.ecnt[E]:
                    continue
                if key == E and E == "pe":
                    continue
                sem = self.esem[key]
            if w.get(key, 0) >= val:
                continue
            self.eng[E].wait_ge(sem, val)
            self.ninst += 1
            w[key] = val

    def _deps(self, reads, writes):
        deps = []
        for b in reads:
            if b.w is not None:
                deps.append(b.w)
        for b in writes:
            if b.w is not None:
                deps.append(b.w)
            deps.extend(b.r.items())
        return deps

    def _stamp(self, stamp, reads, writes):
        k, v = stamp
        for b in reads:
            if b.r.get(k, 0) < v:
                b.r[k] = v
        for b in writes:
            b.w = stamp
            b.r = {}

    def op(self, E, fn, reads=(), writes=(), signal=True):
        self._wait(E, self._deps(reads, writes))
        ins = fn(self.eng[E])
        self.ninst += 1
        if signal:
            self.ecnt[E] += 1
            ins.then_inc(self.esem[E], 1)
            stamp = (E, self.ecnt[E])
        else:
            stamp = (E, self.ecnt[E] + 1)
        self._stamp(stamp, reads, writes)
        return ins

    def dma(self, Q, out, in_, sembuf, reads=(), writes=()):
        self._wait(Q, self._deps(reads, writes))
        ins = self.eng[Q].dma_start(out=out, in_=in_)
        self.ninst += 1
        d = sembuf.dsem
        d.cnt += 16
        ins.then_inc(d.sem, 16)
        self._stamp((d, d.cnt), reads, writes)
        return ins

    def barrier(self):
        for E in self.eng:
            deps = [(k, self.ecnt[k]) for k in self.esem if k != E and self.ecnt[k] > 0]
            deps += [(d, d.cnt) for d in self.all_dsems if d.cnt > 0]
            self._wait(E, deps)
        for d in self.deferred:
            self.put_dsem(d)
        self.deferred = []

    def final_wait(self, E="sp"):
        deps = [(d, d.cnt) for d in self.all_dsems if d.cnt > 0]
        self._wait(E, deps)


def smalls_layout(L):
    off = {}
    o = 0
    for nm in ("ffn1_norm", "mix_norm", "ffn2_norm"):
        off[nm] = o
        o += L * KC
    off["final_norm"] = o
    o += KC
    for nm in ("attn_out_norm", "conv_out_norm", "conv_b", "conv_ln_g", "conv_ln_b"):
        off[nm] = o
        o += L * 16
    off["conv_w"] = o
    o += L * 16 * 31
    return off, o


def pack_smalls(inp, L):
    off, n = smalls_layout(L)
    s = np.zeros((128, n), np.float32)
    for nm in ("ffn1_norm", "mix_norm", "ffn2_norm"):
        a = np.asarray(inp[nm], np.float32)[:L].reshape(L, KC, 128).transpose(2, 0, 1).reshape(128, L * KC)
        s[:, off[nm]:off[nm] + L * KC] = a
    s[:, off["final_norm"]:off["final_norm"] + KC] = np.asarray(inp["final_norm"], np.float32).reshape(KC, 128).T
    for nm in ("attn_out_norm", "conv_out_norm", "conv_b", "conv_ln_g", "conv_ln_b"):
        a = np.asarray(inp[nm], np.float32)[:L].reshape(L, 16, 128).transpose(2, 0, 1).reshape(128, L * 16)
        s[:, off[nm]:off[nm] + L * 16] = a
    cw = np.asarray(inp["conv_w"], np.float32)[:L]
    a = cw.reshape(L, 31, 16, 128).transpose(3, 0, 2, 1).reshape(128, L * 16 * 31)
    s[:, off["conv_w"]:off["conv_w"] + L * 16 * 31] = a
    return s


def win_cols():
    cols = []
    sw = np.concatenate([np.arange(16, 32), np.arange(0, 16)])

    def swap_chunks(base):
        for s in range(4):
            for j in range(4):
                h = 4 * s + j
                cols.append(base + h * 128 + sw)

    swap_chunks(0)
    cols.append(np.arange(0, AW))
    swap_chunks(AW)
    cols.append(np.arange(AW, 2 * AW))
    cols.append(np.arange(2 * AW, 3 * AW))
    for j in range(16):
        cols.append(3 * AW + j * 128 + np.arange(128))
        cols.append(3 * AW + CW + j * 128 + np.arange(128))
    c = np.concatenate(cols)
    assert c.size == NOC * 128
    return c


OC_QSW, OC_Q, OC_KSW, OC_K, OC_V, OC_AG = 0, 4, 20, 24, 40, 56


def tile_w(w, nk, nout):
    return np.ascontiguousarray(
        w.reshape(nk, 128, nout, 128).transpose(2, 1, 0, 3)).reshape(nout * 128, nk * 128)


def tile_wd(w):
    return np.ascontiguousarray(
        w.reshape(2, 24, 128, KC, 128).transpose(0, 3, 2, 1, 4)).reshape(2 * KC * 128, 24 * 128)


def const_tiles():
    j = np.arange(128)[:, None]
    i = np.arange(128)[None, :]
    ident = (j == i).astype(np.float32)
    mA = np.where(j >= i, 0.0, NEG).astype(np.float32)
    mB = np.where(j <= i, 0.0, NEG).astype(np.float32)
    ones = np.ones((128, 128), np.float32)
    m16 = np.concatenate([mA[:, :64], mB[:, :64]], axis=1)
    z1 = lambda m: (m == 0).astype(np.float32)
    return np.concatenate([ident, mA, mB, ones, m16, z1(mA), z1(mB), z1(m16)], axis=1)


def rope_tables(pos):
    inv = (np.float32(ROPE_THETA) ** (-(np.arange(0, 32, 2, dtype=np.float32) / np.float32(32)))).astype(np.float32)
    ang = (pos.astype(np.float32)[:, None] * inv[None, :]).astype(np.float32)
    c = np.cos(ang).astype(np.float32).T
    s = np.sin(ang).astype(np.float32).T
    one = np.ones((96, pos.shape[0]), np.float32)
    zero = np.zeros((96, pos.shape[0]), np.float32)
    return np.ascontiguousarray(np.concatenate([c, c, one, -s, s, zero], axis=0))


class Cfg:
    def __init__(self, L=4, NCH=8, out_chunks=None, phases=("f1", "mix", "f2"), final=True, debug=(), stop=None):
        self.stop = stop
        self.L = L
        self.NCH = NCH
        self.T = NCH * TCH
        self.out_chunks = list(range(NCH)) if out_chunks is None else list(out_chunks)
        self.phases = phases
        self.final = final
        self.debug = debug


def build_program(cfg):
    L, NCH, T = cfg.L, cfg.NCH, cfg.T
    nc = bass.Bass("TRN2", target_bir_lowering=False)
    soff, NS = smalls_layout(L)

    def din(name, shape, dt=F32):
        return nc.dram_tensor(name, list(shape), dt, kind="ExternalInput").ap()

    xT_in = din("xT", [D, T])
    smalls_d = din("smalls", [128, NS])
    consts_d = din("consts", [128, 1024])
    rope_d = din("rope", [256, T])
    kbias_d = din("kbias", [1, PADK + T])
    wts = {}
    for f in ("f1", "f2"):
        wts[f + "g"] = din(f + "g", [L * FCN * 128, KC * 128])
        wts[f + "u"] = din(f + "u", [L * FCN * 128, KC * 128])
        wts[f + "d"] = din(f + "d", [L * 2 * KC * 128, 24 * 128])
    wts["win"] = din("win", [L * NOC * 128, KC * 128])
    wts["wout"] = din("wout", [L * KC * 128, KC * 128])
    nout = len(cfg.out_chunks)
    outT = nc.dram_tensor("outT", [D, nout * TCH], F32, kind="ExternalOutput").ap()

    X = nc.dram_tensor("Xs", [D, T], F32, kind="Internal").ap()
    KT = nc.dram_tensor("KTs", [AW, PADK + T], BF16, kind="Internal").ap()
    QT = nc.dram_tensor("QTs", [AW, TCH], BF16, kind="Internal").ap()
    Vd = nc.dram_tensor("Vs", [PADK + T, AW], BF16, kind="Internal").ap()
    Hd = nc.dram_tensor("Hs", [CW, PADH + T], BF16, kind="Internal").ap()

    es = ExitStack()
    with es:
        S = Sched(nc, es)

        uid = [0]

        def sb(stack, name, shape, dt):
            uid[0] += 1
            return stack.enter_context(nc.sbuf_tensor("s%d_%s" % (uid[0], name), list(shape), dt))

        Xb = [[Buf("X%d_%d" % (c, k)) for k in range(KC)] for c in range(NCH)]
        Xin_b = Buf("xin")
        KTb = [[Buf("KT%d_%d" % (c, h)) for h in range(NH)] for c in range(NCH)]
        KTpad = Buf("KTpad")
        QTb = [Buf("QT%d" % h) for h in range(NH)]
        Vb = [Buf("V%d" % c) for c in range(NCH)]
        Vpad = Buf("Vpad")
        Hb = [[Buf("H%d_%d" % (c, j)) for j in range(16)] for c in range(NCH)]
        Hpad = Buf("Hpad")
        Outb = Buf("out")
        cbuf = Buf("const_in")

        smalls = sb(es, "smalls", [128, NS], F32)
        smalls_b = Buf("smalls", S.get_dsem())
        constf = sb(es, "constf", [128, 1024], BF16)
        constf_b = Buf("constf", S.get_dsem(True))
        NW = 4
        wslot = [sb(es, "wslot%d" % i, [128, KC, 128], BF16) for i in range(NW)]
        wslot_b = [Buf("wslot%d" % i, S.get_dsem(True)) for i in range(NW)]
        wrr = [0]
        psb = [es.enter_context(nc.psum_tensor("ps%d" % i, [128, 512], F32)) for i in range(8)]
        ps_b = [Buf("ps%d" % i) for i in range(8)]
        prr = [0]

        S.dma("sp", smalls[:], smalls_d[:, :], smalls_b, reads=[cbuf], writes=[smalls_b])
        S.dma("pool", constf[:], consts_d[:, :], constf_b, reads=[cbuf], writes=[constf_b])
        ident = constf[:, 0:128]
        ones_bf = constf[:, 384:512]

        def sm(name, l, n, j):
            o = soff[name] + l * n + j
            return smalls[:, o:o + 1]

        ring = [list(range(8))]

        def next_ps(n=1):
            r = []
            for _ in range(n):
                r.append(ring[0][prr[0] % len(ring[0])])
                prr[0] += 1
            return r

        def load_w(src2d, row0, ncols=KC * 128):
            i = wrr[0] % NW
            wrr[0] += 1
            dst = wslot[i][:].rearrange("p k m -> p (k m)")[:, 0:ncols]
            S.dma("pool", dst, src2d[row0:row0 + 128, 0:ncols], wslot_b[i], reads=[cbuf], writes=[wslot_b[i]])
            return wslot[i], wslot_b[i]

        with ExitStack() as ph:
            z = sb(ph, "zpad", [128, 2048], BF16)
            zb = Buf("zpad", S.get_dsem())
            zf = sb(ph, "zpadf", [128, 16 * PADH], BF16)
            zfb = Buf("zpadf", S.get_dsem())
            S.op("dve", lambda e: e.memset(z[:], 0.0), writes=[zb])
            S.op("dve", lambda e: e.memset(zf[:], 0.0), writes=[zfb])
            for j in range(16):
                S.dma("sp", KT[j * 128:(j + 1) * 128, 0:PADK], z[:], zb, reads=[zb], writes=[KTpad])
                S.dma("sp", Vd[j * 128:(j + 1) * 128, :], z[:], zb, reads=[zb], writes=[Vpad])
            S.dma("sp", Hd[:, 0:PADH].rearrange("(j p) t -> p j t", p=128),
                  zf[:].rearrange("p (j t) -> p j t", j=16), zfb, reads=[zfb], writes=[Hpad])
            S.barrier()
            S.put_dsem(zb.dsem)
            S.put_dsem(zfb.dsem)

        def phase_pre(ph, src, src_bufs, c, gname, l, xb, xb_b, want_xb=True, rstd_keep=None):
            rstd = sb(ph, "rstd", [128, TCH], F32)
            rstd_b = Buf("rstd")
            if True:
                sub = ph
                NXS = 4
                xs = [sb(sub, "xs%d" % i, [128, TCH], F32) for i in range(NXS)]
                xs_b = [Buf("xs%d" % i, S.get_dsem()) for i in range(NXS)]
                sq = [sb(sub, "sq%d" % i, [128, TCH], BF16) for i in range(2)]
                sq_b = [Buf("sq%d" % i) for i in range(2)]
                pss = next_ps(2)
                for kc in range(KC):
                    i = kc % NXS
                    S.dma("sp", xs[i][:], src[kc * 128:(kc + 1) * 128, c * TCH:(c + 1) * TCH], xs_b[i],
                          reads=[src_bufs[kc]], writes=[xs_b[i]])
                    q = kc % 2
                    S.op("act", lambda e, i=i, q=q: e.activation(out=sq[q][:], in_=xs[i][:], func=AF.Square),
                         reads=[xs_b[i]], writes=[sq_b[q]])
                    if want_xb:
                        S.op("dve", lambda e, i=i, kc=kc: e.tensor_scalar(
                            out=xb[:, kc, :], in0=xs[i][:], scalar1=sm(gname, l, KC, kc), scalar2=None, op0=ALU.mult),
                            reads=[xs_b[i], smalls_b], writes=[xb_b[kc]])
                    for th in range(2):
                        S.op("pe", lambda e, q=q, th=th, kc=kc: e.matmul(
                            psb[pss[th]][:], lhsT=ones_bf, rhs=sq[q][:, th * 512:(th + 1) * 512],
                            start=(kc == 0), stop=(kc == KC - 1)),
                            reads=[sq_b[q], constf_b], writes=[ps_b[pss[th]]], signal=(th == 1))
                for th in range(2):
                    S.op("act", lambda e, th=th: e.activation(
                        out=rstd[:, th * 512:(th + 1) * 512], in_=psb[pss[th]][:], func=AF.Sqrt,
                        bias=epsb[:, 0:1], scale=1.0 / D),
                        reads=[ps_b[pss[th]], eps_b], writes=[rstd_b])
                S.op("dve", lambda e: e.reciprocal(out=rstd[:], in_=rstd[:]), reads=[rstd_b], writes=[rstd_b])
                if want_xb:
                    for kc in range(KC):
                        S.op("dve", lambda e, kc=kc: e.tensor_tensor(
                            out=xb[:, kc, :], in0=xb[:, kc, :], in1=rstd[:], op=ALU.mult),
                            reads=[rstd_b], writes=[xb_b[kc]])
                S.deferred.extend(b.dsem for b in xs_b)
            return rstd, rstd_b

        epsb = sb(es, "epsb", [128, 2], F32)
        eps_b = Buf("eps")
        S.op("dve", lambda e: e.memset(epsb[:, 0:1], RMS_EPS), writes=[eps_b])
        S.op("dve", lambda e: e.memset(epsb[:, 1:2], LN_EPS), writes=[eps_b])

        def phase_ffn(l, c, f, gname, src, src_bufs):
            with ExitStack() as ph:
                xb = sb(ph, "xb", [128, KC, TCH], BF16)
                xb_b = [Buf("xb%d" % k) for k in range(KC)]
                phase_pre(ph, src, src_bufs, c, gname, l, xb, xb_b)
                h = sb(ph, "h", [128, 24, TCH], BF16)
                h_b = [Buf("h%d" % k) for k in range(24)]
                sg = [sb(ph, "sg%d" % i, [128, 512], F32) for i in range(2)]
                sg_b = [Buf("sg%d" % i) for i in range(2)]
                NXT = 3
                xt = [sb(ph, "xt%d" % i, [128, 512], F32) for i in range(NXT)]
                xt_b = [Buf("xt%d" % i, S.get_dsem()) for i in range(NXT)]
                xn = [sb(ph, "xn%d" % i, [128, 512], F32) for i in range(NXT)]
                xn_b = [Buf("xn%d" % i, S.get_dsem()) for i in range(NXT)]
                ei = [0]
                xi = [0]
                wg_d, wu_d, wd_d = wts[f + "g"], wts[f + "u"], wts[f + "d"]
                for half in range(2):
                    cur_src, cur_bufs = (src, src_bufs) if half == 0 else (X, Xb[c])
                    for fcl in range(24):
                        fc = half * 24 + fcl
                        wg, wg_b = load_w(wg_d, (l * FCN + fc) * 128)
                        wu, wu_b = load_w(wu_d, (l * FCN + fc) * 128)
                        for th in range(2):
                            pg, pu = next_ps(2)
                            for (w, w_b, p) in ((wg, wg_b, pg), (wu, wu_b, pu)):
                                for kc in range(KC):
                                    S.op("pe", lambda e, w=w, p=p, kc=kc, th=th: e.matmul(
                                        psb[p][:], lhsT=w[:, kc, :], rhs=xb[:, kc, th * 512:(th + 1) * 512],
                                        start=(kc == 0), stop=(kc == KC - 1)),
                                        reads=[w_b, xb_b[kc]], writes=[ps_b[p]], signal=(kc == KC - 1))
                            q = ei[0] % 2
                            ei[0] += 1
                            S.op("act", lambda e, q=q, pg=pg: e.activation(out=sg[q][:], in_=psb[pg][:], func=AF.Silu),
                                 reads=[ps_b[pg]], writes=[sg_b[q]])
                            S.op("dve", lambda e, q=q, pu=pu, fcl=fcl, th=th: e.tensor_tensor(
                                out=h[:, fcl, th * 512:(th + 1) * 512], in0=psb[pu][:], in1=sg[q][:], op=ALU.mult),
                                reads=[ps_b[pu], sg_b[q]], writes=[h_b[fcl]])
                    for dc in range(KC):
                        wd, wd_b = load_w(wd_d, ((l * 2 + half) * KC + dc) * 128, ncols=24 * 128)
                        for th in range(2):
                            i = xi[0] % NXT
                            xi[0] += 1
                            cs = slice(c * TCH + th * 512, c * TCH + (th + 1) * 512)
                            S.dma("sp", xt[i][:], cur_src[dc * 128:(dc + 1) * 128, cs], xt_b[i],
                                  reads=[cur_bufs[dc]], writes=[xt_b[i]])
                            (py,) = next_ps(1)
                            for fcl in range(24):
                                S.op("pe", lambda e, py=py, fcl=fcl, th=th, wd=wd: e.matmul(
                                    psb[py][:], lhsT=wd[:, fcl, :], rhs=h[:, fcl, th * 512:(th + 1) * 512],
                                    start=(fcl == 0), stop=(fcl == 23)),
                                    reads=[wd_b, h_b[fcl]], writes=[ps_b[py]], signal=(fcl == 23))
                            S.op("dve", lambda e, i=i, py=py: e.scalar_tensor_tensor(
                                out=xn[i][:], in0=psb[py][:], scalar=0.5, in1=xt[i][:], op0=ALU.mult, op1=ALU.add),
                                reads=[ps_b[py], xt_b[i]], writes=[xn_b[i]])
                            S.dma("sp", X[dc * 128:(dc + 1) * 128, cs], xn[i][:], xn_b[i],
                                  reads=[xn_b[i]], writes=[Xb[c][dc]])
                S.barrier()
                for b in xt_b + xn_b:
                    S.put_dsem(b.dsem)


        SCALE = float(128 ** -0.5)
        maskAB = constf[:, 640:896]
        mask16 = constf[:, 896:1024]

        def proj_chunk(w_d, row0, xb, xb_b, nk=KC):
            w, w_b = load_w(w_d, row0)
            ps = next_ps(2)
            for th in range(2):
                for kc in range(nk):
                    S.op("pe", lambda e, w=w, p=ps[th], kc=kc, th=th: e.matmul(
                        psb[p][:], lhsT=w[:, kc, :], rhs=xb[:, kc, th * 512:(th + 1) * 512],
                        start=(kc == 0), stop=(kc == nk - 1)),
                        reads=[w_b, xb_b[kc]], writes=[ps_b[ps[th]]], signal=(kc == nk - 1))
            return ps

        def phase_mix(l, c):
            win_d, wout_d = wts["win"], wts["wout"]
            wrow = lambda oc: (l * NOC + oc) * 128
            with ExitStack() as ph:
                with ExitStack() as m1:
                    xb = sb(m1, "xb", [128, KC, TCH], BF16)
                    xb_b = [Buf("xb%d" % k) for k in range(KC)]
                    phase_pre(m1, X, Xb[c], c, "mix_norm", l, xb, xb_b)
                    with ExitStack() as qk:
                        rope = sb(qk, "rope", [128, 2, TCH], F32)
                        rope_b = Buf("rope", S.get_dsem())
                        if "q_norope" not in cfg.debug:
                            S.dma("sp", rope[:], rope_d[:, c * TCH:(c + 1) * TCH].rearrange("(a p) t -> p a t", a=2),
                                  rope_b, reads=[cbuf], writes=[rope_b])
                        swp = [sb(qk, "swp%d" % i, [128, TCH], F32) for i in range(4)]
                        swp_b = [Buf("swp%d" % i, S.get_dsem()) for i in range(4)]
                        qsw = [sb(qk, "qsw%d" % i, [128, TCH], F32) for i in range(2)]
                        qsw_b = [Buf("qsw%d" % i, S.get_dsem()) for i in range(2)]
                        t1 = [sb(qk, "t1_%d" % i, [128, TCH], F32) for i in range(2)]
                        t1_b = [Buf("t1_%d" % i) for i in range(2)]
                        t2 = [sb(qk, "t2_%d" % i, [128, TCH], F32) for i in range(2)]
                        for i in range(2):
                            S.op("dve", lambda e, i=i: e.memset(qsw[i][:], 0.0), writes=[qsw_b[i]])
                        t2_b = [Buf("t2_%d" % i) for i in range(2)]
                        qo = [sb(qk, "qo%d" % i, [128, TCH], BF16) for i in range(2)]
                        qo_b = [Buf("qo%d" % i, S.get_dsem()) for i in range(2)]
                        cnt = 0
                        for which, oc_sw, oc_h in ((("q", OC_QSW, OC_Q), ("k", OC_KSW, OC_K)) if "qk" not in cfg.debug else ()):
                            for s4 in range(4):
                                ps = proj_chunk(win_d, wrow(oc_sw + s4), xb, xb_b)
                                for th in range(2):
                                    S.op("act", lambda e, s4=s4, th=th, p=ps[th]: e.copy(
                                        out=swp[s4][:, th * 512:(th + 1) * 512], in_=psb[p][:]),
                                        reads=[ps_b[ps[th]]], writes=[swp_b[s4]])
                            for h in range(NH):
                                i = cnt % 2
                                cnt += 1
                                ps = proj_chunk(win_d, wrow(oc_h + h), xb, xb_b)
                                j4 = h % 4
                                if "q_nodma" not in cfg.debug:
                                    S.dma("sp", qsw[i][0:32, :], swp[h // 4][32 * j4:32 * j4 + 32, :], qsw_b[i],
                                          reads=[swp_b[h // 4]], writes=[qsw_b[i]])
                                for th in range(2):
                                    ts = slice(th * 512, (th + 1) * 512)
                                    if "q_noelem" not in cfg.debug:
                                        S.op("dve", lambda e, i=i, p=ps[th], ts=ts: e.tensor_tensor(
                                            out=t1[i][:, ts], in0=psb[p][:], in1=rope[:, 0, ts], op=ALU.mult),
                                            reads=[ps_b[ps[th]], rope_b], writes=[t1_b[i]])
                                if "q_noelem" not in cfg.debug:
                                    S.op("dve", lambda e, i=i: e.tensor_tensor(
                                        out=t2[i][:], in0=qsw[i][:], in1=rope[:, 1, :], op=ALU.mult),
                                        reads=[qsw_b[i], rope_b], writes=[t2_b[i]])
                                    S.op("dve", lambda e, i=i: e.tensor_tensor(
                                        out=qo[i][:], in0=t1[i][:], in1=t2[i][:], op=ALU.add),
                                        reads=[t1_b[i], t2_b[i]], writes=[qo_b[i]])
                                if which == "q":
                                    S.dma("sp", QT[h * 128:(h + 1) * 128, :], qo[i][:], qo_b[i],
                                          reads=[qo_b[i]], writes=[QTb[h]])
                                else:
                                    S.dma("sp", KT[h * 128:(h + 1) * 128, PADK + c * TCH:PADK + (c + 1) * TCH],
                                          qo[i][:], qo_b[i], reads=[qo_b[i]], writes=[KTb[c][h]])
                        S.barrier()
                        for b in [rope_b] + swp_b + qsw_b + qo_b:
                            S.put_dsem(b.dsem)
                    with ExitStack() as vg:
                        vT = [sb(vg, "vT%d" % i, [128, TCH], BF16) for i in range(2)]
                        vT_b = [Buf("vT%d" % i) for i in range(2)]
                        vtok = sb(vg, "vtok", [128, 8, 1024], BF16)
                        vtok_b = Buf("vtok", S.get_dsem())
                        for hv in (range(NH) if "v" not in cfg.debug else ()):
                            i = hv % 2
                            ps = proj_chunk(win_d, wrow(OC_V + hv), xb, xb_b)
                            for th in range(2):
                                S.op("act", lambda e, i=i, th=th, p=ps[th]: e.copy(
                                    out=vT[i][:, th * 512:(th + 1) * 512], in_=psb[p][:]),
                                    reads=[ps_b[ps[th]]], writes=[vT_b[i]])
                            (pt,) = next_ps(1)
                            ptv = psb[pt][:].bitcast(BF16)
                            for tt in range(8):
                                S.op("pe", lambda e, i=i, tt=tt, ptv=ptv: e.transpose(
                                    ptv[:, tt * 128:(tt + 1) * 128], vT[i][:, tt * 128:(tt + 1) * 128], ident),
                                    reads=[vT_b[i], constf_b], writes=[ps_b[pt]], signal=(tt == 7))
                            hh = hv % 8
                            S.op("dve", lambda e, hh=hh, ptv=ptv: e.tensor_copy(
                                out=vtok[:, :, hh * 128:(hh + 1) * 128],
                                in_=ptv.rearrange("p (t d) -> p t d", t=8)),
                                reads=[ps_b[pt]], writes=[vtok_b])
                            if hh == 7:
                                g8 = hv // 8
                                S.dma("sp", Vd[PADK + c * TCH:PADK + (c + 1) * TCH, g8 * 1024:(g8 + 1) * 1024]
                                      .rearrange("(t p) d -> p t d", p=128), vtok[:], vtok_b,
                                      reads=[vtok_b], writes=[Vb[c]])
                        sgm = [sb(vg, "sgm%d" % i, [128, 512], F32) for i in range(2)]
                        sgm_b = [Buf("sgm%d" % i) for i in range(2)]
                        hT = [sb(vg, "hT%d" % i, [128, TCH], BF16) for i in range(2)]
                        hT_b = [Buf("hT%d" % i, S.get_dsem()) for i in range(2)]
                        si = 0
                        for j in (range(16) if "ag" not in cfg.debug else ()):
                            i = j % 2
                            pa = proj_chunk(win_d, wrow(OC_AG + 2 * j), xb, xb_b)
                            pg = proj_chunk(win_d, wrow(OC_AG + 2 * j + 1), xb, xb_b)
                            for th in range(2):
                                q = si % 2
                                si += 1
                                S.op("act", lambda e, q=q, p=pg[th]: e.activation(
                                    out=sgm[q][:], in_=psb[p][:], func=AF.Sigmoid),
                                    reads=[ps_b[pg[th]]], writes=[sgm_b[q]])
                                S.op("dve", lambda e, i=i, q=q, th=th, p=pa[th]: e.tensor_tensor(
                                    out=hT[i][:, th * 512:(th + 1) * 512], in0=psb[p][:], in1=sgm[q][:], op=ALU.mult),
                                    reads=[ps_b[pa[th]], sgm_b[q]], writes=[hT_b[i]])
                            S.dma("sp", Hd[j * 128:(j + 1) * 128, PADH + c * TCH:PADH + (c + 1) * TCH], hT[i][:],
                                  hT_b[i], reads=[hT_b[i]], writes=[Hb[c][j]])
                        S.barrier()
                        for b in [vtok_b] + hT_b:
                            S.put_dsem(b.dsem)

                if cfg.stop == "m1":
                    return
                attnT = sb(ph, "attnT", [128, NH, TCH], BF16)
                attnT_b = [Buf("attnT%d" % h) for h in range(NH)]
                with ExitStack() as at:
                    kb = sb(at, "kb", [1, 3072], BF16)
                    kb_b = Buf("kb", S.get_dsem(True))
                    S.dma("pool", kb[:], kbias_d[0:1, c * TCH:c * TCH + 3072], kb_b, reads=[cbuf], writes=[kb_b])
                    qT = [sb(at, "qT%d" % i, [128, TCH], BF16) for i in range(2)]
                    qT_b = [Buf("qT%d" % i, S.get_dsem()) for i in range(2)]
                    kT = [sb(at, "kT%d" % i, [128, 3072], BF16) for i in range(2)]
                    kT_b = [Buf("kT%d" % i, S.get_dsem()) for i in range(2)]
                    vt = [sb(at, "vt%d" % i, [128, 53, 256], BF16) for i in range(2)]
                    vt_b = [Buf("vt%d" % i, S.get_dsem()) for i in range(2)]
                    NPT = 5
                    pts = [sb(at, "pts%d" % i, [128, 256], BF16) for i in range(NPT)]
                    pts_b = [Buf("pts%d" % i) for i in range(NPT)]
                    rd = [sb(at, "rd%d" % i, [128, 512], F32) for i in range(2)]
                    rd_b = [Buf("rd%d" % i) for i in range(2)]
                    ring[0] = [4, 5, 6, 7]
                    NUM = (0, 1)
                    DEN = (2, 3)
                    kdeps = [KTpad] if c < 2 else []
                    vdeps = [Vpad] if c < 2 else []
                    for cc in range(max(0, c - 2), c + 1):
                        vdeps.append(Vb[cc])
                    ptis = [0]
                    for h in range(NH):
                        i = h % 2
                        S.dma("sp", qT[i][:], QT[h * 128:(h + 1) * 128, :], qT_b[i], reads=[QTb[h]], writes=[qT_b[i]])
                        S.dma("sp", kT[i][:], KT[h * 128:(h + 1) * 128, c * TCH:c * TCH + 3072], kT_b[i],
                              reads=kdeps + [KTb[cc][h] for cc in range(max(0, c - 2), c + 1)], writes=[kT_b[i]])
                        vi = (h // 2) % 2
                        if h % 2 == 0:
                            cols = slice(h * 128, h * 128 + 256)
                            r1 = PADK + c * TCH - 128
                            S.dma("sp", vt[vi][:, 0:9, :], Vd[r1:r1 + 9 * 128, cols].rearrange("(b p) d -> p b d", p=128),
                                  vt_b[vi], reads=vdeps, writes=[vt_b[vi]])
                            r4 = PADK + c * TCH - 512
                            for r in range(4):
                                S.dma("sp", vt[vi][:, 9 + 3 * r:12 + 3 * r, :],
                                      Vd[r4 + r:r4 + 1536:4, cols].rearrange("(b p) d -> p b d", p=128),
                                      vt_b[vi], reads=vdeps, writes=[vt_b[vi]])
                            rA = PADK + c * TCH - 2048
                            S.dma("sp", vt[vi][:, 21:53:2, :],
                                  Vd[rA:rA + 2048, cols].rearrange("(p r) d -> p r d", r=16),
                                  vt_b[vi], reads=vdeps, writes=[vt_b[vi]])
                            rB = PADK + c * TCH
                            S.dma("sp", vt[vi][0:64, 22:53:2, :],
                                  Vd[rB:rB + 1024, cols].rearrange("(p r) d -> p r d", r=16),
                                  vt_b[vi], reads=vdeps, writes=[vt_b[vi]])
                        hh = h % 2
                        first = [True, True, True, True]
                        tiles = []
                        for qb in range(8):
                            tiles.append((slice(128 * qb, 128 * qb + 128), slice(2048 - 128 + 128 * qb, 2048 + 128 * qb),
                                          slice(2048 + 128 * qb, 2048 + 128 * qb + 128), 128, qb, qb + 1, 128,
                                          [(qb // 4, slice((qb % 4) * 128, (qb % 4) * 128 + 128), slice(0, 128))]))
                        for r in range(4):
                            for qb in range(2):
                                tiles.append((slice(512 * qb + r, 512 * qb + 512, 4),
                                              slice(2048 + 512 * qb - 512 + r, 2048 + 512 * qb, 4),
                                              slice(2048 + 512 * qb + r, 2048 + 512 * qb + 512, 4), 128,
                                              9 + 3 * r + qb, 9 + 3 * r + qb + 1, 128,
                                              [(qb, slice(r, 512, 4), slice(0, 128))]))
                        for r in range(16):
                            tiles.append((slice(r, 1024, 16), slice(r, 2048, 16), slice(2048 + r, 3072, 16), 64,
                                          21 + 2 * r, 22 + 2 * r, 64,
                                          [(0, slice(r, 512, 16), slice(0, 32)), (1, slice(r, 512, 16), slice(32, 64))]))
                        def stage1(tile):
                            pass
                            (qs, ka, kbs, KB, sA, sB, Nq, outs) = tile
                            (sp_,) = next_ps(1)
                            pbank = psb[sp_]
                            rds = [kT_b[i], qT_b[i]]
                            S.op("pe", lambda e, pbank=pbank, ka=ka, qs=qs, Nq=Nq, i=i: e.matmul(
                                pbank[:, 0:Nq], lhsT=kT[i][:, ka], rhs=qT[i][:, qs], start=True, stop=False,
                                skip_group_check=True), reads=rds, writes=[ps_b[sp_]], signal=False)
                            need_kb = (ka.start < 2048 - c * TCH)
                            S.op("pe", lambda e, pbank=pbank, kbs=kbs, qs=qs, Nq=Nq, KB=KB, i=i: e.matmul(
                                pbank[0:KB, Nq:2 * Nq], lhsT=kT[i][:, kbs], rhs=qT[i][:, qs], start=False, stop=not need_kb,
                                skip_group_check=True), reads=rds, writes=[ps_b[sp_]], signal=not need_kb)
                            if need_kb:
                                S.op("pe", lambda e, pbank=pbank, ka=ka, Nq=Nq: e.matmul(
                                    pbank[:, 0:Nq], lhsT=kb[0:1, ka], rhs=constf[0:1, 384:384 + Nq], start=False, stop=True,
                                    skip_group_check=True), reads=[kb_b, constf_b], writes=[ps_b[sp_]], signal=True)
                            msk = maskAB if Nq == 128 else mask16
                            pi = ptis[0] % NPT
                            ptis[0] += 1
                            S.op("act", lambda e, pbank=pbank, pi=pi, Nq=Nq: e.activation(
                                out=pts[pi][:, 0:2 * Nq], in_=pbank[:, 0:2 * Nq], func=AF.Exp, scale=SCALE),
                                reads=[ps_b[sp_]], writes=[pts_b[pi]])
                            S.op("dve", lambda e, pi=pi, Nq=Nq, msk=msk: e.tensor_tensor(
                                out=pts[pi][:, 0:2 * Nq], in0=pts[pi][:, 0:2 * Nq], in1=msk, op=ALU.mult),
                                reads=[pts_b[pi], constf_b], writes=[pts_b[pi]])
                            return pi
                        def stage2(tile, pi):
                            (qs, ka, kbs, KB, sA, sB, Nq, outs) = tile
                            nout_ = len(outs)
                            for oi, (bk, ocs, qsub) in enumerate(outs):
                                for kind in range(2):
                                    bank = (NUM if kind == 0 else DEN)[bk]
                                    fidx = kind * 2 + bk
                                    st = first[fidx]
                                    first[fidx] = False
                                    if kind == 0:
                                        lA = vt[vi][:, sA, hh * 128:(hh + 1) * 128]
                                        lB = vt[vi][0:KB, sB, hh * 128:(hh + 1) * 128]
                                        rdl = [vt_b[vi], pts_b[pi]]
                                    else:
                                        lA = ones_bf
                                        lB = constf[0:KB, 384:512]
                                        rdl = [constf_b, pts_b[pi]]
                                    qa = slice(qsub.start, qsub.stop)
                                    qbb = slice(Nq + qsub.start, Nq + qsub.stop)
                                    S.op("pe", lambda e, bank=bank, ocs=ocs, lA=lA, pi=pi, qa=qa, st=st: e.matmul(
                                        psb[bank][:, ocs], lhsT=lA, rhs=pts[pi][:, qa], start=st, stop=False,
                                        skip_group_check=True), reads=rdl, writes=[ps_b[bank]], signal=False)
                                    last = (oi == nout_ - 1 and kind == 1)
                                    S.op("pe", lambda e, bank=bank, ocs=ocs, lB=lB, pi=pi, qbb=qbb, KB=KB: e.matmul(
                                        psb[bank][:, ocs], lhsT=lB, rhs=pts[pi][0:KB, qbb], start=False, stop=True,
                                        skip_group_check=True), reads=rdl, writes=[ps_b[bank]], signal=last)
                        LAG = 3
                        pis = {}
                        for idx in range(len(tiles) + LAG):
                            if idx < len(tiles):
                                pis[idx] = stage1(tiles[idx])
                            if idx >= LAG:
                                stage2(tiles[idx - LAG], pis[idx - LAG])
                        for bk in range(2):
                            S.op("dve", lambda e, bk=bk: e.tensor_scalar(
                                out=rd[bk][:], in0=psb[DEN[bk]][:], scalar1=1e-30, scalar2=None, op0=ALU.max),
                                reads=[ps_b[DEN[bk]]], writes=[rd_b[bk]])
                            S.op("dve", lambda e, bk=bk: e.reciprocal(out=rd[bk][:], in_=rd[bk][:]),
                                 reads=[rd_b[bk]], writes=[rd_b[bk]])
                            S.op("dve", lambda e, bk=bk, h=h: e.tensor_tensor(
                                out=attnT[:, h, bk * 512:(bk + 1) * 512], in0=psb[NUM[bk]][:], in1=rd[bk][:], op=ALU.mult),
                                reads=[ps_b[NUM[bk]], rd_b[bk]], writes=[attnT_b[h]])
                    ring[0] = list(range(8))
                    S.barrier()
                    for b in [kb_b] + qT_b + kT_b + vt_b:
                        S.put_dsem(b.dsem)

                if cfg.stop == "a":
                    return

                def chan_stats(src3, src_b, n, sq, sq_b, pss, with_sum=None):
                    for j in range(n):
                        q = j % 2
                        S.op("act", lambda e, q=q, j=j: e.activation(out=sq[q][:], in_=src3[:, j, :], func=AF.Square),
                             reads=[src_b[j]], writes=[sq_b[q]])
                        for th in range(2):
                            S.op("pe", lambda e, q=q, th=th, j=j: e.matmul(
                                psb[pss[th]][:], lhsT=ones_bf, rhs=sq[q][:, th * 512:(th + 1) * 512],
                                start=(j == 0), stop=(j == n - 1)),
                                reads=[sq_b[q], constf_b], writes=[ps_b[pss[th]]], signal=(th == 1))

                def rstd_from(pss, dst, dst_b, n, eps_col):
                    for th in range(2):
                        S.op("act", lambda e, th=th: e.activation(
                            out=dst[:, th * 512:(th + 1) * 512], in_=psb[pss[th]][:], func=AF.Sqrt,
                            bias=epsb[:, eps_col:eps_col + 1], scale=1.0 / n),
                            reads=[ps_b[pss[th]], eps_b], writes=[dst_b])
                    S.op("dve", lambda e: e.reciprocal(out=dst[:], in_=dst[:]), reads=[dst_b], writes=[dst_b])

                if True:
                    an = ph
                    sq = [sb(an, "asq%d" % i, [128, TCH], BF16) for i in range(2)]
                    sq_b = [Buf("asq%d" % i) for i in range(2)]
                    rsa = sb(an, "rsa", [128, TCH], F32)
                    rsa_b = Buf("rsa")
                    pss = next_ps(2)
                    chan_stats(attnT, attnT_b, NH, sq, sq_b, pss)
                    rstd_from(pss, rsa, rsa_b, AW, 0)
                    for h in range(NH):
                        S.op("dve", lambda e, h=h: e.scalar_tensor_tensor(
                            out=attnT[:, h, :], in0=attnT[:, h, :], scalar=sm("attn_out_norm", l, 16, h), in1=rsa[:],
                            op0=ALU.mult, op1=ALU.mult),
                            reads=[attnT_b[h], rsa_b, smalls_b], writes=[attnT_b[h]])

                convT = sb(ph, "convT", [128, 16, TCH], BF16)
                convT_b = [Buf("convT%d" % j) for j in range(16)]
                with ExitStack() as cv:
                    hin = [sb(cv, "hin%d" % i, [128, TCH + PADH], BF16) for i in range(2)]
                    hin_b = [Buf("hin%d" % i, S.get_dsem()) for i in range(2)]
                    dg = [sb(cv, "dg%d" % i, [128, 31, 128], BF16) for i in range(2)]
                    dg_b = [Buf("dg%d" % i) for i in range(2)]
                    sq = [sb(cv, "csq%d" % i, [128, 512], BF16) for i in range(2)]
                    sq_b = [Buf("csq%d" % i) for i in range(2)]
                    mean = sb(cv, "mean", [128, TCH], F32)
                    mean_b = Buf("mean")
                    rln = sb(cv, "rln", [128, TCH], F32)
                    rln_b = Buf("rln")
                    tmp = [sb(cv, "ctmp%d" % i, [128, TCH], F32) for i in range(2)]
                    tmp_b = [Buf("ctmp%d" % i) for i in range(2)]
                    psum_s = next_ps(2)
                    psum_q = next_ps(2)
                    ring[0] = [b_ for b_ in range(8) if b_ not in psum_s + psum_q]
                    wo = soff["conv_w"] + l * 16 * 31
                    qi = 0
                    for j in range(16):
                        i = j % 2
                        hdeps = [Hb[c][j]] + ([Hb[c - 1][j]] if c > 0 else [Hpad])
                        S.dma("sp", hin[i][:], Hd[j * 128:(j + 1) * 128, c * TCH:c * TCH + TCH + PADH], hin_b[i],
                              reads=hdeps, writes=[hin_b[i]])
                        for k in range(31):
                            o = wo + j * 31 + k
                            S.op("dve", lambda e, i=i, k=k, o=o: e.tensor_scalar(
                                out=dg[i][:, k, :], in0=ident, scalar1=smalls[:, o:o + 1], scalar2=None, op0=ALU.mult),
                                reads=[constf_b, smalls_b], writes=[dg_b[i]], signal=(k == 30))
                        for th in range(2):
                            ts = slice(th * 512, (th + 1) * 512)
                            (pc,) = next_ps(1)
                            for k in range(31):
                                S.op("pe", lambda e, i=i, k=k, pc=pc, th=th: e.matmul(
                                    psb[pc][:], lhsT=dg[i][:, k, :], rhs=hin[i][:, 2 + k + th * 512:2 + k + th * 512 + 512],
                                    start=(k == 0), stop=(k == 30)),
                                    reads=[dg_b[i], hin_b[i]], writes=[ps_b[pc]], signal=(k == 30))
                            q = qi % 2
                            qi += 1
                            S.op("act", lambda e, j=j, ts=ts, pc=pc: e.activation(
                                out=convT[:, j, ts], in_=psb[pc][:], func=AF.Identity, bias=sm("conv_b", l, 16, j), scale=1.0),
                                reads=[ps_b[pc], smalls_b], writes=[convT_b[j]])
                            S.op("act", lambda e, j=j, q=q, pc=pc: e.activation(
                                out=sq[q][:], in_=psb[pc][:], func=AF.Square, bias=sm("conv_b", l, 16, j), scale=1.0),
                                reads=[ps_b[pc], smalls_b], writes=[sq_b[q]])
                            S.op("pe", lambda e, j=j, th=th, ts=ts: e.matmul(
                                psb[psum_s[th]][:], lhsT=ones_bf, rhs=convT[:, j, ts], start=(j == 0), stop=(j == 15)),
                                reads=[convT_b[j], constf_b], writes=[ps_b[psum_s[th]]], signal=False)
                            S.op("pe", lambda e, q=q, j=j, th=th: e.matmul(
                                psb[psum_q[th]][:], lhsT=ones_bf, rhs=sq[q][:], start=(j == 0), stop=(j == 15)),
                                reads=[sq_b[q], constf_b], writes=[ps_b[psum_q[th]]], signal=True)
                    ring[0] = list(range(8))
                    sq = [sb(cv, "csq2_%d" % i, [128, TCH], BF16) for i in range(2)]
                    sq_b = [Buf("csq2_%d" % i) for i in range(2)]
                    for th in range(2):
                        ts = slice(th * 512, (th + 1) * 512)
                        S.op("dve", lambda e, th=th, ts=ts: e.tensor_scalar(
                            out=mean[:, ts], in0=psb[psum_s[th]][:], scalar1=1.0 / CW, scalar2=None, op0=ALU.mult),
                            reads=[ps_b[psum_s[th]]], writes=[mean_b])
                        S.op("dve", lambda e, ts=ts: e.tensor_tensor(
                            out=tmp[0][:, ts], in0=mean[:, ts], in1=mean[:, ts], op=ALU.mult),
                            reads=[mean_b], writes=[tmp_b[0]])
                        S.op("dve", lambda e, th=th, ts=ts: e.scalar_tensor_tensor(
                            out=rln[:, ts], in0=psb[psum_q[th]][:], scalar=1.0 / CW, in1=tmp[0][:, ts],
                            op0=ALU.mult, op1=ALU.subtract),
                            reads=[ps_b[psum_q[th]], tmp_b[0]], writes=[rln_b])
                    S.op("act", lambda e: e.activation(out=rln[:], in_=rln[:], func=AF.Sqrt, bias=epsb[:, 1:2], scale=1.0),
                         reads=[rln_b, eps_b], writes=[rln_b])
                    S.op("dve", lambda e: e.reciprocal(out=rln[:], in_=rln[:]), reads=[rln_b], writes=[rln_b])
                    pss = next_ps(2)
                    for j in range(16):
                        i = j % 2
                        S.op("dve", lambda e, i=i, j=j: e.tensor_tensor(
                            out=tmp[i][:], in0=convT[:, j, :], in1=mean[:], op=ALU.subtract),
                            reads=[convT_b[j], mean_b], writes=[tmp_b[i]])
                        S.op("dve", lambda e, i=i: e.tensor_tensor(
                            out=tmp[i][:], in0=tmp[i][:], in1=rln[:], op=ALU.mult),
                            reads=[tmp_b[i], rln_b], writes=[tmp_b[i]])
                        S.op("act", lambda e, i=i, j=j: e.activation(
                            out=convT[:, j, :], in_=tmp[i][:], func=AF.Silu,
                            bias=sm("conv_ln_b", l, 16, j), scale=sm("conv_ln_g", l, 16, j)),
                            reads=[tmp_b[i], smalls_b], writes=[convT_b[j]])
                        S.op("act", lambda e, i=i, j=j: e.activation(out=sq[i][:], in_=convT[:, j, :], func=AF.Square),
                             reads=[convT_b[j]], writes=[sq_b[i]])
                        for th in range(2):
                            S.op("pe", lambda e, i=i, th=th, j=j: e.matmul(
                                psb[pss[th]][:], lhsT=ones_bf, rhs=sq[i][:, th * 512:(th + 1) * 512],
                                start=(j == 0), stop=(j == 15)),
                                reads=[sq_b[i], constf_b], writes=[ps_b[pss[th]]], signal=(th == 1))
                    rsc = mean
                    rstd_from(pss, rsc, mean_b, CW, 0)
                    for j in range(16):
                        S.op("dve", lambda e, j=j: e.scalar_tensor_tensor(
                            out=convT[:, j, :], in0=convT[:, j, :], scalar=sm("conv_out_norm", l, 16, j), in1=rsc[:],
                            op0=ALU.mult, op1=ALU.mult),
                            reads=[convT_b[j], mean_b, smalls_b], writes=[convT_b[j]])
                    S.barrier()
                    for b in hin_b:
                        S.put_dsem(b.dsem)

                if cfg.stop == "c":
                    return
                with ExitStack() as m2:
                    NXT = 3
                    xt = [sb(m2, "mxt%d" % i, [128, 512], F32) for i in range(NXT)]
                    xt_b = [Buf("mxt%d" % i, S.get_dsem()) for i in range(NXT)]
                    xn = [sb(m2, "mxn%d" % i, [128, 512], F32) for i in range(NXT)]
                    xn_b = [Buf("mxn%d" % i, S.get_dsem()) for i in range(NXT)]
                    xi = 0
                    for dc in range(KC):
                        w, w_b = load_w(wout_d, (l * KC + dc) * 128)
                        for th in range(2):
                            i = xi % NXT
                            xi += 1
                            cs = slice(c * TCH + th * 512, c * TCH + (th + 1) * 512)
                            ts = slice(th * 512, (th + 1) * 512)
                            S.dma("sp", xt[i][:], X[dc * 128:(dc + 1) * 128, cs], xt_b[i],
                                  reads=[Xb[c][dc]], writes=[xt_b[i]])
                            (py,) = next_ps(1)
                            for kc in range(KC):
                                if kc < 16:
                                    rhs, rb = attnT[:, kc, ts], attnT_b[kc]
                                else:
                                    rhs, rb = convT[:, kc - 16, ts], convT_b[kc - 16]
                                S.op("pe", lambda e, py=py, w=w, kc=kc, rhs=rhs: e.matmul(
                                    psb[py][:], lhsT=w[:, kc, :], rhs=rhs, start=(kc == 0), stop=(kc == KC - 1)),
                                    reads=[w_b, rb], writes=[ps_b[py]], signal=(kc == KC - 1))
                            S.op("dve", lambda e, i=i, py=py: e.tensor_tensor(
                                out=xn[i][:], in0=psb[py][:], in1=xt[i][:], op=ALU.add),
                                reads=[ps_b[py], xt_b[i]], writes=[xn_b[i]])
                            S.dma("sp", X[dc * 128:(dc + 1) * 128, cs], xn[i][:], xn_b[i],
                                  reads=[xn_b[i]], writes=[Xb[c][dc]])
                    S.barrier()
                    for b in xt_b + xn_b:
                        S.put_dsem(b.dsem)

        def phase_final(c, oc):
            with ExitStack() as ph:
                rstd, rstd_b = phase_pre(ph, X, Xb[c], c, "final_norm", 0, None, None, want_xb=False)
                NXS = 3
                xs = [sb(ph, "fx%d" % i, [128, TCH], F32) for i in range(NXS)]
                xs_b = [Buf("fx%d" % i, S.get_dsem()) for i in range(NXS)]
                xo = [sb(ph, "fo%d" % i, [128, TCH], F32) for i in range(NXS)]
                xo_b = [Buf("fo%d" % i, S.get_dsem()) for i in range(NXS)]
                for kc in range(KC):
                    i = kc % NXS
                    S.dma("sp", xs[i][:], X[kc * 128:(kc + 1) * 128, c * TCH:(c + 1) * TCH], xs_b[i],
                          reads=[Xb[c][kc]], writes=[xs_b[i]])
                    o = soff["final_norm"] + kc
                    S.op("dve", lambda e, i=i, o=o: e.scalar_tensor_tensor(
                        out=xo[i][:], in0=xs[i][:], scalar=smalls[:, o:o + 1], in1=rstd[:],
                        op0=ALU.mult, op1=ALU.mult),
                        reads=[xs_b[i], rstd_b, smalls_b], writes=[xo_b[i]])
                    S.dma("sp", outT[kc * 128:(kc + 1) * 128, oc * TCH:(oc + 1) * TCH], xo[i][:], xo_b[i],
                          reads=[xo_b[i]], writes=[Outb])
                S.barrier()
                for b in xs_b + xo_b:
                    S.put_dsem(b.dsem)

        def phase_copy_in(c):
            with ExitStack() as ph:
                NXS = 3
                xs = [sb(ph, "ci%d" % i, [128, TCH], F32) for i in range(NXS)]
                xs_b = [Buf("ci%d" % i, S.get_dsem()) for i in range(NXS)]
                for kc in range(KC):
                    i = kc % NXS
                    S.dma("sp", xs[i][:], xT_in[kc * 128:(kc + 1) * 128, c * TCH:(c + 1) * TCH], xs_b[i],
                          reads=[Xin_b], writes=[xs_b[i]])
                    S.dma("sp", X[kc * 128:(kc + 1) * 128, c * TCH:(c + 1) * TCH], xs[i][:], xs_b[i],
                          reads=[xs_b[i]], writes=[Xb[c][kc]])
                S.barrier()
                for b in xs_b:
                    S.put_dsem(b.dsem)

        from_in = [True] * NCH
        xin_bufs = [Xin_b] * KC
        for l in range(L):
            for c in range(NCH):
                if "f1" in cfg.phases:
                    if from_in[c]:
                        phase_ffn(l, c, "f1", "ffn1_norm", xT_in, xin_bufs)
                        from_in[c] = False
                    else:
                        phase_ffn(l, c, "f1", "ffn1_norm", X, Xb[c])
                if "mix" in cfg.phases:
                    if from_in[c]:
                        phase_copy_in(c)
                        from_in[c] = False
                    phase_mix(l, c)
                if "f2" in cfg.phases:
                    phase_ffn(l, c, "f2", "ffn2_norm", X, Xb[c])
        if cfg.final:
            for oc, c in enumerate(cfg.out_chunks):
                phase_final(c, oc)
        S.final_wait("sp")
        print("instructions emitted:", S.ninst, file=sys.stderr)
    return nc


def make_inputs(inp, cfg, win_start=0):
    L, T = cfg.L, cfg.T
    x = np.asarray(inp["x"], np.float32)[0]
    Sq = x.shape[0]
    xw = np.zeros((T, D), np.float32)
    lo = max(win_start, 0)
    hi = min(win_start + T, Sq)
    xw[lo - win_start:hi - win_start] = x[lo:hi]
    m = {"xT": np.ascontiguousarray(xw.T)}
    m["smalls"] = pack_smalls(inp, L)
    m["consts"] = const_tiles()
    pos = np.arange(win_start, win_start + T).astype(np.float32)
    m["rope"] = rope_tables(pos)
    gpos = np.arange(win_start - PADK, win_start + T)
    m["kbias"] = np.where(gpos >= 0, 0.0, NEG).astype(np.float32)[None, :]
    for f, pre in (("f1", "ffn1"), ("f2", "ffn2")):
        m[f + "g"] = np.concatenate([tile_w(np.asarray(inp[pre + "_w_gate"][l], np.float32), KC, FCN) for l in range(L)], 0)
        m[f + "u"] = np.concatenate([tile_w(np.asarray(inp[pre + "_w_up"][l], np.float32), KC, FCN) for l in range(L)], 0)
        m[f + "d"] = np.concatenate([tile_wd(np.asarray(inp[pre + "_w_down"][l], np.float32)) for l in range(L)], 0)
    cols = win_cols()
    m["win"] = np.concatenate([tile_w(np.asarray(inp["w_in"][l], np.float32)[:, cols], KC, NOC) for l in range(L)], 0)
    m["wout"] = np.concatenate([tile_w(np.asarray(inp["w_out"][l], np.float32), KC, KC) for l in range(L)], 0)
    return m


_PROG = {}


def kernel(**inputs):
    cfg = Cfg(L=4, NCH=8)
    key = "full"
    if key not in _PROG:
        _PROG[key] = build_program(cfg)
    nc = _PROG[key]
    m = make_inputs(inputs, cfg, 0)
    res = run_bass_kernel_spmd(nc, [m], core_ids=[0])
    oT = res.results[0]["outT"]
    return np.ascontiguousarray(oT.T)[None].astype(np.float32)
```
